# Optimizing a Trainium2 kernel written in Bass

```python
import math
import jax, jax.numpy as jnp
from jax import lax
import numpy as np

D_MODEL = 1024
BATCH = 8
SEQ = 4096
DEPTH = 2

CHUNK = 64
Q_BLOCK = 128
RMS_EPS = 1e-6
ROPE_BASE = 10000.0

RET_HEADS = 8
RET_QK_DIM = D_MODEL // RET_HEADS
RET_V_DIM = 2 * RET_QK_DIM
RET_QK_WIDTH = RET_HEADS * RET_QK_DIM
RET_WIDTH = RET_HEADS * RET_V_DIM
RET_IN = 2 * RET_QK_WIDTH + 2 * RET_WIDTH

MLA_HEADS = 8
MLA_NOPE = 128
MLA_ROPE = 64
MLA_V = 128
MLA_WIDTH = MLA_HEADS * MLA_V
Q_LORA = 384
KV_LORA = 256
MLA_IN = Q_LORA + MLA_WIDTH

kernel_name = 'yoco_retention_mla_sandwich'


def rmsnorm(x, g):
    xf = x.astype(jnp.float32)
    y = xf * lax.rsqrt(jnp.mean(xf * xf, axis=-1, keepdims=True) + RMS_EPS)
    return (y * g.astype(jnp.float32)).astype(x.dtype)


def rope(x, pos):
    half = x.shape[-1] // 2
    inv = ROPE_BASE ** (-jnp.arange(half, dtype=jnp.float32) / half)
    ang = pos.astype(jnp.float32)[:, None] * inv[None, :]
    cos = jnp.cos(ang)[None, :, None, :].astype(x.dtype)
    sin = jnp.sin(ang)[None, :, None, :].astype(x.dtype)
    x1, x2 = x[..., :half], x[..., half:]
    return jnp.concatenate([x1 * cos - x2 * sin, x1 * sin + x2 * cos], axis=-1)


def retention_decays(dtype):
    log_g = jnp.log(1.0 - jnp.exp2(-5.0 - jnp.arange(RET_HEADS, dtype=jnp.float32)))
    idx = jnp.arange(CHUNK, dtype=jnp.float32)
    dist = jnp.abs(idx[:, None] - idx[None, :])
    inner = jnp.exp(log_g[:, None, None] * dist[None])
    xi = jnp.exp(log_g[:, None] * (idx + 1.0)[None])
    zeta = jnp.exp(log_g[:, None] * (CHUNK - 1.0 - idx)[None])
    g_c = jnp.exp(log_g * CHUNK)
    return inner.astype(dtype), xi.astype(dtype), zeta.astype(dtype), g_c.astype(dtype)


def retention_layer(x, g_pre, w_in, gn_gain, w_out, g_post):
    b, s, _ = x.shape
    nc = s // CHUNK
    h = rmsnorm(x, g_pre)
    proj = h @ w_in
    q, k, v, gate = jnp.split(proj, [RET_QK_WIDTH, 2 * RET_QK_WIDTH, 2 * RET_QK_WIDTH + RET_WIDTH], axis=-1)
    pos = jnp.arange(s)
    q = rope(q.reshape(b, s, RET_HEADS, RET_QK_DIM), pos)
    k = rope(k.reshape(b, s, RET_HEADS, RET_QK_DIM), pos) * (RET_QK_DIM ** -0.5)
    v = v.reshape(b, s, RET_HEADS, RET_V_DIM)

    def to_chunks(t):
        return t.reshape(b, nc, CHUNK, RET_HEADS, t.shape[-1]).transpose(1, 0, 3, 2, 4)

    qc, kc, vc = to_chunks(q), to_chunks(k), to_chunks(v)
    inner_dec, xi, zeta, g_c = retention_decays(x.dtype)
    scores = jnp.einsum('nbhcd,nbhmd->nbhcm', qc, kc) * inner_dec[None, None]
    inner = jnp.einsum('nbhcm,nbhme->nbhce', scores, vc)

    def step(R, inp):
        q_i, k_i, v_i = inp
        cross = jnp.einsum('bhcd,bhde->bhce', q_i, R) * xi[None, :, :, None]
        R = R * g_c[None, :, None, None] + jnp.einsum('bhcd,bhce->bhde', k_i * zeta[None, :, :, None], v_i)
        return R, cross

    R0 = jnp.zeros((b, RET_HEADS, RET_QK_DIM, RET_V_DIM), x.dtype)
    _, cross = lax.scan(step, R0, (qc, kc, vc))
    o = (inner + cross).transpose(1, 0, 3, 2, 4).reshape(b, s, RET_HEADS, RET_V_DIM)
    of = o.astype(jnp.float32)
    mu = jnp.mean(of, axis=-1, keepdims=True)
    var = jnp.mean(jnp.square(of - mu), axis=-1, keepdims=True)
    o = ((of - mu) * lax.rsqrt(var + RMS_EPS)).reshape(b, s, RET_WIDTH) * gn_gain.astype(jnp.float32)
    y = (jax.nn.silu(gate) * o.astype(x.dtype)) @ w_out
    return x + rmsnorm(y, g_post)


def shared_latent_kv(h, g_kv, w_kv_a, g_kv_lat, w_uk, w_uv):
    b, s, _ = h.shape
    src = rmsnorm(h, g_kv)
    a = src @ w_kv_a
    c_kv, k_r = a[..., :KV_LORA], a[..., KV_LORA:]
    c_kv = rmsnorm(c_kv, g_kv_lat)
    k_nope = (c_kv @ w_uk).reshape(b, s, MLA_HEADS, MLA_NOPE)
    v = (c_kv @ w_uv).reshape(b, s, MLA_HEADS, MLA_V)
    k_rope = rope(k_r[:, :, None, :], jnp.arange(s))[:, :, 0, :]
    return k_nope, k_rope, v


def mla_attention(q_nope, q_rope, k_nope, k_rope, v):
    b, s = q_nope.shape[:2]
    nb = s // Q_BLOCK
    qn = q_nope.reshape(b, nb, Q_BLOCK, MLA_HEADS, MLA_NOPE).transpose(1, 0, 2, 3, 4)
    qr = q_rope.reshape(b, nb, Q_BLOCK, MLA_HEADS, MLA_ROPE).transpose(1, 0, 2, 3, 4)
    k_chunk = jnp.arange(s) // CHUNK
    scale = (MLA_NOPE + MLA_ROPE) ** -0.5

    def block(args):
        qn_i, qr_i, i = args
        sc = jnp.einsum('bqhd,bkhd->bhqk', qn_i, k_nope) + jnp.einsum('bqhr,bkr->bhqk', qr_i, k_rope)
        sc = sc.astype(jnp.float32) * scale
        q_chunk = (i * Q_BLOCK + jnp.arange(Q_BLOCK)) // CHUNK
        mask = k_chunk[None, :] <= q_chunk[:, None]
        sc = jnp.where(mask[None, None], sc, -jnp.inf)
        p = jax.nn.softmax(sc, axis=-1).astype(v.dtype)
        return jnp.einsum('bhqk,bkhd->bqhd', p, v)

    out = lax.map(block, (qn, qr, jnp.arange(nb)))
    return out.transpose(1, 0, 2, 3, 4).reshape(b, s, MLA_WIDTH)


def mla_layer(x, g_pre, w_in, g_q_lat, w_uq, w_out, g_post, k_nope, k_rope, v):
    b, s, _ = x.shape
    h = rmsnorm(x, g_pre)
    proj = h @ w_in
    c_q, gate = proj[..., :Q_LORA], proj[..., Q_LORA:]
    q = (rmsnorm(c_q, g_q_lat) @ w_uq).reshape(b, s, MLA_HEADS, MLA_NOPE + MLA_ROPE)
    q_nope = q[..., :MLA_NOPE]
    q_rope = rope(q[..., MLA_NOPE:], jnp.arange(s))
    o = mla_attention(q_nope, q_rope, k_nope, k_rope, v)
    y = (jax.nn.silu(gate) * o) @ w_out
    return x + rmsnorm(y, g_post)


def setup_inputs(seed: int = 0) -> dict:
    key = jax.random.key(seed)
    ks = jax.random.split(key, 20)
    n_a = DEPTH // 2
    n_b = DEPTH - n_a
    f32 = jnp.float32

    def w(k, shape, fan_in):
        return jax.random.normal(k, shape, f32) * (fan_in ** -0.5)

    def gain(k, shape):
        return 1.0 + 0.02 * jax.random.normal(k, shape, f32)

    return {
        'x': jax.random.normal(ks[0], (BATCH, SEQ, D_MODEL), f32),
        'g_pre_a': gain(ks[1], (n_a, D_MODEL)),
        'w_in_a': w(ks[2], (n_a, D_MODEL, RET_IN), D_MODEL),
        'gn_gain_a': gain(ks[3], (n_a, RET_WIDTH)),
        'w_out_a': w(ks[4], (n_a, RET_WIDTH, D_MODEL), RET_WIDTH),
        'g_post_a': gain(ks[5], (n_a, D_MODEL)),
        'g_kv': gain(ks[6], (D_MODEL,)),
        'w_kv_a': w(ks[7], (D_MODEL, KV_LORA + MLA_ROPE), D_MODEL),
        'g_kv_lat': gain(ks[8], (KV_LORA,)),
        'w_uk': w(ks[9], (KV_LORA, MLA_HEADS * MLA_NOPE), KV_LORA),
        'w_uv': w(ks[10], (KV_LORA, MLA_WIDTH), KV_LORA),
        'g_pre_b': gain(ks[11], (n_b, D_MODEL)),
        'w_in_b': w(ks[12], (n_b, D_MODEL, MLA_IN), D_MODEL),
        'g_q_lat': gain(ks[13], (n_b, Q_LORA)),
        'w_uq': w(ks[14], (n_b, Q_LORA, MLA_HEADS * (MLA_NOPE + MLA_ROPE)), Q_LORA),
        'w_out_b': w(ks[15], (n_b, MLA_WIDTH, D_MODEL), MLA_WIDTH),
        'g_post_b': gain(ks[16], (n_b, D_MODEL)),
    }


def reference(x, g_pre_a, w_in_a, gn_gain_a, w_out_a, g_post_a, g_kv, w_kv_a, g_kv_lat, w_uk, w_uv,
              g_pre_b, w_in_b, g_q_lat, w_uq, w_out_b, g_post_b):
    n_a = w_in_a.shape[0]
    h = x
    k_nope = k_rope = v = None
    for layer in range(DEPTH):
        if layer < n_a:
            h = retention_layer(h, g_pre_a[layer], w_in_a[layer], gn_gain_a[layer], w_out_a[layer], g_post_a[layer])
        else:
            if layer == n_a:
                k_nope, k_rope, v = shared_latent_kv(h, g_kv, w_kv_a, g_kv_lat, w_uk, w_uv)
            j = layer - n_a
            h = mla_layer(h, g_pre_b[j], w_in_b[j], g_q_lat[j], w_uq[j], w_out_b[j], g_post_b[j], k_nope, k_rope, v)
    return h
```

```python
import contextlib
import numpy as np
import concourse.bass as bass
import concourse.mybir as mybir
from concourse.bass_utils import run_bass_kernel_spmd

F32 = mybir.dt.float32
BF16 = mybir.dt.bfloat16
ALU = mybir.AluOpType
AF = mybir.ActivationFunctionType

D = 1024
EPS = 1e-6
NH = 8
SEQ = 4096
STOP_AFTER = None
SKIP = ''
B1N = None


class _Stop(Exception):
    pass


class Op:
    __slots__ = ("eng", "fn", "deps", "idx", "signal", "dma_key", "cnt")

    def __init__(self, eng, fn, deps, dma_key):
        self.eng = eng
        self.fn = fn
        self.deps = deps
        self.dma_key = dma_key
        self.signal = False
        self.cnt = None


class Sched:
    COMPUTE = ("pe", "dve", "act", "pool")

    def __init__(self, nc):
        self.nc = nc
        self.ops = []
        self.last_w = {}
        self.readers = {}
        self.bar = []
        self.last_eng = {}
        self.last_key = {}

    PSUM_ROOTS = {"PT", "PS", "PO", "PU", "PP", "PY", "PV", "PA", "PF", "PSc", "POc", "PDc"}

    @classmethod
    def _excl(cls, key):
        while not isinstance(key, str):
            key = key[0]
        return key in cls.PSUM_ROOTS

    def add(self, eng, fn, reads=(), writes=(), dma_key=None):
        ex = [r for r in reads if self._excl(r)]
        if ex:
            reads = [r for r in reads if not self._excl(r)]
            writes = list(writes) + [r for r in ex if r not in writes]
        deps = list(self.bar)
        for r in reads:
            w = self.last_w.get(r)
            if w is not None:
                deps.append(w)
        for r in writes:
            w = self.last_w.get(r)
            if w is not None:
                deps.append(w)
            deps.extend(self.readers.get(r, ()))
        if getattr(self, "limit", None) is not None and len(self.ops) >= self.limit:
            raise _Stop()
        op = Op(eng, fn, deps, dma_key)
        op.idx = len(self.ops)
        self.ops.append(op)
        for r in reads:
            self.readers.setdefault(r, []).append(op)
        for r in writes:
            self.last_w[r] = op
            self.readers[r] = []
        if dma_key is None:
            self.last_eng[eng] = op
        else:
            self.last_key[dma_key] = op
        return op

    def I(self, eng, meth, *args, reads=(), writes=(), dma_key=None, **kw):
        return self.add(eng, lambda e: getattr(e, meth)(*args, **kw), reads=reads, writes=writes, dma_key=dma_key)

    def barrier(self):
        self.bar = list(self.last_eng.values()) + list(self.last_key.values())
        self.last_w = {}
        self.readers = {}

    @staticmethod
    def _pe_pe(d, op):
        return d.dma_key is None and op.dma_key is None and d.eng == "pe" and op.eng == "pe"

    def emit(self, final_wait_ops=()):
        nc = self.nc
        for op in self.ops:
            for d in op.deps:
                if not self._pe_pe(d, op):
                    d.signal = True
        for op in final_wait_ops:
            op.signal = True
        for op in self.ops:
            if op.dma_key is not None:
                op.signal = True
        eng_cnt = {e: 0 for e in self.COMPUTE}
        key_cnt = {}
        for op in self.ops:
            if not op.signal:
                continue
            if op.dma_key is not None:
                key_cnt[op.dma_key] = key_cnt.get(op.dma_key, 0) + 1
                op.cnt = key_cnt[op.dma_key] * 16
            else:
                eng_cnt[op.eng] += 1
                op.cnt = eng_cnt[op.eng]
        sems = {}
        with contextlib.ExitStack() as st:
            for e in self.COMPUTE:
                sems[("eng", e)] = st.enter_context(nc.semaphore("s_" + e))
            for i, k in enumerate(sorted(key_cnt, key=str)):
                sems[("dma", k)] = st.enter_context(nc.semaphore("d%d" % i))
            block = st.enter_context(nc.Block())
            queues = {}
            for op in self.ops:
                queues.setdefault(op.eng, []).append(op)
            engmap = {"pe": ("tensor", nc.tensor), "dve": ("vector", nc.vector),
                      "act": ("scalar", nc.scalar), "pool": ("gpsimd", nc.gpsimd),
                      "sp": ("sync", nc.sync)}

            def semof(op):
                if op.dma_key is not None:
                    return sems[("dma", op.dma_key)]
                return sems[("eng", op.eng)]

            def run_queue(eng, ops, final):
                known = {}
                for op in ops:
                    need = {}
                    for d in op.deps:
                        if d.cnt is None or self._pe_pe(d, op):
                            continue
                        s = semof(d)
                        key = id(s)
                        if known.get(key, 0) >= d.cnt:
                            continue
                        if key not in need or need[key][1] < d.cnt:
                            need[key] = (s, d.cnt)
                    for key, (s, c) in need.items():
                        eng.wait_ge(s, c)
                        known[key] = c
                    ins = op.fn(eng)
                    if op.signal:
                        ins.then_inc(semof(op), 16 if op.dma_key is not None else 1)
                for op in final:
                    eng.wait_ge(semof(op), op.cnt)

            for ename, (attr, eng) in engmap.items():
                ops = queues.get(ename, [])
                final = list(final_wait_ops) if ename == "sp" else []
                if not ops and not final:
                    continue

                def body(e, _ops=ops, _final=final):
                    run_queue(e, _ops, _final)
                getattr(block, attr)(body)


class Buf:
    def __init__(self, alloc, name, shape, dt, nbuf=1):
        self.t = [alloc("%s_%d" % (name, i), shape, dt) for i in range(nbuf)]
        self.name = name
        self.n = nbuf

    def __call__(self, i=0):
        j = i % self.n
        return self.t[j], (self.name, j)


def build(S_len=SEQ):
    ctx = {}
    try:
        return _build(S_len, ctx)
    except _Stop:
        return ctx["nc"]


def _build(S_len, ctx):
    NT = S_len // 128
    QB = min(512, S_len)
    NQB = S_len // QB
    SUB = QB // 128
    nc = bass.Bass("TRN2", target_bir_lowering=False)

    def din(name, shape, dt=F32):
        return nc.dram_tensor(name, list(shape), dt, kind="ExternalInput").ap()

    def dscr(name, shape, dt):
        return nc.dram_tensor(name, list(shape), dt, kind="Internal").ap()

    x = din("x", [S_len, D])
    w_in_a = din("w_in_a", [128, 8, 6144])
    g_pre_a = din("g_pre_a", [128, 8])
    gn_gain_a = din("gn_gain_a", [128, 16])
    w_out_a = din("w_out_a", [128, 16, D])
    g_post_a = din("g_post_a", [128, D])
    g_kv = din("g_kv", [128, 8])
    w_kv_a = din("w_kv_a", [128, 8, 320])
    g_kv_lat = din("g_kv_lat", [128, 2])
    w_uk = din("w_uk", [128, 2, D])
    w_uv = din("w_uv", [128, 2, D])
    g_pre_b = din("g_pre_b", [128, 8])
    w_in_b = din("w_in_b", [128, 8, 1408])
    g_q_lat = din("g_q_lat", [128, 3])
    w_uq = din("w_uq", [128, 3, 1536])
    w_out_b = din("w_out_b", [128, 8, D])
    g_post_b = din("g_post_b", [128, D])
    csA = din("csA", [128, NT, 2, 64])
    csB = din("csB", [128, NT, 2, 32])
    t2t_d = din("t2t", [128, 8, 128])
    dxi_d = din("dxi", [128, 8, 128])
    ident_d = din("ident", [128, 128])
    zs_d = din("zs", [128, 8])
    gc_host = None
    out = nc.dram_tensor("out", [S_len, D], F32, kind="ExternalOutput").ap()

    zTa = dscr("zTa", [NT, 128, 16, 128], BF16)
    h1 = dscr("h1", [S_len, D], F32)
    knT = dscr("knT", [8, 128, S_len], BF16)
    krT = dscr("krT", [128, S_len], BF16)
    vS = dscr("vS", [NT, 128, D], BF16)
    qnT = dscr("qnT", [8, 128, S_len], BF16)
    qrT = dscr("qrT", [4, 128, S_len], BF16)
    sgT = dscr("sgT", [8, 128, S_len], BF16)
    zTb = dscr("zTb", [NQB, 128, 8, QB], BF16)

    GC = [float((1.0 - 2.0 ** (-5.0 - h)) ** 128) for h in range(NH)]

    S = Sched(nc)
    ctx["S"] = S
    ctx["nc"] = nc

    def stop_here(tag):
        if STOP_AFTER == tag:
            last = [o for o in S.ops if o.dma_key is not None][-1]
            S.emit(final_wait_ops=[last, S.ops[-1]])
            raise _Stop()

    def load_weight(sb, ps_alloc, src, dst, K, N, scale, stage, eng_cycle, tag):
        i = 0
        for k in range(K):
            for n0 in range(0, N, 2048):
                n1 = min(N, n0 + 2048)
                st_t, st_k = stage(i)
                S.I("sp", "dma_start", out=st_t[:, 0:n1 - n0], in_=src[:, k, n0:n1],
                      writes=[st_k], dma_key=st_k)
                eng = eng_cycle[i % len(eng_cycle)]
                rd = [st_k] + ([("gsc", tag)] if scale is not None else [])
                wkey = (tag, k, n0)
                if scale is None:
                    if eng == "act":
                        S.I("act", "activation", out=dst[:, k, n0:n1], in_=st_t[:, 0:n1 - n0], func=AF.Copy,
                              reads=rd, writes=[wkey])
                    else:
                        S.I(eng, "tensor_copy", out=dst[:, k, n0:n1], in_=st_t[:, 0:n1 - n0],
                              reads=rd, writes=[wkey])
                else:
                    if eng == "act":
                        S.I("act", "activation", out=dst[:, k, n0:n1], in_=st_t[:, 0:n1 - n0], func=AF.Copy, scale=scale[:, k:k + 1],
                              reads=rd, writes=[wkey])
                    else:
                        S.I(eng, "tensor_scalar", out=dst[:, k, n0:n1], in0=st_t[:, 0:n1 - n0], scalar1=scale[:, k:k + 1], scalar2=None, op0=ALU.mult,
                              reads=rd, writes=[wkey])
                i += 1

    def wkeys(tag, K, N):
        return [(tag, k, n0) for k in range(K) for n0 in range(0, N, 2048)]

    def wkey_for(tag, k, c0):
        return (tag, k, (c0 // 2048) * 2048)

    def small_load(dst, src, key):
        S.I("sp", "dma_start", out=dst, in_=src, writes=[key], dma_key=key)

    def rstd_chain(slot_i, ssq_ap, ssq_key, sq, rstd, inv_n, nm):
        sq_t, sq_k = sq(slot_i)
        r_t, r_k = rstd(slot_i)
        S.I("act", "activation", out=sq_t[:], in_=ssq_ap, func=AF.Sqrt, scale=inv_n, bias=epsT[:],
              reads=[ssq_key, "epsT"], writes=[sq_k])
        S.I("dve", "reciprocal", out=r_t[:], in_=sq_t[:], reads=[sq_k], writes=[r_k])
        return r_t, r_k

    with contextlib.ExitStack() as stk:
        def sb(name, shape, dt):
            return stk.enter_context(nc.sbuf_tensor("sb_" + name, list(shape), dt))

        def ps(name, shape, dt):
            return stk.enter_context(nc.psum_tensor("ps_" + name, list(shape), dt))

        epsT = sb("epsT", [128, 1], F32)
        ident = sb("ident", [128, 128], BF16)
        identf = sb("identf", [128, 128], F32)
        S.I("pool", "memset", epsT[:], EPS, writes=["epsT"])
        small_load(identf[:], ident_d[:, :], "identf")
        S.I("dve", "tensor_copy", out=ident[:], in_=identf[:], reads=["identf"], writes=["ident"])

        with contextlib.ExitStack() as stA:
            def sbA(name, shape, dt):
                return stA.enter_context(nc.sbuf_tensor("A1" + name, list(shape), dt))

            def psA(name, shape, dt):
                return stA.enter_context(nc.psum_tensor("A1p" + name, list(shape), dt))

            WinB = sbA("WinB", [128, 8, 6144], BF16)
            stage = Buf(sbA, "stage", [128, 2048], F32, 2)
            gpa = sbA("gpa", [128, 8], F32)
            t2t = sbA("t2t", [128, 8, 128], F32)
            dxi = sbA("dxi", [128, 8, 128], BF16)
            zs = sbA("zs", [128, 8], F32)
            xt = Buf(sbA, "xt", [128, D], F32, 2)
            junk = sbA("junk", [128, D], BF16)
            ssq = Buf(sbA, "ssq", [128, 1], F32, 2)
            sq = Buf(sbA, "sq", [128, 1], F32, 2)
            rstd = Buf(sbA, "rstd", [128, 1], F32, 2)
            hb = Buf(sbA, "hb", [128, D], BF16, 1)
            hT = Buf(sbA, "hT", [128, 8, 128], BF16, 2)
            cs = Buf(sbA, "cs", [128, 2, 64], F32, 2)
            rtmp = Buf(sbA, "rtmp", [128, 4, 4, 64], F32, 2)
            qr = Buf(sbA, "qr", [128, 8, 128], BF16, 1)
            kr = Buf(sbA, "kr", [128, 8, 128], BF16, 2)
            qT = Buf(sbA, "qT", [128, 8, 128], BF16, 2)
            kT = Buf(sbA, "kT", [128, 8, 128], BF16, 2)
            vt = Buf(sbA, "vt", [128, 8, 256], BF16, 2)
            sg = Buf(sbA, "sg", [128, 2048], BF16, 2)
            pS = Buf(sbA, "pS", [128, 8, 128], BF16, 1)
            stats = sbA("stats", [128, 8, 6], F32)
            mv = sbA("mv", [128, 8, 2], F32)
            gsq = sbA("gsq", [128, 8], F32)
            grs = sbA("grs", [128, 8], F32)
            gnb = sbA("gnb", [128, 8], F32)
            on = Buf(sbA, "on", [128, 2048], BF16, 1)
            zT = Buf(sbA, "zT", [128, 16, 128], BF16, 2)
            Rf = sbA("Rf", [128, 8, 256], F32)
            Rb = sbA("Rb", [128, 8, 256], BF16)

            PT = psA("PT", [128, 8, 128], F32)
            PP = Buf(psA, "PP", [128, 512], F32, 2)
            PS_ = psA("PS", [128, 4, 128], F32)
            PO = Buf(psA, "PO", [128, 2, 256], F32, 2)
            PU = psA("PU", [128, 2, 256], F32)

            small_load(gpa[:], g_pre_a[:, :], "gpa")
            S.last_w[("gsc", "WinB")] = S.last_w["gpa"]
            small_load(t2t[:], t2t_d[:, :, :], "t2t")
            small_load(zs[:], zs_d[:, :], "zs")
            st_t, st_k = stage(0)
            S.I("sp", "dma_start", out=st_t[:, 0:1024], in_=dxi_d.rearrange("p h c -> p (h c)"), writes=[st_k], dma_key=st_k)
            S.I("dve", "tensor_copy", out=dxi[:].rearrange("p h c -> p (h c)"), in_=st_t[:, 0:1024], reads=[st_k], writes=["dxi"])
            load_weight(sbA, psA, w_in_a, WinB, 8, 6144, gpa, stage, ["dve", "pool", "act"], "WinB")

            def A_load(t):
                xt_t, xt_k = xt(t)
                cs_t, cs_k = cs(t)
                S.I("sp", "dma_start", out=xt_t[:], in_=x[t * 128:(t + 1) * 128, :], writes=[xt_k], dma_key=xt_k)
                S.I("sp", "dma_start", out=cs_t[:], in_=csA[:, t, :, :], writes=[(cs_k, 0), (cs_k, 1)], dma_key=cs_k)

            def A_S1(t):
                xt_t, xt_k = xt(t)
                cs_t, cs_k = cs(t)
                ssq_t, ssq_k = ssq(t)
                S.I("act", "activation", out=junk[:], in_=xt_t[:], func=AF.Square, accum_out=ssq_t[:],
                      reads=[xt_k], writes=["junk", ssq_k])
                r_t, r_k = rstd_chain(t, ssq_t[:], ssq_k, sq, rstd, 1.0 / D, "x")
                hb_t, hb_k = hb(t)
                S.I("dve", "tensor_scalar", out=hb_t[:], in0=xt_t[:], scalar1=r_t[:], scalar2=None, op0=ALU.mult,
                      reads=[xt_k, r_k], writes=[hb_k])
                for k in range(8):
                    S.I("pe", "matmul", PT[:, k, :], lhsT=hb_t[:, k * 128:(k + 1) * 128], rhs=ident[:], start=True, stop=True,
                          reads=[hb_k, "ident"], writes=[("PT", k // 4)])
                hT_t, hT_k = hT(t)
                S.I("act", "activation", out=hT_t[:], in_=PT[:], func=AF.Copy,
                      reads=[("PT", 0), ("PT", 1)], writes=[hT_k])
                yield
                qr_t, qr_k = qr(t)
                kr_t, kr_k = kr(t)
                vt_t, vt_k = vt(t)
                sg_t, sg_k = sg(t)
                for cb in range(12):
                    pp_t, pp_k = PP(cb)
                    for k in range(8):
                        S.I("pe", "matmul", pp_t[:], lhsT=hT_t[:, k, :], rhs=WinB[:, k, cb * 512:(cb + 1) * 512], start=(k == 0), stop=(k == 7),
                              reads=[hT_k, wkey_for("WinB", k, cb * 512)], writes=[pp_k])
                    if cb < 4:
                        dst_t, dst_k = (qr_t, qr_k) if cb < 2 else (kr_t, kr_k)
                        hh = (cb % 2) * 4
                        p3 = pp_t[:].rearrange("p (h d) -> p h d", h=4)
                        x1 = p3[:, :, 0:64]
                        x2 = p3[:, :, 64:128]
                        cosb = cs_t[:, 0, :].unsqueeze(1).to_broadcast([128, 4, 64])
                        sinb = cs_t[:, 1, :].unsqueeze(1).to_broadcast([128, 4, 64])
                        tm_t, tm_k = rtmp(cb)
                        S.I("dve", "tensor_tensor", out=tm_t[:, 0], in0=x1, in1=cosb, op=ALU.mult,
                              reads=[pp_k, (cs_k, 0)], writes=[(tm_k, 0)])
                        S.I("dve", "tensor_tensor", out=tm_t[:, 1], in0=x2, in1=sinb, op=ALU.mult,
                              reads=[pp_k, (cs_k, 1)], writes=[(tm_k, 1)])
                        S.I("dve", "tensor_tensor", out=tm_t[:, 2], in0=x1, in1=sinb, op=ALU.mult,
                              reads=[pp_k, (cs_k, 1)], writes=[(tm_k, 2)])
                        S.I("dve", "tensor_tensor", out=tm_t[:, 3], in0=x2, in1=cosb, op=ALU.mult,
                              reads=[pp_k, (cs_k, 0)], writes=[(tm_k, 3)])
                        S.I("pool", "tensor_tensor", out=dst_t[:, hh:hh + 4, 0:64], in0=tm_t[:, 0], in1=tm_t[:, 1], op=ALU.subtract,
                              reads=[(tm_k, 0), (tm_k, 1)], writes=[(dst_k, cb % 2, 0)])
                        S.I("pool", "tensor_tensor", out=dst_t[:, hh:hh + 4, 64:128], in0=tm_t[:, 2], in1=tm_t[:, 3], op=ALU.add,
                              reads=[(tm_k, 2), (tm_k, 3)], writes=[(dst_k, cb % 2, 1)])
                    elif cb < 8:
                        for j in range(2):
                            h = (cb - 4) * 2 + j
                            S.I("act", "activation", out=vt_t[:, h, :], in_=pp_t[:, j * 256:(j + 1) * 256], func=AF.Copy, scale=zs[:, h:h + 1],
                                  reads=[pp_k, "zs"], writes=[(vt_k, h)])
                    else:
                        c0 = (cb - 8) * 512
                        S.I("act", "activation", out=sg_t[:, c0:c0 + 512], in_=pp_t[:], func=AF.Silu,
                              reads=[pp_k], writes=[(sg_k, cb - 8)])
                    yield
                for h in range(8):
                    S.I("pe", "matmul", PT[:, h, :], lhsT=qr_t[:, h, :], rhs=dxi[:, h, :], start=True, stop=True,
                          reads=[(qr_k, h // 4, 0), (qr_k, h // 4, 1), "dxi"], writes=[("PT", h // 4)])
                qT_t, qT_k = qT(t)
                S.I("dve", "tensor_copy", out=qT_t[:], in_=PT[:], reads=[("PT", 0), ("PT", 1)], writes=[qT_k])
                yield
                for h in range(8):
                    S.I("pe", "matmul", PT[:, h, :], lhsT=kr_t[:, h, :], rhs=ident[:], start=True, stop=True,
                          reads=[(kr_k, h // 4, 0), (kr_k, h // 4, 1), "ident"], writes=[("PT", h // 4)])
                kT_t, kT_k = kT(t)
                S.I("act", "activation", out=kT_t[:], in_=PT[:], func=AF.Copy, reads=[("PT", 0), ("PT", 1)], writes=[kT_k])

            def A_S2(t):
                qT_t, qT_k = qT(t)
                kT_t, kT_k = kT(t)
                kr_t, kr_k = kr(t)
                vt_t, vt_k = vt(t)
                sg_t, sg_k = sg(t)
                pS_t, pS_k = pS(t)
                on_t, on_k = on(t)
                for g in range(2):
                    for j in range(4):
                        h = 4 * g + j
                        S.I("pe", "matmul", PS_[:, j, :], lhsT=kT_t[:, h, :], rhs=qT_t[:, h, :], start=True, stop=True,
                              reads=[kT_k, qT_k], writes=["PS"])
                    S.I("dve", "tensor_tensor", out=pS_t[:, 4 * g:4 * g + 4, :], in0=PS_[:], in1=t2t[:, 4 * g:4 * g + 4, :], op=ALU.mult,
                          reads=["PS"] + ["t2t"], writes=[(pS_k, g)])
                    yield
                for hp in range(4):
                    po_t, po_k = PO(hp)
                    for j in range(2):
                        h = 2 * hp + j
                        S.I("pe", "matmul", po_t[:, j, :], lhsT=pS_t[:, h, :], rhs=vt_t[:, h, :], start=True, stop=(t == 0),
                              reads=[(pS_k, h // 4), (vt_k, h)], writes=[po_k])
                        if t > 0:
                            S.I("pe", "matmul", po_t[:, j, :], lhsT=qT_t[:, h, :], rhs=Rb[:, h, :], start=False, stop=True,
                                  reads=[qT_k, ("Rb", hp)], writes=[po_k])
                    for j in range(2):
                        h = 2 * hp + j
                        S.I("dve", "bn_stats", out=stats[:, h, :], in_=po_t[:, j, :],
                              reads=[po_k], writes=[("stats", h)])
                        S.I("dve", "bn_aggr", out=mv[:, h, :], in_=stats[:, h, :],
                              reads=[("stats", h)], writes=[("mv", h)])
                    pr = slice(2 * hp, 2 * hp + 2)
                    S.I("act", "activation", out=gsq[:, pr], in_=mv[:, pr, 1], func=AF.Sqrt, bias=epsT[:],
                          reads=[("mv", 2 * hp), ("mv", 2 * hp + 1), "epsT"], writes=[("gsq", hp)])
                    S.I("dve", "reciprocal", out=grs[:, pr], in_=gsq[:, pr], reads=[("gsq", hp)], writes=[("grs", hp)])
                    S.I("dve", "scalar_tensor_tensor", out=gnb[:, pr], in0=mv[:, pr, 0], scalar=-1.0, in1=grs[:, pr], op0=ALU.mult, op1=ALU.mult,
                          reads=[("mv", 2 * hp), ("mv", 2 * hp + 1), ("grs", hp)], writes=[("gnb", hp)])
                    for j in range(2):
                        h = 2 * hp + j
                        S.I("act", "activation", out=on_t[:, h * 256:(h + 1) * 256], in_=po_t[:, j, :], func=AF.Identity, scale=grs[:, h:h + 1], bias=gnb[:, h:h + 1],
                              reads=[po_k, ("grs", hp), ("gnb", hp)], writes=[(on_k, h)])
                    c0 = hp * 512
                    S.I("pool", "tensor_tensor", out=on_t[:, c0:c0 + 512], in0=on_t[:, c0:c0 + 512], in1=sg_t[:, c0:c0 + 512], op=ALU.mult,
                          reads=[(on_k, 2 * hp), (on_k, 2 * hp + 1), (sg_k, hp)], writes=[(on_k, 2 * hp), (on_k, 2 * hp + 1)])
                    if t < NT - 1:
                        for j in range(2):
                            h = 2 * hp + j
                            S.I("pe", "matmul", PU[:, j, :], lhsT=kr_t[:, h, :], rhs=vt_t[:, h, :], start=True, stop=True,
                                  reads=[(kr_k, h // 4, 0), (kr_k, h // 4, 1), (vt_k, h)], writes=["PU"])
                            if t == 0:
                                S.I("dve", "tensor_copy", out=Rf[:, h, :], in_=PU[:, j, :],
                                      reads=["PU"], writes=[("Rf", h)])
                            else:
                                S.I("dve", "scalar_tensor_tensor", out=Rf[:, h, :], in0=Rf[:, h, :], scalar=GC[h], in1=PU[:, j, :], op0=ALU.mult, op1=ALU.add,
                                      reads=["PU", ("Rf", h)], writes=[("Rf", h)])
                        S.I("pool", "tensor_copy", out=Rb[:, pr, :], in_=Rf[:, pr, :],
                              reads=[("Rf", 2 * hp), ("Rf", 2 * hp + 1)], writes=[("Rb", hp)])
                    yield
                zT_t, zT_k = zT(t)
                for r in range(2):
                    for c in range(8):
                        cc = 8 * r + c
                        S.I("pe", "matmul", PT[:, c, :], lhsT=on_t[:, cc * 128:(cc + 1) * 128], rhs=ident[:], start=True, stop=True,
                              reads=[(on_k, cc // 2), "ident"], writes=[("PT", c // 4)])
                    S.I("act", "activation", out=zT_t[:, 8 * r:8 * r + 8, :], in_=PT[:], func=AF.Copy,
                          reads=[("PT", 0), ("PT", 1)], writes=[(zT_k, r)])
                    yield
                S.I("sp", "dma_start", out=zTa[t], in_=zT_t[:], reads=[(zT_k, 0), (zT_k, 1)], writes=[("zTa", t)], dma_key=("zTst", t % 2))

            def interleave(*gens):
                gens = [g for g in gens if g is not None]
                while gens:
                    for g in list(gens):
                        try:
                            next(g)
                        except StopIteration:
                            gens.remove(g)

            A_load(0)
            if NT > 1:
                A_load(1)
            interleave(A_S1(0))
            for t in range(NT):
                if t + 2 < NT:
                    A_load(t + 2)
                interleave(A_S1(t + 1) if t + 1 < NT else None, A_S2(t))
        S.barrier()
        if STOP_AFTER == "A1":
            S.emit(final_wait_ops=[S.ops[-1]])
            return nc

        def outproj_phase(tagp, w_d, KC, gscale_d, gpost_d, load_z, resid, dst, nblk_sub):
            with contextlib.ExitStack() as stO:
                def sbO(name, shape, dt):
                    return stO.enter_context(nc.sbuf_tensor(tagp + name, list(shape), dt))

                def psO(name, shape, dt):
                    return stO.enter_context(nc.psum_tensor(tagp + "p" + name, list(shape), dt))
                WoB = sbO("WoB", [128, KC, D], BF16)
                stage = Buf(sbO, "stage", [128, 2048], F32, 2)
                gsc = None
                if gscale_d is not None:
                    gsc = sbO("gsc", [128, KC], F32)
                    small_load(gsc[:], gscale_d[:, :], tagp + "gsc")
                    S.last_w[("gsc", tagp + "WoB")] = S.last_w[tagp + "gsc"]
                gpo = sbO("gpo", [128, D], F32)
                small_load(gpo[:], gpost_d[:, :], tagp + "gpo")
                load_weight(sbO, psO, w_d, WoB, KC, D, gsc, stage, ["dve", "pool", "act"], tagp + "WoB")
                xr = Buf(sbO, "xr", [128, D], F32, 3)
                yf = Buf(sbO, "yf", [128, D], F32, 2)
                junk = sbO("junk", [128, 512], BF16)
                ssqy = Buf(sbO, "ssqy", [128, 2], F32, 2)
                ssq1 = Buf(sbO, "ssq1", [128, 1], F32, 2)
                sq = Buf(sbO, "sq", [128, 1], F32, 2)
                rstd = Buf(sbO, "rstd", [128, 1], F32, 2)
                PY = Buf(psO, "PY", [128, 2, 512], F32, 3)
                zload, zget = load_z(sbO)

                def loads(t):
                    zload(t)
                    xr_t, xr_k = xr(t)
                    S.I("sp", "dma_start", out=xr_t[:], in_=resid[t * 128:(t + 1) * 128, :], writes=[xr_k], dma_key=xr_k)
                for t in range(min(2, NT)):
                    loads(t)
                for t in range(NT):
                    if t + 2 < NT:
                        loads(t + 2)
                    lhs_of, zkeys = zget(t)
                    xr_t, xr_k = xr(t)
                    py_t, py_k = PY(t)
                    sy_t, sy_k = ssqy(t)
                    for half in range(2):
                        for c in range(KC):
                            S.I("pe", "matmul", py_t[:, half, :], lhsT=lhs_of(c), rhs=WoB[:, c, half * 512:(half + 1) * 512], start=(c == 0), stop=(c == KC - 1),
                                  reads=zkeys + [wkey_for(tagp + "WoB", c, half * 512)], writes=[(py_k, half)])
                        S.I("act", "activation", out=junk[:], in_=py_t[:, half, :], func=AF.Square, accum_out=sy_t[:, half:half + 1],
                              reads=[(py_k, half)], writes=[tagp + "junk", (sy_k, half)])
                    s1_t, s1_k = ssq1(t)
                    S.I("dve", "tensor_tensor", out=s1_t[:], in0=sy_t[:, 0:1], in1=sy_t[:, 1:2], op=ALU.add,
                          reads=[(sy_k, 0), (sy_k, 1)], writes=[s1_k])
                    r_t, r_k = rstd_chain(t, s1_t[:], s1_k, sq, rstd, 1.0 / D, tagp)
                    yf_t, yf_k = yf(t)
                    for half in range(2):
                        hs = slice(half * 512, (half + 1) * 512)
                        S.I("dve", "scalar_tensor_tensor", out=yf_t[:, hs], in0=py_t[:, half, :], scalar=r_t[:], in1=gpo[:, hs], op0=ALU.mult, op1=ALU.mult,
                              reads=[(py_k, half), r_k, tagp + "gpo"], writes=[(yf_k, half)])
                        S.I("pool", "tensor_tensor", out=yf_t[:, hs], in0=yf_t[:, hs], in1=xr_t[:, hs], op=ALU.add,
                              reads=[(yf_k, half), xr_k], writes=[(yf_k, half)])
                    o = S.I("sp", "dma_start", out=dst[t * 128:(t + 1) * 128, :], in_=yf_t[:],
                              reads=[(yf_k, 0), (yf_k, 1)], writes=[(tagp + "dst", t)], dma_key=(tagp + "yst", t % 2))
                    final_ops.append(o)
            S.barrier()

        final_ops = []

        def load_z_A(sbO):
            zT = Buf(sbO, "zTin", [128, 16, 128], BF16, 3)

            def load(t):
                z_t, z_k = zT(t)
                S.I("sp", "dma_start", out=z_t[:], in_=zTa[t], writes=[z_k], dma_key=z_k)

            def get(t):
                z_t, z_k = zT(t)
                return (lambda c: z_t[:, c, :]), [z_k]
            return load, get

        outproj_phase("A2", w_out_a, 16, gn_gain_a, g_post_a, load_z_A, x, (out if STOP_AFTER == "A2" else h1), 1)
        if STOP_AFTER == "A2":
            S.emit(final_wait_ops=list(final_ops))
            return nc
        final_ops.clear()

        with contextlib.ExitStack() as stB:
            def sbB(name, shape, dt):
                return stB.enter_context(nc.sbuf_tensor("B1" + name, list(shape), dt))

            def psB(name, shape, dt):
                return stB.enter_context(nc.psum_tensor("B1p" + name, list(shape), dt))
            stage = Buf(sbB, "stage", [128, 2048], F32, 2)
            WkvaB = sbB("WkvaB", [128, 8, 320], BF16)
            WukB = sbB("WukB", [128, 2, D], BF16)
            WuvB = sbB("WuvB", [128, 2, D], BF16)
            WinbB = sbB("WinbB", [128, 8, 1408], BF16)
            WuqB = sbB("WuqB", [128, 3, 1536], BF16)
            gkv = sbB("gkv", [128, 8], F32)
            gkl = sbB("gkl", [128, 2], F32)
            gpb = sbB("gpb", [128, 8], F32)
            gql = sbB("gql", [128, 3], F32)
            for nm, dst_, src_ in (("gkv", gkv, g_kv), ("gkl", gkl, g_kv_lat), ("gpb", gpb, g_pre_b), ("gql", gql, g_q_lat)):
                small_load(dst_[:], src_[:, :], "B1" + nm)
            S.last_w[("gsc", "WkvaB")] = S.last_w["B1gkv"]
            S.last_w[("gsc", "WukB")] = S.last_w["B1gkl"]
            S.last_w[("gsc", "WuvB")] = S.last_w["B1gkl"]
            S.last_w[("gsc", "WinbB")] = S.last_w["B1gpb"]
            S.last_w[("gsc", "WuqB")] = S.last_w["B1gql"]
            engs = ["dve", "pool", "act"]
            load_weight(sbB, psB, w_kv_a, WkvaB, 8, 320, gkv, stage, engs, "WkvaB")
            load_weight(sbB, psB, w_in_b, WinbB, 8, 1408, gpb, stage, engs, "WinbB")
            load_weight(sbB, psB, w_uk, WukB, 2, D, gkl, stage, engs, "WukB")
            load_weight(sbB, psB, w_uv, WuvB, 2, D, gkl, stage, engs, "WuvB")
            load_weight(sbB, psB, w_uq, WuqB, 3, 1536, gql, stage, engs, "WuqB")
            if STOP_AFTER == "B1w":
                S.emit(final_wait_ops=[S.ops[-1]])
                return nc
            if B1N is not None:
                S.limit = len(S.ops) + B1N
            if STOP_AFTER == "B1x":
                tst = sbB("tst", [128, D], F32)
                o = S.I("sp", "dma_start", out=tst[:], in_=h1[0:128, :], writes=["tst"], dma_key="tst")
                S.emit(final_wait_ops=[o])
                return nc
            if STOP_AFTER == "B1z":
                xt = Buf(sbB, "xt", [128, D], F32, 4)
                xt_t, xt_k = xt(0)
                o = S.I("sp", "dma_start", out=xt_t[:], in_=h1[0:128, :], writes=[xt_k], dma_key=xt_k)
                S.emit(final_wait_ops=[o])
                return nc
            if STOP_AFTER == "B1y":
                tst = sbB("tst", [128, D], F32)
                o = S.I("sp", "dma_start", out=tst[:], in_=x[0:128, :], writes=["tst"], dma_key="tst")
                S.emit(final_wait_ops=[o])
                return nc

            xt = Buf(sbB, "xt", [128, D], F32, 4)
            junk = sbB("junk", [128, D], BF16)
            ssq = Buf(sbB, "ssq", [128, 1], F32, 4)
            sq = Buf(sbB, "sq", [128, 1], F32, 4)
            rstd = Buf(sbB, "rstd", [128, 1], F32, 4)
            ssqc = Buf(sbB, "ssqc", [128, 1], F32, 4)
            sqc = Buf(sbB, "sqc", [128, 1], F32, 4)
            rstdc = Buf(sbB, "rstdc", [128, 1], F32, 4)
            ssqq = Buf(sbB, "ssqq", [128, 1], F32, 4)
            sqq = Buf(sbB, "sqq", [128, 1], F32, 4)
            rstdq = Buf(sbB, "rstdq", [128, 1], F32, 4)
            hb = Buf(sbB, "hb", [128, D], BF16, 4)
            hTb = Buf(sbB, "hTb", [128, 8, QB], BF16, 2)
            csb = Buf(sbB, "csb", [128, 2, 32], F32, 8)
            chat = Buf(sbB, "chat", [128, 256], BF16, 4)
            kro = Buf(sbB, "kro", [128, 2, 64], BF16, 4)
            ktmp = Buf(sbB, "ktmp", [128, 4, 32], F32, 4)
            chT = Buf(sbB, "chT", [128, 2, QB], BF16, 2)
            krTb = Buf(sbB, "krTb", [128, QB], BF16, 2)
            cqh = Buf(sbB, "cqh", [128, 384], BF16, 4)
            cqT = Buf(sbB, "cqT", [128, 3, QB], BF16, 2)
            knTs = Buf(sbB, "knTs", [128, 8, QB], BF16, 2)
            qnTs = Buf(sbB, "qnTs", [128, 8, QB], BF16, 2)
            sgTs = Buf(sbB, "sgTs", [128, 8, QB], BF16, 2)
            qrTs = Buf(sbB, "qrTs", [128, 4, QB], BF16, 2)
            vsb = Buf(sbB, "vsb", [128, D], BF16, 2)
            qtmp = Buf(sbB, "qtmp", [128, 4, 8, 32], F32, 2)
            qro = Buf(sbB, "qro", [128, 8, 64], BF16, 4)

            PT = psB("PT", [128, 8, 128], F32)
            PA = Buf(psB, "PA", [128, 512], F32, 2)
            PF = Buf(psB, "PF", [128, 512], F32, 2)
            PV = psB("PV", [128, 2, 512], F32)

            def B_load(t):
                xt_t, xt_k = xt(t)
                cs_t, cs_k = csb(t)
                S.I("sp", "dma_start", out=xt_t[:], in_=h1[t * 128:(t + 1) * 128, :], writes=[xt_k], dma_key=xt_k)
                S.I("sp", "dma_start", out=cs_t[:], in_=csB[:, t, :, :], writes=[(cs_k, 0), (cs_k, 1)], dma_key=cs_k)

            def B_sub(bq, sub):
                hTb_t, hTb_k = hTb(bq)
                chT_t, chT_k = chT(bq)
                krT_t, krT_k = krTb(bq)
                cqT_t, cqT_k = cqT(bq)
                qrT_t, qrT_k = qrTs(bq)
                t = bq * SUB + sub
                ts_ = slice(sub * 128, (sub + 1) * 128)
                xt_t, xt_k = xt(t)
                cs_t, cs_k = csb(t)
                ssq_t, ssq_k = ssq(t)
                S.I("act", "activation", out=junk[:], in_=xt_t[:], func=AF.Square, accum_out=ssq_t[:],
                      reads=[xt_k], writes=["B1junk", ssq_k])
                r_t, r_k = rstd_chain(t, ssq_t[:], ssq_k, sq, rstd, 1.0 / D, "b")
                hb_t, hb_k = hb(t)
                S.I("dve", "tensor_scalar", out=hb_t[:], in0=xt_t[:], scalar1=r_t[:], scalar2=None, op0=ALU.mult,
                      reads=[xt_k, r_k], writes=[hb_k])
                for k in range(8):
                    S.I("pe", "matmul", PT[:, k, :], lhsT=hb_t[:, k * 128:(k + 1) * 128], rhs=ident[:], start=True, stop=True,
                          reads=[hb_k, "ident"], writes=[("PT", k // 4)])
                S.I("act", "activation", out=hTb_t[:, :, ts_], in_=PT[:], func=AF.Copy,
                      reads=[("PT", 0), ("PT", 1)], writes=[(hTb_k, sub)])
                yield
                pa_t, pa_k = PA(2 * t)
                for k in range(8):
                    S.I("pe", "matmul", pa_t[:, 0:320], lhsT=hTb_t[:, k, ts_], rhs=WkvaB[:, k, :], start=(k == 0), stop=(k == 7),
                          reads=[(hTb_k, sub), wkey_for("WkvaB", k, 0)], writes=[pa_k])
                sc_t, sc_k = ssqc(t)
                S.I("act", "activation", out=junk[:, 0:256], in_=pa_t[:, 0:256], func=AF.Square, accum_out=sc_t[:],
                      reads=[pa_k], writes=["B1junk", sc_k])
                rc_t, rc_k = rstd_chain(t, sc_t[:], sc_k, sqc, rstdc, 1.0 / 256, "c")
                ch_t, ch_k = chat(t)
                S.I("dve", "tensor_scalar", out=ch_t[:], in0=pa_t[:, 0:256], scalar1=rc_t[:], scalar2=None, op0=ALU.mult,
                      reads=[pa_k, rc_k], writes=[ch_k])
                kt_t, kt_k = ktmp(t)
                ko_t, ko_k = kro(t)
                x1 = pa_t[:, 256:288]
                x2 = pa_t[:, 288:320]
                S.I("dve", "tensor_tensor", out=kt_t[:, 0, :], in0=x1, in1=cs_t[:, 0, :], op=ALU.mult, reads=[pa_k, (cs_k, 0)], writes=[(kt_k, 0)])
                S.I("dve", "tensor_tensor", out=kt_t[:, 1, :], in0=x2, in1=cs_t[:, 1, :], op=ALU.mult, reads=[pa_k, (cs_k, 1)], writes=[(kt_k, 1)])
                S.I("dve", "tensor_tensor", out=kt_t[:, 2, :], in0=x1, in1=cs_t[:, 1, :], op=ALU.mult, reads=[pa_k, (cs_k, 1)], writes=[(kt_k, 2)])
                S.I("dve", "tensor_tensor", out=kt_t[:, 3, :], in0=x2, in1=cs_t[:, 0, :], op=ALU.mult, reads=[pa_k, (cs_k, 0)], writes=[(kt_k, 3)])
                S.I("pool", "tensor_tensor", out=ko_t[:, 0, 0:32], in0=kt_t[:, 0, :], in1=kt_t[:, 1, :], op=ALU.subtract, reads=[(kt_k, 0), (kt_k, 1)], writes=[(ko_k, 0)])
                S.I("pool", "tensor_tensor", out=ko_t[:, 0, 32:64], in0=kt_t[:, 2, :], in1=kt_t[:, 3, :], op=ALU.add, reads=[(kt_k, 2), (kt_k, 3)], writes=[(ko_k, 1)])
                S.I("pool", "tensor_copy", out=ko_t[:, 1, :], in_=ko_t[:, 0, :], reads=[(ko_k, 0), (ko_k, 1)], writes=[(ko_k, 2)])
                yield
                for c in range(2):
                    S.I("pe", "matmul", PT[:, c, :], lhsT=ch_t[:, c * 128:(c + 1) * 128], rhs=ident[:], start=True, stop=True,
                          reads=[ch_k, "ident"], writes=[("PT", c // 4)])
                S.I("pe", "matmul", PT[:, 2, :], lhsT=ko_t[:].rearrange("p a b -> p (a b)"), rhs=ident[:], start=True, stop=True,
                      reads=[(ko_k, 0), (ko_k, 1), (ko_k, 2), "ident"], writes=[("PT", 0)])
                S.I("act", "activation", out=chT_t[:, :, ts_], in_=PT[:, 0:2, :], func=AF.Copy,
                      reads=[("PT", 0)], writes=[(chT_k, sub)])
                S.I("dve", "tensor_copy", out=krT_t[:, ts_], in_=PT[:, 2, :],
                      reads=[("PT", 0)], writes=[(krT_k, sub)])
                yield
                pq_t, pq_k = PA(2 * t + 1)
                for k in range(8):
                    S.I("pe", "matmul", pq_t[:, 0:384], lhsT=hTb_t[:, k, ts_], rhs=WinbB[:, k, 0:384], start=(k == 0), stop=(k == 7),
                          reads=[(hTb_k, sub), wkey_for("WinbB", k, 0)], writes=[pq_k])
                sq_t2, sq_k2 = ssqq(t)
                S.I("act", "activation", out=junk[:, 0:384], in_=pq_t[:, 0:384], func=AF.Square, accum_out=sq_t2[:],
                      reads=[pq_k], writes=["B1junk", sq_k2])
                rq_t, rq_k = rstd_chain(t, sq_t2[:], sq_k2, sqq, rstdq, 1.0 / 384, "q")
                cq_t, cq_k = cqh(t)
                S.I("dve", "tensor_scalar", out=cq_t[:], in0=pq_t[:, 0:384], scalar1=rq_t[:], scalar2=None, op0=ALU.mult,
                      reads=[pq_k, rq_k], writes=[cq_k])
                for c in range(3):
                    S.I("pe", "matmul", PT[:, 4 + c, :], lhsT=cq_t[:, c * 128:(c + 1) * 128], rhs=ident[:], start=True, stop=True,
                          reads=[cq_k, "ident"], writes=[("PT", 1)])
                S.I("act", "activation", out=cqT_t[:, :, ts_], in_=PT[:, 4:7, :], func=AF.Copy,
                      reads=[("PT", 1)], writes=[(cqT_k, sub)])
                yield
                for half in range(2):
                    for c in range(2):
                        S.I("pe", "matmul", PV[:, half, :], lhsT=chT_t[:, c, ts_], rhs=WuvB[:, c, half * 512:(half + 1) * 512], start=(c == 0), stop=(c == 1),
                              reads=[(chT_k, sub), wkey_for("WuvB", c, half * 512)], writes=[("PV", half)])
                v_t, v_k = vsb(t)
                S.I("act", "activation", out=v_t[:], in_=PV[:].rearrange("p a b -> p (a b)"), func=AF.Copy,
                      reads=[("PV", 0), ("PV", 1)], writes=[v_k])
                S.I("sp", "dma_start", out=vS[t], in_=v_t[:], reads=[v_k], writes=[("vS", t)], dma_key=("vst", t % 2))
                yield
                if 'q' not in SKIP:
                    pr_t, pr_k = PF(2 * t)
                    wq_r = WuqB[:].rearrange("p c (h d) -> p c h d", h=8)
                    for c in range(3):
                        S.I("pe", "matmul", pr_t[:].rearrange("p (h d) -> p h d", h=8), lhsT=cqT_t[:, c, ts_], rhs=wq_r[:, c, :, 128:192], start=(c == 0), stop=(c == 2),
                              reads=[(cqT_k, sub), wkey_for("WuqB", c, 0)], writes=[pr_k])
                    p3 = pr_t[:].rearrange("p (h d) -> p h d", h=8)
                    q1 = p3[:, :, 0:32]
                    q2 = p3[:, :, 32:64]
                    cb_ = cs_t[:, 0, :].unsqueeze(1).to_broadcast([128, 8, 32])
                    sb_ = cs_t[:, 1, :].unsqueeze(1).to_broadcast([128, 8, 32])
                    qt_t, qt_k = qtmp(t)
                    qo_t, qo_k = qro(t)
                    S.I("dve", "tensor_tensor", out=qt_t[:, 0], in0=q1, in1=cb_, op=ALU.mult, reads=[pr_k, (cs_k, 0)], writes=[(qt_k, 0)])
                    S.I("dve", "tensor_tensor", out=qt_t[:, 1], in0=q2, in1=sb_, op=ALU.mult, reads=[pr_k, (cs_k, 1)], writes=[(qt_k, 1)])
                    S.I("dve", "tensor_tensor", out=qt_t[:, 2], in0=q1, in1=sb_, op=ALU.mult, reads=[pr_k, (cs_k, 1)], writes=[(qt_k, 2)])
                    S.I("dve", "tensor_tensor", out=qt_t[:, 3], in0=q2, in1=cb_, op=ALU.mult, reads=[pr_k, (cs_k, 0)], writes=[(qt_k, 3)])
                    S.I("pool", "tensor_tensor", out=qo_t[:, :, 0:32], in0=qt_t[:, 0], in1=qt_t[:, 1], op=ALU.subtract, reads=[(qt_k, 0), (qt_k, 1)], writes=[(qo_k, 0)])
                    S.I("pool", "tensor_tensor", out=qo_t[:, :, 32:64], in0=qt_t[:, 2], in1=qt_t[:, 3], op=ALU.add, reads=[(qt_k, 2), (qt_k, 3)], writes=[(qo_k, 1)])
                    for pr_i in range(4):
                        S.I("pe", "matmul", PT[:, pr_i, :], lhsT=qo_t[:, 2 * pr_i:2 * pr_i + 2, :].rearrange("p a b -> p (a b)"), rhs=ident[:], start=True, stop=True,
                              reads=[(qo_k, 0), (qo_k, 1), "ident"], writes=[("PT", 0)])
                    S.I("act", "activation", out=qrT_t[:, :, ts_], in_=PT[:, 0:4, :], func=AF.Copy,
                          reads=[("PT", 0)], writes=[(qrT_k, sub)])
                yield

            def B_feat(bq):
                hTb_t, hTb_k = hTb(bq)
                chT_t, chT_k = chT(bq)
                krT_t, krT_k = krTb(bq)
                cqT_t, cqT_k = cqT(bq)
                qrT_t, qrT_k = qrTs(bq)
                allsub = lambda key: [(key, s_) for s_ in range(SUB)]
                kn_t, kn_k = knTs(bq)
                qn_t, qn_k = qnTs(bq)
                sgT_t, sgT_k = sgTs(bq)
                fi = 0
                for h in range(8):
                    pf_t, pf_k = PF(fi); fi += 1
                    for c in range(2):
                        S.I("pe", "matmul", pf_t[:, 0:QB], lhsT=WukB[:, c, h * 128:(h + 1) * 128], rhs=chT_t[:, c, :], start=(c == 0), stop=(c == 1),
                              reads=allsub(chT_k) + [wkey_for("WukB", c, h * 128)], writes=[pf_k])
                    S.I("dve", "tensor_copy", out=kn_t[:, h, :], in_=pf_t[:, 0:QB], reads=[pf_k], writes=[(kn_k, h)])
                for h in range(8):
                    pf_t, pf_k = PF(fi); fi += 1
                    for c in range(3):
                        S.I("pe", "matmul", pf_t[:, 0:QB], lhsT=WuqB[:, c, h * 192:h * 192 + 128], rhs=cqT_t[:, c, :], start=(c == 0), stop=(c == 2),
                              reads=allsub(cqT_k) + [wkey_for("WuqB", c, h * 192), wkey_for("WuqB", c, h * 192 + 127)], writes=[pf_k])
                    S.I("dve", "tensor_copy", out=qn_t[:, h, :], in_=pf_t[:, 0:QB], reads=[pf_k], writes=[(qn_k, h)])
                for h in range(8):
                    pf_t, pf_k = PF(fi); fi += 1
                    for k in range(8):
                        S.I("pe", "matmul", pf_t[:, 0:QB], lhsT=WinbB[:, k, 384 + h * 128:384 + (h + 1) * 128], rhs=hTb_t[:, k, :], start=(k == 0), stop=(k == 7),
                              reads=allsub(hTb_k) + [wkey_for("WinbB", k, 384 + h * 128), wkey_for("WinbB", k, 384 + h * 128 + 127)], writes=[pf_k])
                    S.I("act", "activation", out=sgT_t[:, h, :], in_=pf_t[:, 0:QB], func=AF.Silu, reads=[pf_k], writes=[(sgT_k, h)])
                bs = slice(bq * QB, (bq + 1) * QB)
                S.I("sp", "dma_start", out=knT[:, :, bs].rearrange("h p t -> p h t"), in_=kn_t[:],
                      reads=[(kn_k, h) for h in range(8)], writes=[("knT", bq)], dma_key=("knst", bq % 2))
                S.I("sp", "dma_start", out=qnT[:, :, bs].rearrange("h p t -> p h t"), in_=qn_t[:],
                      reads=[(qn_k, h) for h in range(8)], writes=[("qnT", bq)], dma_key=("qnst", bq % 2))
                S.I("sp", "dma_start", out=sgT[:, :, bs].rearrange("h p t -> p h t"), in_=sgT_t[:],
                      reads=[(sgT_k, h) for h in range(8)], writes=[("sgT", bq)], dma_key=("sgst", bq % 2))
                S.I("sp", "dma_start", out=qrT[:, :, bs].rearrange("h p t -> p h t"), in_=qrT_t[:],
                      reads=allsub(qrT_k), writes=[("qrT", bq)], dma_key=("qrst", bq % 2))
                S.I("sp", "dma_start", out=krT[:, bs], in_=krT_t[:],
                      reads=allsub(krT_k), writes=[("krT", bq)], dma_key=("krst", bq % 2))
                yield

            def B_tail(bq):
                if bq + 1 < NQB:
                    for sub in range(SUB):
                        B_load((bq + 1) * SUB + sub)
                if bq >= 1:
                    yield from B_feat(bq - 1)
                yield

            for sub in range(SUB):
                B_load(sub)
            for bq in range(NQB + 1):
                gens = []
                if bq < NQB:
                    gens += [B_sub(bq, sub) for sub in range(SUB)]
                gens.append(B_tail(bq))
                interleave(*gens)
        S.barrier()

        if STOP_AFTER == "B1":
            S.emit(final_wait_ops=[S.ops[-1]])
            return nc
        SCALE = float((128 + 64) ** -0.5)
        with contextlib.ExitStack() as stC:
            def sbC(name, shape, dt):
                return stC.enter_context(nc.sbuf_tensor("B2" + name, list(shape), dt))

            def psC(name, shape, dt):
                return stC.enter_context(nc.psum_tensor("B2p" + name, list(shape), dt))
            ones = sbC("ones", [128, 128], BF16)
            S.I("pool", "memset", ones[:], 1.0, writes=["ones"])
            krS = sbC("krS", [128, S_len], BF16)
            S.I("sp", "dma_start", out=krS[:], in_=krT[:, :], writes=["krS"], dma_key="krS")
            knS = Buf(sbC, "knS", [128, S_len], BF16, 2)
            qnS = Buf(sbC, "qnS", [128, S_len], BF16, 2)
            sgS = Buf(sbC, "sgS", [128, S_len], BF16, 2)
            qrS = Buf(sbC, "qrS", [128, S_len], BF16, 2)
            v2S = Buf(sbC, "v2S", [128, NT, 256], BF16, 2)
            pTb = Buf(sbC, "pTb", [128, QB], BF16, 3)
            rden = Buf(sbC, "rden", [128, QB], F32, 2)
            ob = Buf(sbC, "ob", [128, QB], F32, 2)
            zb = Buf(sbC, "zb", [128, QB], BF16, 2)
            PSc = Buf(psC, "PSc", [128, 512], F32, 3)
            POc = Buf(psC, "POc", [128, 512], F32, 2)
            PDc = Buf(psC, "PDc", [128, 512], F32, 2)
            heads = {}

            def load_head(h):
                hp = h // 2
                kn_t, kn_k = knS(h)
                qn_t, qn_k = qnS(h)
                sg_t, sg_k = sgS(h)
                S.I("sp", "dma_start", out=kn_t[:], in_=knT[h], writes=[kn_k], dma_key=kn_k)
                S.I("sp", "dma_start", out=qn_t[:], in_=qnT[h], writes=[qn_k], dma_key=qn_k)
                S.I("sp", "dma_start", out=sg_t[:], in_=sgT[h], writes=[sg_k], dma_key=sg_k)
                qr_t, qr_k = qrS(hp)
                v2_t, v2_k = v2S(hp)
                if h % 2 == 0:
                    S.I("sp", "dma_start", out=qr_t[:], in_=qrT[hp], writes=[qr_k], dma_key=qr_k)
                    S.I("sp", "dma_start", out=v2_t[:], in_=vS[:, :, hp * 256:(hp + 1) * 256].rearrange("t p d -> p t d"), writes=[v2_k], dma_key=v2_k)
                heads[h] = (kn_t, kn_k, qn_t, qn_k, sg_t, sg_k, qr_t, qr_k, v2_t, v2_k)

            items = []
            blk = 0
            for h in range(8):
                for qb in range(NQB):
                    nkt = SUB * qb + SUB
                    for kt in range(nkt):
                        items.append((h, qb, kt, nkt, blk))
                    blk += 1

            def geom(i):
                h, qb, kt, nkt, blk_ = items[i]
                j = kt - SUB * qb
                c0 = 128 * j if j > 0 else 0
                return h, qb, kt, nkt, blk_, j, c0

            def qk(i):
                h, qb, kt, nkt, blk_, j, c0 = geom(i)
                kn_t, kn_k, qn_t, qn_k, sg_t, sg_k, qr_t, qr_k, v2_t, v2_k = heads[h]
                pb = 64 * (h % 2)
                qs = slice(qb * QB + c0, (qb + 1) * QB)
                ks = slice(kt * 128, (kt + 1) * 128)
                ps_t, ps_k = PSc(i)
                pt_t, pt_k = pTb(i)
                S.I("pe", "matmul", ps_t[:, c0:QB], lhsT=kn_t[:, ks], rhs=qn_t[:, qs], start=True, stop=False,
                    reads=[kn_k, qn_k], writes=[ps_k])
                S.I("pe", "matmul", ps_t[:, c0:QB], lhsT=krS[pb:pb + 64, ks], rhs=qr_t[pb:pb + 64, qs], start=False, stop=True,
                    reads=["krS", qr_k], writes=[ps_k])
                S.I("act", "activation", out=pt_t[:, c0:QB], in_=ps_t[:, c0:QB], func=AF.Exp, scale=SCALE,
                    reads=[ps_k], writes=[pt_k])
                if j >= 0:
                    S.I("pool", "memset", pt_t[64:128, c0:c0 + 64], 0.0, reads=[], writes=[pt_k])

            def pv(i):
                h, qb, kt, nkt, blk_, j, c0 = geom(i)
                kn_t, kn_k, qn_t, qn_k, sg_t, sg_k, qr_t, qr_k, v2_t, v2_k = heads[h]
                pt_t, pt_k = pTb(i)
                po_t, po_k = POc(blk_)
                pd_t, pd_k = PDc(blk_)
                S.I("pe", "matmul", po_t[:, c0:QB], lhsT=v2_t[:, kt, (h % 2) * 128:(h % 2) * 128 + 128], rhs=pt_t[:, c0:QB], start=(kt == 0), stop=(kt == nkt - 1),
                    reads=[v2_k, pt_k], writes=[po_k])
                S.I("pe", "matmul", pd_t[:, c0:QB], lhsT=ones[:], rhs=pt_t[:, c0:QB], start=(kt == 0), stop=(kt == nkt - 1),
                    reads=["ones", pt_k], writes=[pd_k])
                if kt == nkt - 1:
                    rd_t, rd_k = rden(blk_)
                    ob_t, ob_k = ob(blk_)
                    zb_t, zb_k = zb(blk_)
                    S.I("dve", "reciprocal", out=rd_t[:], in_=pd_t[:, 0:QB], reads=[pd_k], writes=[rd_k])
                    S.I("dve", "tensor_tensor", out=ob_t[:], in0=po_t[:, 0:QB], in1=rd_t[:], op=ALU.mult, reads=[po_k, rd_k], writes=[ob_k])
                    S.I("pool", "tensor_tensor", out=zb_t[:], in0=ob_t[:], in1=sg_t[:, qb * QB:(qb + 1) * QB], op=ALU.mult, reads=[ob_k, sg_k], writes=[zb_k])
                    S.I("sp", "dma_start", out=zTb[qb, :, h, :], in_=zb_t[:], reads=[zb_k], writes=[("zTb", qb, h)], dma_key=("zbst", blk_ % 2))
                    if qb == NQB - 1 and h + 2 < 8:
                        load_head(h + 2)

            load_head(0)
            load_head(1)
            LOOK = 2
            for i in range(min(LOOK, len(items))):
                qk(i)
            for i in range(len(items)):
                if i + LOOK < len(items):
                    qk(i + LOOK)
                pv(i)
        S.barrier()

        if STOP_AFTER == "B2":
            S.emit(final_wait_ops=[S.ops[-1]])
            return nc
        def load_z_B(sbO):
            zT = Buf(sbO, "zTin", [128, 8, QB], BF16, 3)
            done = set()

            def load(t):
                bq = t // SUB
                for b_ in (bq, bq + 1):
                    if b_ < NQB and b_ not in done:
                        done.add(b_)
                        z_t, z_k = zT(b_)
                        S.I("sp", "dma_start", out=z_t[:], in_=zTb[b_], writes=[z_k], dma_key=z_k)

            def get(t):
                bq, sub = divmod(t, SUB)
                z_t, z_k = zT(bq)
                return (lambda c: z_t[:, c, sub * 128:(sub + 1) * 128]), [z_k]
            return load, get

        outproj_phase("B3", w_out_b, 8, None, g_post_b, load_z_B, h1, out, 1)
        S.emit(final_wait_ops=list(final_ops))
    return nc


def _chunkT(w, K):
    n = w.shape[1]
    return np.ascontiguousarray(w.reshape(K, 128, n).transpose(1, 0, 2))


def _vecT(g, K):
    return np.ascontiguousarray(g.reshape(K, 128).T)


def _rope_tables(S_len, half):
    inv = (np.float32(10000.0) ** (-np.arange(half, dtype=np.float32) / np.float32(half))).astype(np.float32)
    ang = (np.arange(S_len, dtype=np.float32)[:, None] * inv[None, :]).astype(np.float32)
    c = np.cos(ang).astype(np.float32)
    s = np.sin(ang).astype(np.float32)
    nt = S_len // 128
    lay = lambda a: np.ascontiguousarray(a.reshape(nt, 128, half).transpose(1, 0, 2))
    return lay(c), lay(s)


def _decay_tables():
    h = np.arange(NH, dtype=np.float64)
    lg = np.log(1.0 - np.exp2(-5.0 - h))
    i = np.arange(128, dtype=np.float64)
    xi = np.exp(lg[:, None] * (i + 1.0)[None])
    zeta = np.exp(lg[:, None] * (127.0 - i)[None])
    ch = (np.arange(128) // 64)
    c_ = i[:, None]
    m_ = i[None, :]
    same = ch[:, None] == ch[None, :]
    prev = ch[None, :] < ch[:, None]
    Dm = np.zeros((NH, 128, 128))
    for hh in range(NH):
        Dm[hh] = np.where(same, np.exp(lg[hh] * np.abs(c_ - m_)), np.where(prev, np.exp(lg[hh] * (c_ - m_)), 0.0))
    T2 = Dm / (xi[:, :, None] * zeta[:, None, :])
    t2t = np.ascontiguousarray(T2.transpose(2, 0, 1)).astype(np.float32)
    dxi = np.zeros((128, NH, 128), np.float32)
    for hh in range(NH):
        dxi[np.arange(128), hh, np.arange(128)] = xi[hh]
    zs = np.ascontiguousarray((zeta * (128.0 ** -0.5)).T).astype(np.float32)
    return t2t, dxi, zs


def _prep_shared(inp, S_len):
    f = lambda a: np.asarray(a, dtype=np.float32)
    cA, sA = _rope_tables(S_len, 64)
    cB, sB = _rope_tables(S_len, 32)
    t2t, dxi, zs = _decay_tables()
    return {
        "w_in_a": _chunkT(f(inp["w_in_a"])[0], 8),
        "g_pre_a": _vecT(f(inp["g_pre_a"])[0], 8),
        "gn_gain_a": _vecT(f(inp["gn_gain_a"])[0], 16),
        "w_out_a": _chunkT(f(inp["w_out_a"])[0], 16),
        "g_post_a": np.ascontiguousarray(np.broadcast_to(f(inp["g_post_a"])[0][None, :], (128, D))),
        "g_kv": _vecT(f(inp["g_kv"]), 8),
        "w_kv_a": _chunkT(f(inp["w_kv_a"]), 8),
        "g_kv_lat": _vecT(f(inp["g_kv_lat"]), 2),
        "w_uk": _chunkT(f(inp["w_uk"]), 2),
        "w_uv": _chunkT(f(inp["w_uv"]), 2),
        "g_pre_b": _vecT(f(inp["g_pre_b"])[0], 8),
        "w_in_b": _chunkT(f(inp["w_in_b"])[0], 8),
        "g_q_lat": _vecT(f(inp["g_q_lat"])[0], 3),
        "w_uq": _chunkT(f(inp["w_uq"])[0], 3),
        "w_out_b": _chunkT(f(inp["w_out_b"])[0], 8),
        "g_post_b": np.ascontiguousarray(np.broadcast_to(f(inp["g_post_b"])[0][None, :], (128, D))),
        "csA": np.ascontiguousarray(np.stack([cA, sA], axis=2)), "csB": np.ascontiguousarray(np.stack([cB, sB], axis=2)),
        "t2t": t2t, "dxi": dxi, "zs": zs,
        "ident": np.eye(128, dtype=np.float32),
    }


def kernel(**inputs):
    x = np.asarray(inputs["x"], dtype=np.float32)
    B, S_len, _ = x.shape
    shared = _prep_shared(inputs, S_len)
    nc = build(S_len)
    in_maps = []
    for b in range(B):
        m = dict(shared)
        m["x"] = np.ascontiguousarray(x[b])
        in_maps.append(m)
    res = run_bass_kernel_spmd(nc, in_maps, core_ids=list(range(B)))
    return np.stack([np.asarray(r["out"], dtype=np.float32) for r in res.results], axis=0)
```

```python
import contextlib
import numpy as np
import concourse.bass as bass
import concourse.mybir as mybir
from concourse.bass_utils import run_bass_kernel_spmd

F32 = mybir.dt.float32
BF16 = mybir.dt.bfloat16
ALU = mybir.AluOpType
AF = mybir.ActivationFunctionType

D = 1024
EPS = 1e-6
NH = 8
SEQ = 4096
STOP_AFTER = None
SKIP = ''
B1N = None


class _Stop(Exception):
    pass


class Op:
    __slots__ = ("eng", "fn", "deps", "idx", "signal", "dma_key", "cnt")

    def __init__(self, eng, fn, deps, dma_key):
        self.eng = eng
        self.fn = fn
        self.deps = deps
        self.dma_key = dma_key
        self.signal = False
        self.cnt = None


class Sched:
    COMPUTE = ("pe", "dve", "act", "pool")

    def __init__(self, nc):
        self.nc = nc
        self.ops = []
        self.last_w = {}
        self.readers = {}
        self.bar = []
        self.last_eng = {}
        self.last_key = {}

    PSUM_ROOTS = {"PT", "PS", "PO", "PU", "PP", "PY", "PV", "PA", "PF", "PSc", "POc", "PDc"}

    @classmethod
    def _excl(cls, key):
        while not isinstance(key, str):
            key = key[0]
        return key in cls.PSUM_ROOTS

    def add(self, eng, fn, reads=(), writes=(), dma_key=None):
        ex = [r for r in reads if self._excl(r)]
        if ex:
            reads = [r for r in reads if not self._excl(r)]
            writes = list(writes) + [r for r in ex if r not in writes]
        deps = list(self.bar)
        for r in reads:
            w = self.last_w.get(r)
            if w is not None:
                deps.append(w)
        for r in writes:
            w = self.last_w.get(r)
            if w is not None:
                deps.append(w)
            deps.extend(self.readers.get(r, ()))
        if getattr(self, "limit", None) is not None and len(self.ops) >= self.limit:
            raise _Stop()
        op = Op(eng, fn, deps, dma_key)
        op.idx = len(self.ops)
        self.ops.append(op)
        for r in reads:
            self.readers.setdefault(r, []).append(op)
        for r in writes:
            self.last_w[r] = op
            self.readers[r] = []
        if dma_key is None:
            self.last_eng[eng] = op
        else:
            self.last_key[dma_key] = op
        return op

    def I(self, eng, meth, *args, reads=(), writes=(), dma_key=None, **kw):
        return self.add(eng, lambda e: getattr(e, meth)(*args, **kw), reads=reads, writes=writes, dma_key=dma_key)

    def barrier(self):
        self.bar = list(self.last_eng.values()) + list(self.last_key.values())
        self.last_w = {}
        self.readers = {}

    @staticmethod
    def _pe_pe(d, op):
        return d.dma_key is None and op.dma_key is None and d.eng == "pe" and op.eng == "pe"

    def emit(self, final_wait_ops=()):
        nc = self.nc
        for op in self.ops:
            for d in op.deps:
                if not self._pe_pe(d, op):
                    d.signal = True
        for op in final_wait_ops:
            op.signal = True
        for op in self.ops:
            if op.dma_key is not None:
                op.signal = True
        eng_cnt = {e: 0 for e in self.COMPUTE}
        key_cnt = {}
        for op in self.ops:
            if not op.signal:
                continue
            if op.dma_key is not None:
                key_cnt[op.dma_key] = key_cnt.get(op.dma_key, 0) + 1
                op.cnt = key_cnt[op.dma_key] * 16
            else:
                eng_cnt[op.eng] += 1
                op.cnt = eng_cnt[op.eng]
        sems = {}
        with contextlib.ExitStack() as st:
            for e in self.COMPUTE:
                sems[("eng", e)] = st.enter_context(nc.semaphore("s_" + e))
            for i, k in enumerate(sorted(key_cnt, key=str)):
                sems[("dma", k)] = st.enter_context(nc.semaphore("d%d" % i))
            block = st.enter_context(nc.Block())
            queues = {}
            for op in self.ops:
                queues.setdefault(op.eng, []).append(op)
            engmap = {"pe": ("tensor", nc.tensor), "dve": ("vector", nc.vector),
                      "act": ("scalar", nc.scalar), "pool": ("gpsimd", nc.gpsimd),
                      "sp": ("sync", nc.sync)}

            def semof(op):
                if op.dma_key is not None:
                    return sems[("dma", op.dma_key)]
                return sems[("eng", op.eng)]

            def run_queue(eng, ops, final):
                known = {}
                for op in ops:
                    need = {}
                    for d in op.deps:
                        if d.cnt is None or self._pe_pe(d, op):
                            continue
                        s = semof(d)
                        key = id(s)
                        if known.get(key, 0) >= d.cnt:
                            continue
                        if key not in need or need[key][1] < d.cnt:
                            need[key] = (s, d.cnt)
                    for key, (s, c) in need.items():
                        eng.wait_ge(s, c)
                        known[key] = c
                    ins = op.fn(eng)
                    if op.signal:
                        ins.then_inc(semof(op), 16 if op.dma_key is not None else 1)
                for op in final:
                    eng.wait_ge(semof(op), op.cnt)

            for ename, (attr, eng) in engmap.items():
                ops = queues.get(ename, [])
                final = list(final_wait_ops) if ename == "sp" else []
                if not ops and not final:
                    continue

                def body(e, _ops=ops, _final=final):
                    run_queue(e, _ops, _final)
                getattr(block, attr)(body)


class Buf:
    def __init__(self, alloc, name, shape, dt, nbuf=1):
        self.t = [alloc("%s_%d" % (name, i), shape, dt) for i in range(nbuf)]
        self.name = name
        self.n = nbuf

    def __call__(self, i=0):
        j = i % self.n
        return self.t[j], (self.name, j)


def build(S_len=SEQ):
    ctx = {}
    try:
        return _build(S_len, ctx)
    except _Stop:
        return ctx["nc"]


def _build(S_len, ctx):
    NT = S_len // 128
    QB = min(512, S_len)
    NQB = S_len // QB
    SUB = QB // 128
    nc = bass.Bass("TRN2", target_bir_lowering=False)

    def din(name, shape, dt=F32):
        return nc.dram_tensor(name, list(shape), dt, kind="ExternalInput").ap()

    def dscr(name, shape, dt):
        return nc.dram_tensor(name, list(shape), dt, kind="Internal").ap()

    x = din("x", [S_len, D])
    w_in_a = din("w_in_a", [128, 8, 6144])
    g_pre_a = din("g_pre_a", [128, 8])
    gn_gain_a = din("gn_gain_a", [128, 16])
    w_out_a = din("w_out_a", [128, 16, D])
    g_post_a = din("g_post_a", [128, D])
    g_kv = din("g_kv", [128, 8])
    w_kv_a = din("w_kv_a", [128, 8, 320])
    g_kv_lat = din("g_kv_lat", [128, 2])
    w_uk = din("w_uk", [128, 2, D])
    w_uv = din("w_uv", [128, 2, D])
    g_pre_b = din("g_pre_b", [128, 8])
    w_in_b = din("w_in_b", [128, 8, 1408])
    g_q_lat = din("g_q_lat", [128, 3])
    w_uq = din("w_uq", [128, 3, 1536])
    w_out_b = din("w_out_b", [128, 8, D])
    g_post_b = din("g_post_b", [128, D])
    csA = din("csA", [128, NT, 2, 64])
    csB = din("csB", [128, NT, 2, 32])
    t2t_d = din("t2t", [128, 8, 128])
    dxi_d = din("dxi", [128, 8, 128])
    ident_d = din("ident", [128, 128])
    zs_d = din("zs", [128, 8])
    gc_host = None
    out = nc.dram_tensor("out", [S_len, D], F32, kind="ExternalOutput").ap()

    zTa = dscr("zTa", [NT, 128, 16, 128], BF16)
    h1 = dscr("h1", [S_len, D], F32)
    knT = dscr("knT", [8, 128, S_len], BF16)
    krT = dscr("krT", [128, S_len], BF16)
    vS = dscr("vS", [NT, 128, D], BF16)
    qnT = dscr("qnT", [8, 128, S_len], BF16)
    qrT = dscr("qrT", [4, 128, S_len], BF16)
    sgT = dscr("sgT", [8, 128, S_len], BF16)
    zTb = dscr("zTb", [NQB, 128, 8, QB], BF16)

    GC = [float((1.0 - 2.0 ** (-5.0 - h)) ** 128) for h in range(NH)]

    S = Sched(nc)
    ctx["S"] = S
    ctx["nc"] = nc

    def stop_here(tag):
        if STOP_AFTER == tag:
            last = [o for o in S.ops if o.dma_key is not None][-1]
            S.emit(final_wait_ops=[last, S.ops[-1]])
            raise _Stop()

    def load_weight(sb, ps_alloc, src, dst, K, N, scale, stage, eng_cycle, tag):
        i = 0
        for k in range(K):
            for n0 in range(0, N, 2048):
                n1 = min(N, n0 + 2048)
                st_t, st_k = stage(i)
                S.I("sp", "dma_start", out=st_t[:, 0:n1 - n0], in_=src[:, k, n0:n1],
                      writes=[st_k], dma_key=st_k)
                eng = eng_cycle[i % len(eng_cycle)]
                rd = [st_k] + ([("gsc", tag)] if scale is not None else [])
                wkey = (tag, k, n0)
                if scale is None:
                    if eng == "act":
                        S.I("act", "activation", out=dst[:, k, n0:n1], in_=st_t[:, 0:n1 - n0], func=AF.Copy,
                              reads=rd, writes=[wkey])
                    else:
                        S.I(eng, "tensor_copy", out=dst[:, k, n0:n1], in_=st_t[:, 0:n1 - n0],
                              reads=rd, writes=[wkey])
                else:
                    if eng == "act":
                        S.I("act", "activation", out=dst[:, k, n0:n1], in_=st_t[:, 0:n1 - n0], func=AF.Copy, scale=scale[:, k:k + 1],
                              reads=rd, writes=[wkey])
                    else:
                        S.I(eng, "tensor_scalar", out=dst[:, k, n0:n1], in0=st_t[:, 0:n1 - n0], scalar1=scale[:, k:k + 1], scalar2=None, op0=ALU.mult,
                              reads=rd, writes=[wkey])
                i += 1

    def wkeys(tag, K, N):
        return [(tag, k, n0) for k in range(K) for n0 in range(0, N, 2048)]

    def wkey_for(tag, k, c0):
        return (tag, k, (c0 // 2048) * 2048)

    def small_load(dst, src, key):
        S.I("sp", "dma_start", out=dst, in_=src, writes=[key], dma_key=key)

    def rstd_chain(slot_i, ssq_ap, ssq_key, sq, rstd, inv_n, nm):
        sq_t, sq_k = sq(slot_i)
        r_t, r_k = rstd(slot_i)
        S.I("act", "activation", out=sq_t[:], in_=ssq_ap, func=AF.Sqrt, scale=inv_n, bias=epsT[:],
              reads=[ssq_key, "epsT"], writes=[sq_k])
        S.I("dve", "reciprocal", out=r_t[:], in_=sq_t[:], reads=[sq_k], writes=[r_k])
        return r_t, r_k

    with contextlib.ExitStack() as stk:
        def sb(name, shape, dt):
            return stk.enter_context(nc.sbuf_tensor("sb_" + name, list(shape), dt))

        def ps(name, shape, dt):
            return stk.enter_context(nc.psum_tensor("ps_" + name, list(shape), dt))

        epsT = sb("epsT", [128, 1], F32)
        ident = sb("ident", [128, 128], BF16)
        identf = sb("identf", [128, 128], F32)
        S.I("pool", "memset", epsT[:], EPS, writes=["epsT"])
        small_load(identf[:], ident_d[:, :], "identf")
        S.I("dve", "tensor_copy", out=ident[:], in_=identf[:], reads=["identf"], writes=["ident"])

        with contextlib.ExitStack() as stA:
            def sbA(name, shape, dt):
                return stA.enter_context(nc.sbuf_tensor("A1" + name, list(shape), dt))

            def psA(name, shape, dt):
                return stA.enter_context(nc.psum_tensor("A1p" + name, list(shape), dt))

            WinB = sbA("WinB", [128, 8, 6144], BF16)
            stage = Buf(sbA, "stage", [128, 2048], F32, 2)
            gpa = sbA("gpa", [128, 8], F32)
            t2t = sbA("t2t", [128, 8, 128], F32)
            dxi = sbA("dxi", [128, 8, 128], BF16)
            zs = sbA("zs", [128, 8], F32)
            xt = Buf(sbA, "xt", [128, D], F32, 2)
            junk = sbA("junk", [128, D], BF16)
            ssq = Buf(sbA, "ssq", [128, 1], F32, 2)
            sq = Buf(sbA, "sq", [128, 1], F32, 2)
            rstd = Buf(sbA, "rstd", [128, 1], F32, 2)
            hb = Buf(sbA, "hb", [128, D], BF16, 1)
            hT = Buf(sbA, "hT", [128, 8, 128], BF16, 2)
            cs = Buf(sbA, "cs", [128, 2, 64], F32, 2)
            rtmp = Buf(sbA, "rtmp", [128, 4, 4, 64], F32, 2)
            qr = Buf(sbA, "qr", [128, 8, 128], BF16, 1)
            kr = Buf(sbA, "kr", [128, 8, 128], BF16, 2)
            qT = Buf(sbA, "qT", [128, 8, 128], BF16, 2)
            kT = Buf(sbA, "kT", [128, 8, 128], BF16, 2)
            vt = Buf(sbA, "vt", [128, 8, 256], BF16, 2)
            sg = Buf(sbA, "sg", [128, 2048], BF16, 2)
            pS = Buf(sbA, "pS", [128, 8, 128], BF16, 1)
            stats = sbA("stats", [128, 8, 6], F32)
            mv = sbA("mv", [128, 8, 2], F32)
            gsq = sbA("gsq", [128, 8], F32)
            grs = sbA("grs", [128, 8], F32)
            gnb = sbA("gnb", [128, 8], F32)
            on = Buf(sbA, "on", [128, 2048], BF16, 1)
            zT = Buf(sbA, "zT", [128, 16, 128], BF16, 2)
            Rf = sbA("Rf", [128, 8, 256], F32)
            Rb = sbA("Rb", [128, 8, 256], BF16)

            PT = psA("PT", [128, 8, 128], F32)
            PP = Buf(psA, "PP", [128, 512], F32, 2)
            PS_ = psA("PS", [128, 4, 128], F32)
            PO = Buf(psA, "PO", [128, 2, 256], F32, 2)
            PU = psA("PU", [128, 2, 256], F32)

            small_load(gpa[:], g_pre_a[:, :], "gpa")
            S.last_w[("gsc", "WinB")] = S.last_w["gpa"]
            small_load(t2t[:], t2t_d[:, :, :], "t2t")
            small_load(zs[:], zs_d[:, :], "zs")
            st_t, st_k = stage(0)
            S.I("sp", "dma_start", out=st_t[:, 0:1024], in_=dxi_d.rearrange("p h c -> p (h c)"), writes=[st_k], dma_key=st_k)
            S.I("dve", "tensor_copy", out=dxi[:].rearrange("p h c -> p (h c)"), in_=st_t[:, 0:1024], reads=[st_k], writes=["dxi"])
            load_weight(sbA, psA, w_in_a, WinB, 8, 6144, gpa, stage, ["dve", "pool", "act"], "WinB")

            def A_load(t):
                xt_t, xt_k = xt(t)
                cs_t, cs_k = cs(t)
                S.I("sp", "dma_start", out=xt_t[:], in_=x[t * 128:(t + 1) * 128, :], writes=[xt_k], dma_key=xt_k)
                S.I("sp", "dma_start", out=cs_t[:], in_=csA[:, t, :, :], writes=[(cs_k, 0), (cs_k, 1)], dma_key=cs_k)

            def A_S1(t):
                xt_t, xt_k = xt(t)
                cs_t, cs_k = cs(t)
                ssq_t, ssq_k = ssq(t)
                S.I("act", "activation", out=junk[:], in_=xt_t[:], func=AF.Square, accum_out=ssq_t[:],
                      reads=[xt_k], writes=["junk", ssq_k])
                r_t, r_k = rstd_chain(t, ssq_t[:], ssq_k, sq, rstd, 1.0 / D, "x")
                hb_t, hb_k = hb(t)
                S.I("dve", "tensor_scalar", out=hb_t[:], in0=xt_t[:], scalar1=r_t[:], scalar2=None, op0=ALU.mult,
                      reads=[xt_k, r_k], writes=[hb_k])
                for k in range(8):
                    S.I("pe", "matmul", PT[:, k, :], lhsT=hb_t[:, k * 128:(k + 1) * 128], rhs=ident[:], start=True, stop=True,
                          reads=[hb_k, "ident"], writes=[("PT", k // 4)])
                hT_t, hT_k = hT(t)
                S.I("act", "activation", out=hT_t[:], in_=PT[:], func=AF.Copy,
                      reads=[("PT", 0), ("PT", 1)], writes=[hT_k])
                qr_t, qr_k = qr(t)
                kr_t, kr_k = kr(t)
                vt_t, vt_k = vt(t)
                sg_t, sg_k = sg(t)
                for cb in range(12):
                    pp_t, pp_k = PP(cb)
                    for k in range(8):
                        S.I("pe", "matmul", pp_t[:], lhsT=hT_t[:, k, :], rhs=WinB[:, k, cb * 512:(cb + 1) * 512], start=(k == 0), stop=(k == 7),
                              reads=[hT_k, wkey_for("WinB", k, cb * 512)], writes=[pp_k])
                    if cb < 4:
                        dst_t, dst_k = (qr_t, qr_k) if cb < 2 else (kr_t, kr_k)
                        hh = (cb % 2) * 4
                        p3 = pp_t[:].rearrange("p (h d) -> p h d", h=4)
                        x1 = p3[:, :, 0:64]
                        x2 = p3[:, :, 64:128]
                        cosb = cs_t[:, 0, :].unsqueeze(1).to_broadcast([128, 4, 64])
                        sinb = cs_t[:, 1, :].unsqueeze(1).to_broadcast([128, 4, 64])
                        tm_t, tm_k = rtmp(cb)
                        S.I("dve", "tensor_tensor", out=tm_t[:, 0], in0=x1, in1=cosb, op=ALU.mult,
                              reads=[pp_k, (cs_k, 0)], writes=[(tm_k, 0)])
                        S.I("dve", "tensor_tensor", out=tm_t[:, 1], in0=x2, in1=sinb, op=ALU.mult,
                              reads=[pp_k, (cs_k, 1)], writes=[(tm_k, 1)])
                        S.I("dve", "tensor_tensor", out=tm_t[:, 2], in0=x1, in1=sinb, op=ALU.mult,
                              reads=[pp_k, (cs_k, 1)], writes=[(tm_k, 2)])
                        S.I("dve", "tensor_tensor", out=tm_t[:, 3], in0=x2, in1=cosb, op=ALU.mult,
                              reads=[pp_k, (cs_k, 0)], writes=[(tm_k, 3)])
                        S.I("pool", "tensor_tensor", out=dst_t[:, hh:hh + 4, 0:64], in0=tm_t[:, 0], in1=tm_t[:, 1], op=ALU.subtract,
                              reads=[(tm_k, 0), (tm_k, 1)], writes=[(dst_k, cb % 2, 0)])
                        S.I("pool", "tensor_tensor", out=dst_t[:, hh:hh + 4, 64:128], in0=tm_t[:, 2], in1=tm_t[:, 3], op=ALU.add,
                              reads=[(tm_k, 2), (tm_k, 3)], writes=[(dst_k, cb % 2, 1)])
                    elif cb < 8:
                        for j in range(2):
                            h = (cb - 4) * 2 + j
                            S.I("act", "activation", out=vt_t[:, h, :], in_=pp_t[:, j * 256:(j + 1) * 256], func=AF.Copy, scale=zs[:, h:h + 1],
                                  reads=[pp_k, "zs"], writes=[(vt_k, h)])
                    else:
                        c0 = (cb - 8) * 512
                        S.I("act", "activation", out=sg_t[:, c0:c0 + 512], in_=pp_t[:], func=AF.Silu,
                              reads=[pp_k], writes=[(sg_k, cb - 8)])
                for h in range(8):
                    S.I("pe", "matmul", PT[:, h, :], lhsT=qr_t[:, h, :], rhs=dxi[:, h, :], start=True, stop=True,
                          reads=[(qr_k, h // 4, 0), (qr_k, h // 4, 1), "dxi"], writes=[("PT", h // 4)])
                qT_t, qT_k = qT(t)
                S.I("dve", "tensor_copy", out=qT_t[:], in_=PT[:], reads=[("PT", 0), ("PT", 1)], writes=[qT_k])
                for h in range(8):
                    S.I("pe", "matmul", PT[:, h, :], lhsT=kr_t[:, h, :], rhs=ident[:], start=True, stop=True,
                          reads=[(kr_k, h // 4, 0), (kr_k, h // 4, 1), "ident"], writes=[("PT", h // 4)])
                kT_t, kT_k = kT(t)
                S.I("act", "activation", out=kT_t[:], in_=PT[:], func=AF.Copy, reads=[("PT", 0), ("PT", 1)], writes=[kT_k])

            def A_S2(t):
                qT_t, qT_k = qT(t)
                kT_t, kT_k = kT(t)
                kr_t, kr_k = kr(t)
                vt_t, vt_k = vt(t)
                sg_t, sg_k = sg(t)
                pS_t, pS_k = pS(t)
                on_t, on_k = on(t)
                for g in range(2):
                    for j in range(4):
                        h = 4 * g + j
                        S.I("pe", "matmul", PS_[:, j, :], lhsT=kT_t[:, h, :], rhs=qT_t[:, h, :], start=True, stop=True,
                              reads=[kT_k, qT_k], writes=["PS"])
                    S.I("dve", "tensor_tensor", out=pS_t[:, 4 * g:4 * g + 4, :], in0=PS_[:], in1=t2t[:, 4 * g:4 * g + 4, :], op=ALU.mult,
                          reads=["PS"] + ["t2t"], writes=[(pS_k, g)])
                for hp in range(4):
                    po_t, po_k = PO(hp)
                    for j in range(2):
                        h = 2 * hp + j
                        S.I("pe", "matmul", po_t[:, j, :], lhsT=pS_t[:, h, :], rhs=vt_t[:, h, :], start=True, stop=(t == 0),
                              reads=[(pS_k, h // 4), (vt_k, h)], writes=[po_k])
                        if t > 0:
                            S.I("pe", "matmul", po_t[:, j, :], lhsT=qT_t[:, h, :], rhs=Rb[:, h, :], start=False, stop=True,
                                  reads=[qT_k, ("Rb", hp)], writes=[po_k])
                    for j in range(2):
                        h = 2 * hp + j
                        S.I("dve", "bn_stats", out=stats[:, h, :], in_=po_t[:, j, :],
                              reads=[po_k], writes=[("stats", h)])
                        S.I("dve", "bn_aggr", out=mv[:, h, :], in_=stats[:, h, :],
                              reads=[("stats", h)], writes=[("mv", h)])
                    pr = slice(2 * hp, 2 * hp + 2)
                    S.I("act", "activation", out=gsq[:, pr], in_=mv[:, pr, 1], func=AF.Sqrt, bias=epsT[:],
                          reads=[("mv", 2 * hp), ("mv", 2 * hp + 1), "epsT"], writes=[("gsq", hp)])
                    S.I("dve", "reciprocal", out=grs[:, pr], in_=gsq[:, pr], reads=[("gsq", hp)], writes=[("grs", hp)])
                    S.I("dve", "scalar_tensor_tensor", out=gnb[:, pr], in0=mv[:, pr, 0], scalar=-1.0, in1=grs[:, pr], op0=ALU.mult, op1=ALU.mult,
                          reads=[("mv", 2 * hp), ("mv", 2 * hp + 1), ("grs", hp)], writes=[("gnb", hp)])
                    for j in range(2):
                        h = 2 * hp + j
                        S.I("act", "activation", out=on_t[:, h * 256:(h + 1) * 256], in_=po_t[:, j, :], func=AF.Identity, scale=grs[:, h:h + 1], bias=gnb[:, h:h + 1],
                              reads=[po_k, ("grs", hp), ("gnb", hp)], writes=[(on_k, h)])
                    c0 = hp * 512
                    S.I("pool", "tensor_tensor", out=on_t[:, c0:c0 + 512], in0=on_t[:, c0:c0 + 512], in1=sg_t[:, c0:c0 + 512], op=ALU.mult,
                          reads=[(on_k, 2 * hp), (on_k, 2 * hp + 1), (sg_k, hp)], writes=[(on_k, 2 * hp), (on_k, 2 * hp + 1)])
                    if t < NT - 1:
                        for j in range(2):
                            h = 2 * hp + j
                            S.I("pe", "matmul", PU[:, j, :], lhsT=kr_t[:, h, :], rhs=vt_t[:, h, :], start=True, stop=True,
                                  reads=[(kr_k, h // 4, 0), (kr_k, h // 4, 1), (vt_k, h)], writes=["PU"])
                            if t == 0:
                                S.I("dve", "tensor_copy", out=Rf[:, h, :], in_=PU[:, j, :],
                                      reads=["PU"], writes=[("Rf", h)])
                            else:
                                S.I("dve", "scalar_tensor_tensor", out=Rf[:, h, :], in0=Rf[:, h, :], scalar=GC[h], in1=PU[:, j, :], op0=ALU.mult, op1=ALU.add,
                                      reads=["PU", ("Rf", h)], writes=[("Rf", h)])
                        S.I("pool", "tensor_copy", out=Rb[:, pr, :], in_=Rf[:, pr, :],
                              reads=[("Rf", 2 * hp), ("Rf", 2 * hp + 1)], writes=[("Rb", hp)])
                zT_t, zT_k = zT(t)
                for r in range(2):
                    for c in range(8):
                        cc = 8 * r + c
                        S.I("pe", "matmul", PT[:, c, :], lhsT=on_t[:, cc * 128:(cc + 1) * 128], rhs=ident[:], start=True, stop=True,
                              reads=[(on_k, cc // 2), "ident"], writes=[("PT", c // 4)])
                    S.I("act", "activation", out=zT_t[:, 8 * r:8 * r + 8, :], in_=PT[:], func=AF.Copy,
                          reads=[("PT", 0), ("PT", 1)], writes=[(zT_k, r)])
                S.I("sp", "dma_start", out=zTa[t], in_=zT_t[:], reads=[(zT_k, 0), (zT_k, 1)], writes=[("zTa", t)], dma_key=("zTst", t % 2))

            A_load(0)
            if NT > 1:
                A_load(1)
            A_S1(0)
            for t in range(NT):
                if t + 1 < NT:
                    A_S1(t + 1)
                if t + 2 < NT:
                    A_load(t + 2)
                A_S2(t)
        S.barrier()
        if STOP_AFTER == "A1":
            S.emit(final_wait_ops=[S.ops[-1]])
            return nc

        def outproj_phase(tagp, w_d, KC, gscale_d, gpost_d, load_z, resid, dst, nblk_sub):
            with contextlib.ExitStack() as stO:
                def sbO(name, shape, dt):
                    return stO.enter_context(nc.sbuf_tensor(tagp + name, list(shape), dt))

                def psO(name, shape, dt):
                    return stO.enter_context(nc.psum_tensor(tagp + "p" + name, list(shape), dt))
                WoB = sbO("WoB", [128, KC, D], BF16)
                stage = Buf(sbO, "stage", [128, 2048], F32, 2)
                gsc = None
                if gscale_d is not None:
                    gsc = sbO("gsc", [128, KC], F32)
                    small_load(gsc[:], gscale_d[:, :], tagp + "gsc")
                    S.last_w[("gsc", tagp + "WoB")] = S.last_w[tagp + "gsc"]
                gpo = sbO("gpo", [128, D], F32)
                small_load(gpo[:], gpost_d[:, :], tagp + "gpo")
                load_weight(sbO, psO, w_d, WoB, KC, D, gsc, stage, ["dve", "pool", "act"], tagp + "WoB")
                xr = Buf(sbO, "xr", [128, D], F32, 3)
                yf = Buf(sbO, "yf", [128, D], F32, 2)
                junk = sbO("junk", [128, 512], BF16)
                ssqy = Buf(sbO, "ssqy", [128, 2], F32, 2)
                ssq1 = Buf(sbO, "ssq1", [128, 1], F32, 2)
                sq = Buf(sbO, "sq", [128, 1], F32, 2)
                rstd = Buf(sbO, "rstd", [128, 1], F32, 2)
                PY = Buf(psO, "PY", [128, 2, 512], F32, 3)
                zload, zget = load_z(sbO)

                def loads(t):
                    zload(t)
                    xr_t, xr_k = xr(t)
                    S.I("sp", "dma_start", out=xr_t[:], in_=resid[t * 128:(t + 1) * 128, :], writes=[xr_k], dma_key=xr_k)
                for t in range(min(2, NT)):
                    loads(t)
                for t in range(NT):
                    if t + 2 < NT:
                        loads(t + 2)
                    lhs_of, zkeys = zget(t)
                    xr_t, xr_k = xr(t)
                    py_t, py_k = PY(t)
                    sy_t, sy_k = ssqy(t)
                    for half in range(2):
                        for c in range(KC):
                            S.I("pe", "matmul", py_t[:, half, :], lhsT=lhs_of(c), rhs=WoB[:, c, half * 512:(half + 1) * 512], start=(c == 0), stop=(c == KC - 1),
                                  reads=zkeys + [wkey_for(tagp + "WoB", c, half * 512)], writes=[(py_k, half)])
                        S.I("act", "activation", out=junk[:], in_=py_t[:, half, :], func=AF.Square, accum_out=sy_t[:, half:half + 1],
                              reads=[(py_k, half)], writes=[tagp + "junk", (sy_k, half)])
                    s1_t, s1_k = ssq1(t)
                    S.I("dve", "tensor_tensor", out=s1_t[:], in0=sy_t[:, 0:1], in1=sy_t[:, 1:2], op=ALU.add,
                          reads=[(sy_k, 0), (sy_k, 1)], writes=[s1_k])
                    r_t, r_k = rstd_chain(t, s1_t[:], s1_k, sq, rstd, 1.0 / D, tagp)
                    yf_t, yf_k = yf(t)
                    for half in range(2):
                        hs = slice(half * 512, (half + 1) * 512)
                        S.I("dve", "scalar_tensor_tensor", out=yf_t[:, hs], in0=py_t[:, half, :], scalar=r_t[:], in1=gpo[:, hs], op0=ALU.mult, op1=ALU.mult,
                              reads=[(py_k, half), r_k, tagp + "gpo"], writes=[(yf_k, half)])
                        S.I("pool", "tensor_tensor", out=yf_t[:, hs], in0=yf_t[:, hs], in1=xr_t[:, hs], op=ALU.add,
                              reads=[(yf_k, half), xr_k], writes=[(yf_k, half)])
                    o = S.I("sp", "dma_start", out=dst[t * 128:(t + 1) * 128, :], in_=yf_t[:],
                              reads=[(yf_k, 0), (yf_k, 1)], writes=[(tagp + "dst", t)], dma_key=(tagp + "yst", t % 2))
                    final_ops.append(o)
            S.barrier()

        final_ops = []

        def load_z_A(sbO):
            zT = Buf(sbO, "zTin", [128, 16, 128], BF16, 3)

            def load(t):
                z_t, z_k = zT(t)
                S.I("sp", "dma_start", out=z_t[:], in_=zTa[t], writes=[z_k], dma_key=z_k)

            def get(t):
                z_t, z_k = zT(t)
                return (lambda c: z_t[:, c, :]), [z_k]
            return load, get

        outproj_phase("A2", w_out_a, 16, gn_gain_a, g_post_a, load_z_A, x, (out if STOP_AFTER == "A2" else h1), 1)
        if STOP_AFTER == "A2":
            S.emit(final_wait_ops=list(final_ops))
            return nc
        final_ops.clear()

        with contextlib.ExitStack() as stB:
            def sbB(name, shape, dt):
                return stB.enter_context(nc.sbuf_tensor("B1" + name, list(shape), dt))

            def psB(name, shape, dt):
                return stB.enter_context(nc.psum_tensor("B1p" + name, list(shape), dt))
            stage = Buf(sbB, "stage", [128, 2048], F32, 2)
            WkvaB = sbB("WkvaB", [128, 8, 320], BF16)
            WukB = sbB("WukB", [128, 2, D], BF16)
            WuvB = sbB("WuvB", [128, 2, D], BF16)
            WinbB = sbB("WinbB", [128, 8, 1408], BF16)
            WuqB = sbB("WuqB", [128, 3, 1536], BF16)
            gkv = sbB("gkv", [128, 8], F32)
            gkl = sbB("gkl", [128, 2], F32)
            gpb = sbB("gpb", [128, 8], F32)
            gql = sbB("gql", [128, 3], F32)
            for nm, dst_, src_ in (("gkv", gkv, g_kv), ("gkl", gkl, g_kv_lat), ("gpb", gpb, g_pre_b), ("gql", gql, g_q_lat)):
                small_load(dst_[:], src_[:, :], "B1" + nm)
            S.last_w[("gsc", "WkvaB")] = S.last_w["B1gkv"]
            S.last_w[("gsc", "WukB")] = S.last_w["B1gkl"]
            S.last_w[("gsc", "WuvB")] = S.last_w["B1gkl"]
            S.last_w[("gsc", "WinbB")] = S.last_w["B1gpb"]
            S.last_w[("gsc", "WuqB")] = S.last_w["B1gql"]
            engs = ["dve", "pool", "act"]
            load_weight(sbB, psB, w_kv_a, WkvaB, 8, 320, gkv, stage, engs, "WkvaB")
            load_weight(sbB, psB, w_in_b, WinbB, 8, 1408, gpb, stage, engs, "WinbB")
            load_weight(sbB, psB, w_uk, WukB, 2, D, gkl, stage, engs, "WukB")
            load_weight(sbB, psB, w_uv, WuvB, 2, D, gkl, stage, engs, "WuvB")
            load_weight(sbB, psB, w_uq, WuqB, 3, 1536, gql, stage, engs, "WuqB")
            if STOP_AFTER == "B1w":
                S.emit(final_wait_ops=[S.ops[-1]])
                return nc
            if B1N is not None:
                S.limit = len(S.ops) + B1N
            if STOP_AFTER == "B1x":
                tst = sbB("tst", [128, D], F32)
                o = S.I("sp", "dma_start", out=tst[:], in_=h1[0:128, :], writes=["tst"], dma_key="tst")
                S.emit(final_wait_ops=[o])
                return nc
            if STOP_AFTER == "B1z":
                xt = Buf(sbB, "xt", [128, D], F32, 2)
                xt_t, xt_k = xt(0)
                o = S.I("sp", "dma_start", out=xt_t[:], in_=h1[0:128, :], writes=[xt_k], dma_key=xt_k)
                S.emit(final_wait_ops=[o])
                return nc
            if STOP_AFTER == "B1y":
                tst = sbB("tst", [128, D], F32)
                o = S.I("sp", "dma_start", out=tst[:], in_=x[0:128, :], writes=["tst"], dma_key="tst")
                S.emit(final_wait_ops=[o])
                return nc

            xt = Buf(sbB, "xt", [128, D], F32, 2)
            junk = sbB("junk", [128, D], BF16)
            ssq = Buf(sbB, "ssq", [128, 1], F32, 2)
            sq = Buf(sbB, "sq", [128, 1], F32, 2)
            rstd = Buf(sbB, "rstd", [128, 1], F32, 2)
            ssqc = Buf(sbB, "ssqc", [128, 1], F32, 2)
            sqc = Buf(sbB, "sqc", [128, 1], F32, 2)
            rstdc = Buf(sbB, "rstdc", [128, 1], F32, 2)
            ssqq = Buf(sbB, "ssqq", [128, 1], F32, 2)
            sqq = Buf(sbB, "sqq", [128, 1], F32, 2)
            rstdq = Buf(sbB, "rstdq", [128, 1], F32, 2)
            hb = Buf(sbB, "hb", [128, D], BF16, 2)
            hTb = Buf(sbB, "hTb", [128, 8, QB], BF16, 2)
            csb = Buf(sbB, "csb", [128, 2, 32], F32, 2)
            chat = Buf(sbB, "chat", [128, 256], BF16, 2)
            kro = Buf(sbB, "kro", [128, 2, 64], BF16, 2)
            ktmp = Buf(sbB, "ktmp", [128, 4, 32], F32, 2)
            chT = Buf(sbB, "chT", [128, 2, QB], BF16, 2)
            krTb = Buf(sbB, "krTb", [128, QB], BF16, 2)
            cqh = Buf(sbB, "cqh", [128, 384], BF16, 2)
            cqT = Buf(sbB, "cqT", [128, 3, QB], BF16, 2)
            knTs = Buf(sbB, "knTs", [128, 8, QB], BF16, 2)
            qnTs = Buf(sbB, "qnTs", [128, 8, QB], BF16, 2)
            sgTs = Buf(sbB, "sgTs", [128, 8, QB], BF16, 2)
            qrTs = Buf(sbB, "qrTs", [128, 4, QB], BF16, 2)
            vsb = Buf(sbB, "vsb", [128, D], BF16, 2)
            qtmp = Buf(sbB, "qtmp", [128, 4, 8, 32], F32, 2)
            qro = Buf(sbB, "qro", [128, 8, 64], BF16, 2)

            PT = psB("PT", [128, 8, 128], F32)
            PA = Buf(psB, "PA", [128, 512], F32, 2)
            PF = Buf(psB, "PF", [128, 512], F32, 2)
            PV = psB("PV", [128, 2, 512], F32)

            def B_load(t):
                xt_t, xt_k = xt(t)
                cs_t, cs_k = csb(t)
                S.I("sp", "dma_start", out=xt_t[:], in_=h1[t * 128:(t + 1) * 128, :], writes=[xt_k], dma_key=xt_k)
                S.I("sp", "dma_start", out=cs_t[:], in_=csB[:, t, :, :], writes=[(cs_k, 0), (cs_k, 1)], dma_key=cs_k)

            for bq in range(NQB):
                hTb_t, hTb_k = hTb(bq)
                chT_t, chT_k = chT(bq)
                krT_t, krT_k = krTb(bq)
                cqT_t, cqT_k = cqT(bq)
                qrT_t, qrT_k = qrTs(bq)
                for sub in range(SUB):
                    t = bq * SUB + sub
                    ts_ = slice(sub * 128, (sub + 1) * 128)
                    xt_t, xt_k = xt(t)
                    cs_t, cs_k = csb(t)
                    if t == 0:
                        B_load(0)
                    if t + 1 < NT:
                        B_load(t + 1)
                    ssq_t, ssq_k = ssq(t)
                    S.I("act", "activation", out=junk[:], in_=xt_t[:], func=AF.Square, accum_out=ssq_t[:],
                          reads=[xt_k], writes=["B1junk", ssq_k])
                    r_t, r_k = rstd_chain(t, ssq_t[:], ssq_k, sq, rstd, 1.0 / D, "b")
                    hb_t, hb_k = hb(t)
                    S.I("dve", "tensor_scalar", out=hb_t[:], in0=xt_t[:], scalar1=r_t[:], scalar2=None, op0=ALU.mult,
                          reads=[xt_k, r_k], writes=[hb_k])
                    for k in range(8):
                        S.I("pe", "matmul", PT[:, k, :], lhsT=hb_t[:, k * 128:(k + 1) * 128], rhs=ident[:], start=True, stop=True,
                              reads=[hb_k, "ident"], writes=[("PT", k // 4)])
                    S.I("act", "activation", out=hTb_t[:, :, ts_], in_=PT[:], func=AF.Copy,
                          reads=[("PT", 0), ("PT", 1)], writes=[(hTb_k, sub)])
                    if t == 0:
                        stop_here("B1a")
                    pa_t, pa_k = PA(2 * t)
                    for k in range(8):
                        S.I("pe", "matmul", pa_t[:, 0:320], lhsT=hTb_t[:, k, ts_], rhs=WkvaB[:, k, :], start=(k == 0), stop=(k == 7),
                              reads=[(hTb_k, sub), wkey_for("WkvaB", k, 0)], writes=[pa_k])
                    sc_t, sc_k = ssqc(t)
                    S.I("act", "activation", out=junk[:, 0:256], in_=pa_t[:, 0:256], func=AF.Square, accum_out=sc_t[:],
                          reads=[pa_k], writes=["B1junk", sc_k])
                    rc_t, rc_k = rstd_chain(t, sc_t[:], sc_k, sqc, rstdc, 1.0 / 256, "c")
                    ch_t, ch_k = chat(t)
                    S.I("dve", "tensor_scalar", out=ch_t[:], in0=pa_t[:, 0:256], scalar1=rc_t[:], scalar2=None, op0=ALU.mult,
                          reads=[pa_k, rc_k], writes=[ch_k])
                    if t == 0:
                        stop_here("B1b")
                    kt_t, kt_k = ktmp(t)
                    ko_t, ko_k = kro(t)
                    x1 = pa_t[:, 256:288]
                    x2 = pa_t[:, 288:320]
                    S.I("dve", "tensor_tensor", out=kt_t[:, 0, :], in0=x1, in1=cs_t[:, 0, :], op=ALU.mult, reads=[pa_k, (cs_k, 0)], writes=[(kt_k, 0)])
                    S.I("dve", "tensor_tensor", out=kt_t[:, 1, :], in0=x2, in1=cs_t[:, 1, :], op=ALU.mult, reads=[pa_k, (cs_k, 1)], writes=[(kt_k, 1)])
                    S.I("dve", "tensor_tensor", out=kt_t[:, 2, :], in0=x1, in1=cs_t[:, 1, :], op=ALU.mult, reads=[pa_k, (cs_k, 1)], writes=[(kt_k, 2)])
                    S.I("dve", "tensor_tensor", out=kt_t[:, 3, :], in0=x2, in1=cs_t[:, 0, :], op=ALU.mult, reads=[pa_k, (cs_k, 0)], writes=[(kt_k, 3)])
                    S.I("pool", "tensor_tensor", out=ko_t[:, 0, 0:32], in0=kt_t[:, 0, :], in1=kt_t[:, 1, :], op=ALU.subtract, reads=[(kt_k, 0), (kt_k, 1)], writes=[(ko_k, 0)])
                    S.I("pool", "tensor_tensor", out=ko_t[:, 0, 32:64], in0=kt_t[:, 2, :], in1=kt_t[:, 3, :], op=ALU.add, reads=[(kt_k, 2), (kt_k, 3)], writes=[(ko_k, 1)])
                    S.I("pool", "tensor_copy", out=ko_t[:, 1, :], in_=ko_t[:, 0, :], reads=[(ko_k, 0), (ko_k, 1)], writes=[(ko_k, 2)])
                    if t == 0:
                        stop_here("B1c")
                    for c in range(2):
                        S.I("pe", "matmul", PT[:, c, :], lhsT=ch_t[:, c * 128:(c + 1) * 128], rhs=ident[:], start=True, stop=True,
                              reads=[ch_k, "ident"], writes=[("PT", c // 4)])
                    S.I("pe", "matmul", PT[:, 2, :], lhsT=ko_t[:].rearrange("p a b -> p (a b)"), rhs=ident[:], start=True, stop=True,
                          reads=[(ko_k, 0), (ko_k, 1), (ko_k, 2), "ident"], writes=[("PT", 0)])
                    S.I("act", "activation", out=chT_t[:, :, ts_], in_=PT[:, 0:2, :], func=AF.Copy,
                          reads=[("PT", 0)], writes=[(chT_k, sub)])
                    S.I("dve", "tensor_copy", out=krT_t[:, ts_], in_=PT[:, 2, :],
                          reads=[("PT", 0)], writes=[(krT_k, sub)])
                    if t == 0:
                        stop_here("B1d")
                    pq_t, pq_k = PA(2 * t + 1)
                    for k in range(8):
                        S.I("pe", "matmul", pq_t[:, 0:384], lhsT=hTb_t[:, k, ts_], rhs=WinbB[:, k, 0:384], start=(k == 0), stop=(k == 7),
                              reads=[(hTb_k, sub), wkey_for("WinbB", k, 0)], writes=[pq_k])
                    sq_t2, sq_k2 = ssqq(t)
                    S.I("act", "activation", out=junk[:, 0:384], in_=pq_t[:, 0:384], func=AF.Square, accum_out=sq_t2[:],
                          reads=[pq_k], writes=["B1junk", sq_k2])
                    rq_t, rq_k = rstd_chain(t, sq_t2[:], sq_k2, sqq, rstdq, 1.0 / 384, "q")
                    cq_t, cq_k = cqh(t)
                    S.I("dve", "tensor_scalar", out=cq_t[:], in0=pq_t[:, 0:384], scalar1=rq_t[:], scalar2=None, op0=ALU.mult,
                          reads=[pq_k, rq_k], writes=[cq_k])
                    for c in range(3):
                        S.I("pe", "matmul", PT[:, 4 + c, :], lhsT=cq_t[:, c * 128:(c + 1) * 128], rhs=ident[:], start=True, stop=True,
                              reads=[cq_k, "ident"], writes=[("PT", 1)])
                    S.I("act", "activation", out=cqT_t[:, :, ts_], in_=PT[:, 4:7, :], func=AF.Copy,
                          reads=[("PT", 1)], writes=[(cqT_k, sub)])
                    if t == 0:
                        stop_here("B1e")
                    for half in range(2):
                        for c in range(2):
                            S.I("pe", "matmul", PV[:, half, :], lhsT=chT_t[:, c, ts_], rhs=WuvB[:, c, half * 512:(half + 1) * 512], start=(c == 0), stop=(c == 1),
                                  reads=[(chT_k, sub), wkey_for("WuvB", c, half * 512)], writes=[("PV", half)])
                    v_t, v_k = vsb(t)
                    S.I("act", "activation", out=v_t[:], in_=PV[:].rearrange("p a b -> p (a b)"), func=AF.Copy,
                          reads=[("PV", 0), ("PV", 1)], writes=[v_k])
                    S.I("sp", "dma_start", out=vS[t], in_=v_t[:], reads=[v_k], writes=[("vS", t)], dma_key=("vst", t % 2))
                    if 'q' not in SKIP:
                        pr_t, pr_k = PF(2 * t)
                        wq_r = WuqB[:].rearrange("p c (h d) -> p c h d", h=8)
                        for c in range(3):
                            S.I("pe", "matmul", pr_t[:].rearrange("p (h d) -> p h d", h=8), lhsT=cqT_t[:, c, ts_], rhs=wq_r[:, c, :, 128:192], start=(c == 0), stop=(c == 2),
                                  reads=[(cqT_k, sub), wkey_for("WuqB", c, 0)], writes=[pr_k])
                        p3 = pr_t[:].rearrange("p (h d) -> p h d", h=8)
                        q1 = p3[:, :, 0:32]
                        q2 = p3[:, :, 32:64]
                        cb_ = cs_t[:, 0, :].unsqueeze(1).to_broadcast([128, 8, 32])
                        sb_ = cs_t[:, 1, :].unsqueeze(1).to_broadcast([128, 8, 32])
                        qt_t, qt_k = qtmp(t)
                        qo_t, qo_k = qro(t)
                        S.I("dve", "tensor_tensor", out=qt_t[:, 0], in0=q1, in1=cb_, op=ALU.mult, reads=[pr_k, (cs_k, 0)], writes=[(qt_k, 0)])
                        S.I("dve", "tensor_tensor", out=qt_t[:, 1], in0=q2, in1=sb_, op=ALU.mult, reads=[pr_k, (cs_k, 1)], writes=[(qt_k, 1)])
                        S.I("dve", "tensor_tensor", out=qt_t[:, 2], in0=q1, in1=sb_, op=ALU.mult, reads=[pr_k, (cs_k, 1)], writes=[(qt_k, 2)])
                        S.I("dve", "tensor_tensor", out=qt_t[:, 3], in0=q2, in1=cb_, op=ALU.mult, reads=[pr_k, (cs_k, 0)], writes=[(qt_k, 3)])
                        S.I("pool", "tensor_tensor", out=qo_t[:, :, 0:32], in0=qt_t[:, 0], in1=qt_t[:, 1], op=ALU.subtract, reads=[(qt_k, 0), (qt_k, 1)], writes=[(qo_k, 0)])
                        S.I("pool", "tensor_tensor", out=qo_t[:, :, 32:64], in0=qt_t[:, 2], in1=qt_t[:, 3], op=ALU.add, reads=[(qt_k, 2), (qt_k, 3)], writes=[(qo_k, 1)])
                        for pr_i in range(4):
                            S.I("pe", "matmul", PT[:, pr_i, :], lhsT=qo_t[:, 2 * pr_i:2 * pr_i + 2, :].rearrange("p a b -> p (a b)"), rhs=ident[:], start=True, stop=True,
                                  reads=[(qo_k, 0), (qo_k, 1), "ident"], writes=[("PT", 0)])
                        S.I("act", "activation", out=qrT_t[:, :, ts_], in_=PT[:, 0:4, :], func=AF.Copy,
                              reads=[("PT", 0)], writes=[(qrT_k, sub)])
                if 'f' not in SKIP:
                    allsub = lambda key: [(key, s_) for s_ in range(SUB)]
                    kn_t, kn_k = knTs(bq)
                    qn_t, qn_k = qnTs(bq)
                    sgT_t, sgT_k = sgTs(bq)
                    fi = 0
                    for h in range(8):
                        pf_t, pf_k = PF(fi); fi += 1
                        for c in range(2):
                            S.I("pe", "matmul", pf_t[:, 0:QB], lhsT=WukB[:, c, h * 128:(h + 1) * 128], rhs=chT_t[:, c, :], start=(c == 0), stop=(c == 1),
                                  reads=allsub(chT_k) + [wkey_for("WukB", c, h * 128)], writes=[pf_k])
                        S.I("dve", "tensor_copy", out=kn_t[:, h, :], in_=pf_t[:, 0:QB], reads=[pf_k], writes=[(kn_k, h)])
                    for h in range(8):
                        pf_t, pf_k = PF(fi); fi += 1
                        for c in range(3):
                            S.I("pe", "matmul", pf_t[:, 0:QB], lhsT=WuqB[:, c, h * 192:h * 192 + 128], rhs=cqT_t[:, c, :], start=(c == 0), stop=(c == 2),
                                  reads=allsub(cqT_k) + [wkey_for("WuqB", c, h * 192), wkey_for("WuqB", c, h * 192 + 127)], writes=[pf_k])
                        S.I("dve", "tensor_copy", out=qn_t[:, h, :], in_=pf_t[:, 0:QB], reads=[pf_k], writes=[(qn_k, h)])
                    for h in range(8):
                        pf_t, pf_k = PF(fi); fi += 1
                        for k in range(8):
                            S.I("pe", "matmul", pf_t[:, 0:QB], lhsT=WinbB[:, k, 384 + h * 128:384 + (h + 1) * 128], rhs=hTb_t[:, k, :], start=(k == 0), stop=(k == 7),
                                  reads=allsub(hTb_k) + [wkey_for("WinbB", k, 384 + h * 128), wkey_for("WinbB", k, 384 + h * 128 + 127)], writes=[pf_k])
                        S.I("act", "activation", out=sgT_t[:, h, :], in_=pf_t[:, 0:QB], func=AF.Silu, reads=[pf_k], writes=[(sgT_k, h)])
                if 's' not in SKIP:
                    bs = slice(bq * QB, (bq + 1) * QB)
                    S.I("sp", "dma_start", out=knT[:, :, bs].rearrange("h p t -> p h t"), in_=kn_t[:],
                          reads=[(kn_k, h) for h in range(8)], writes=[("knT", bq)], dma_key=("knst", bq % 2))
                    S.I("sp", "dma_start", out=qnT[:, :, bs].rearrange("h p t -> p h t"), in_=qn_t[:],
                          reads=[(qn_k, h) for h in range(8)], writes=[("qnT", bq)], dma_key=("qnst", bq % 2))
                    S.I("sp", "dma_start", out=sgT[:, :, bs].rearrange("h p t -> p h t"), in_=sgT_t[:],
                          reads=[(sgT_k, h) for h in range(8)], writes=[("sgT", bq)], dma_key=("sgst", bq % 2))
                    S.I("sp", "dma_start", out=qrT[:, :, bs].rearrange("h p t -> p h t"), in_=qrT_t[:],
                          reads=allsub(qrT_k), writes=[("qrT", bq)], dma_key=("qrst", bq % 2))
                    S.I("sp", "dma_start", out=krT[:, bs], in_=krT_t[:],
                          reads=allsub(krT_k), writes=[("krT", bq)], dma_key=("krst", bq % 2))
        S.barrier()

        if STOP_AFTER == "B1":
            S.emit(final_wait_ops=[S.ops[-1]])
            return nc
        SCALE = float((128 + 64) ** -0.5)
        with contextlib.ExitStack() as stC:
            def sbC(name, shape, dt):
                return stC.enter_context(nc.sbuf_tensor("B2" + name, list(shape), dt))

            def psC(name, shape, dt):
                return stC.enter_context(nc.psum_tensor("B2p" + name, list(shape), dt))
            ones = sbC("ones", [128, 128], BF16)
            S.I("pool", "memset", ones[:], 1.0, writes=["ones"])
            krS = sbC("krS", [128, S_len], BF16)
            S.I("sp", "dma_start", out=krS[:], in_=krT[:, :], writes=["krS"], dma_key="krS")
            knS = Buf(sbC, "knS", [128, S_len], BF16, 2)
            qnS = Buf(sbC, "qnS", [128, S_len], BF16, 2)
            sgS = Buf(sbC, "sgS", [128, S_len], BF16, 2)
            qrS = Buf(sbC, "qrS", [128, S_len], BF16, 2)
            v2S = Buf(sbC, "v2S", [128, NT, 256], BF16, 2)
            pTb = Buf(sbC, "pTb", [128, QB], BF16, 3)
            rden = Buf(sbC, "rden", [128, QB], F32, 2)
            ob = Buf(sbC, "ob", [128, QB], F32, 2)
            zb = Buf(sbC, "zb", [128, QB], BF16, 2)
            PSc = Buf(psC, "PSc", [128, 512], F32, 3)
            POc = Buf(psC, "POc", [128, 512], F32, 2)
            PDc = Buf(psC, "PDc", [128, 512], F32, 2)
            heads = {}

            def load_head(h):
                hp = h // 2
                kn_t, kn_k = knS(h)
                qn_t, qn_k = qnS(h)
                sg_t, sg_k = sgS(h)
                S.I("sp", "dma_start", out=kn_t[:], in_=knT[h], writes=[kn_k], dma_key=kn_k)
                S.I("sp", "dma_start", out=qn_t[:], in_=qnT[h], writes=[qn_k], dma_key=qn_k)
                S.I("sp", "dma_start", out=sg_t[:], in_=sgT[h], writes=[sg_k], dma_key=sg_k)
                qr_t, qr_k = qrS(hp)
                v2_t, v2_k = v2S(hp)
                if h % 2 == 0:
                    S.I("sp", "dma_start", out=qr_t[:], in_=qrT[hp], writes=[qr_k], dma_key=qr_k)
                    S.I("sp", "dma_start", out=v2_t[:], in_=vS[:, :, hp * 256:(hp + 1) * 256].rearrange("t p d -> p t d"), writes=[v2_k], dma_key=v2_k)
                heads[h] = (kn_t, kn_k, qn_t, qn_k, sg_t, sg_k, qr_t, qr_k, v2_t, v2_k)

            items = []
            blk = 0
            for h in range(8):
                for qb in range(NQB):
                    nkt = SUB * qb + SUB
                    for kt in range(nkt):
                        items.append((h, qb, kt, nkt, blk))
                    blk += 1

            def geom(i):
                h, qb, kt, nkt, blk_ = items[i]
                j = kt - SUB * qb
                c0 = 128 * j if j > 0 else 0
                return h, qb, kt, nkt, blk_, j, c0

            def qk(i):
                h, qb, kt, nkt, blk_, j, c0 = geom(i)
                kn_t, kn_k, qn_t, qn_k, sg_t, sg_k, qr_t, qr_k, v2_t, v2_k = heads[h]
                pb = 64 * (h % 2)
                qs = slice(qb * QB + c0, (qb + 1) * QB)
                ks = slice(kt * 128, (kt + 1) * 128)
                ps_t, ps_k = PSc(i)
                pt_t, pt_k = pTb(i)
                S.I("pe", "matmul", ps_t[:, c0:QB], lhsT=kn_t[:, ks], rhs=qn_t[:, qs], start=True, stop=False,
                    reads=[kn_k, qn_k], writes=[ps_k])
                S.I("pe", "matmul", ps_t[:, c0:QB], lhsT=krS[pb:pb + 64, ks], rhs=qr_t[pb:pb + 64, qs], start=False, stop=True,
                    reads=["krS", qr_k], writes=[ps_k])
                S.I("act", "activation", out=pt_t[:, c0:QB], in_=ps_t[:, c0:QB], func=AF.Exp, scale=SCALE,
                    reads=[ps_k], writes=[pt_k])
                if j >= 0:
                    S.I("pool", "memset", pt_t[64:128, c0:c0 + 64], 0.0, reads=[], writes=[pt_k])

            def pv(i):
                h, qb, kt, nkt, blk_, j, c0 = geom(i)
                kn_t, kn_k, qn_t, qn_k, sg_t, sg_k, qr_t, qr_k, v2_t, v2_k = heads[h]
                pt_t, pt_k = pTb(i)
                po_t, po_k = POc(blk_)
                pd_t, pd_k = PDc(blk_)
                S.I("pe", "matmul", po_t[:, c0:QB], lhsT=v2_t[:, kt, (h % 2) * 128:(h % 2) * 128 + 128], rhs=pt_t[:, c0:QB], start=(kt == 0), stop=(kt == nkt - 1),
                    reads=[v2_k, pt_k], writes=[po_k])
                S.I("pe", "matmul", pd_t[:, c0:QB], lhsT=ones[:], rhs=pt_t[:, c0:QB], start=(kt == 0), stop=(kt == nkt - 1),
                    reads=["ones", pt_k], writes=[pd_k])
                if kt == nkt - 1:
                    rd_t, rd_k = rden(blk_)
                    ob_t, ob_k = ob(blk_)
                    zb_t, zb_k = zb(blk_)
                    S.I("dve", "reciprocal", out=rd_t[:], in_=pd_t[:, 0:QB], reads=[pd_k], writes=[rd_k])
                    S.I("dve", "tensor_tensor", out=ob_t[:], in0=po_t[:, 0:QB], in1=rd_t[:], op=ALU.mult, reads=[po_k, rd_k], writes=[ob_k])
                    S.I("pool", "tensor_tensor", out=zb_t[:], in0=ob_t[:], in1=sg_t[:, qb * QB:(qb + 1) * QB], op=ALU.mult, reads=[ob_k, sg_k], writes=[zb_k])
                    S.I("sp", "dma_start", out=zTb[qb, :, h, :], in_=zb_t[:], reads=[zb_k], writes=[("zTb", qb, h)], dma_key=("zbst", blk_ % 2))
                    if qb == NQB - 1 and h + 2 < 8:
                        load_head(h + 2)

            load_head(0)
            load_head(1)
            LOOK = 2
            for i in range(min(LOOK, len(items))):
                qk(i)
            for i in range(len(items)):
                if i + LOOK < len(items):
                    qk(i + LOOK)
                pv(i)
        S.barrier()

        if STOP_AFTER == "B2":
            S.emit(final_wait_ops=[S.ops[-1]])
            return nc
        def load_z_B(sbO):
            zT = Buf(sbO, "zTin", [128, 8, QB], BF16, 3)
            done = set()

            def load(t):
                bq = t // SUB
                for b_ in (bq, bq + 1):
                    if b_ < NQB and b_ not in done:
                        done.add(b_)
                        z_t, z_k = zT(b_)
                        S.I("sp", "dma_start", out=z_t[:], in_=zTb[b_], writes=[z_k], dma_key=z_k)

            def get(t):
                bq, sub = divmod(t, SUB)
                z_t, z_k = zT(bq)
                return (lambda c: z_t[:, c, sub * 128:(sub + 1) * 128]), [z_k]
            return load, get

        outproj_phase("B3", w_out_b, 8, None, g_post_b, load_z_B, h1, out, 1)
        S.emit(final_wait_ops=list(final_ops))
    return nc


def _chunkT(w, K):
    n = w.shape[1]
    return np.ascontiguousarray(w.reshape(K, 128, n).transpose(1, 0, 2))


def _vecT(g, K):
    return np.ascontiguousarray(g.reshape(K, 128).T)


def _rope_tables(S_len, half):
    inv = 10000.0 ** (-np.arange(half, dtype=np.float64) / float(half))
    ang = np.arange(S_len, dtype=np.float64)[:, None] * inv[None, :]
    c = np.cos(ang).astype(np.float32)
    s = np.sin(ang).astype(np.float32)
    nt = S_len // 128
    lay = lambda a: np.ascontiguousarray(a.reshape(nt, 128, half).transpose(1, 0, 2))
    return lay(c), lay(s)


def _decay_tables():
    h = np.arange(NH, dtype=np.float64)
    lg = np.log(1.0 - np.exp2(-5.0 - h))
    i = np.arange(128, dtype=np.float64)
    xi = np.exp(lg[:, None] * (i + 1.0)[None])
    zeta = np.exp(lg[:, None] * (127.0 - i)[None])
    ch = (np.arange(128) // 64)
    c_ = i[:, None]
    m_ = i[None, :]
    same = ch[:, None] == ch[None, :]
    prev = ch[None, :] < ch[:, None]
    Dm = np.zeros((NH, 128, 128))
    for hh in range(NH):
        Dm[hh] = np.where(same, np.exp(lg[hh] * np.abs(c_ - m_)), np.where(prev, np.exp(lg[hh] * (c_ - m_)), 0.0))
    T2 = Dm / (xi[:, :, None] * zeta[:, None, :])
    t2t = np.ascontiguousarray(T2.transpose(2, 0, 1)).astype(np.float32)
    dxi = np.zeros((128, NH, 128), np.float32)
    for hh in range(NH):
        dxi[np.arange(128), hh, np.arange(128)] = xi[hh]
    zs = np.ascontiguousarray((zeta * (128.0 ** -0.5)).T).astype(np.float32)
    return t2t, dxi, zs


def _prep_shared(inp, S_len):
    f = lambda a: np.asarray(a, dtype=np.float32)
    cA, sA = _rope_tables(S_len, 64)
    cB, sB = _rope_tables(S_len, 32)
    t2t, dxi, zs = _decay_tables()
    return {
        "w_in_a": _chunkT(f(inp["w_in_a"])[0], 8),
        "g_pre_a": _vecT(f(inp["g_pre_a"])[0], 8),
        "gn_gain_a": _vecT(f(inp["gn_gain_a"])[0], 16),
        "w_out_a": _chunkT(f(inp["w_out_a"])[0], 16),
        "g_post_a": np.ascontiguousarray(np.broadcast_to(f(inp["g_post_a"])[0][None, :], (128, D))),
        "g_kv": _vecT(f(inp["g_kv"]), 8),
        "w_kv_a": _chunkT(f(inp["w_kv_a"]), 8),
        "g_kv_lat": _vecT(f(inp["g_kv_lat"]), 2),
        "w_uk": _chunkT(f(inp["w_uk"]), 2),
        "w_uv": _chunkT(f(inp["w_uv"]), 2),
        "g_pre_b": _vecT(f(inp["g_pre_b"])[0], 8),
        "w_in_b": _chunkT(f(inp["w_in_b"])[0], 8),
        "g_q_lat": _vecT(f(inp["g_q_lat"])[0], 3),
        "w_uq": _chunkT(f(inp["w_uq"])[0], 3),
        "w_out_b": _chunkT(f(inp["w_out_b"])[0], 8),
        "g_post_b": np.ascontiguousarray(np.broadcast_to(f(inp["g_post_b"])[0][None, :], (128, D))),
        "csA": np.ascontiguousarray(np.stack([cA, sA], axis=2)), "csB": np.ascontiguousarray(np.stack([cB, sB], axis=2)),
        "t2t": t2t, "dxi": dxi, "zs": zs,
        "ident": np.eye(128, dtype=np.float32),
    }


def kernel(**inputs):
    x = np.asarray(inputs["x"], dtype=np.float32)
    B, S_len, _ = x.shape
    shared = _prep_shared(inputs, S_len)
    nc = build(S_len)
    in_maps = []
    for b in range(B):
        m = dict(shared)
        m["x"] = np.ascontiguousarray(x[b])
        in_maps.append(m)
    res = run_bass_kernel_spmd(nc, in_maps, core_ids=list(range(B)))
    return np.stack([np.asarray(r["out"], dtype=np.float32) for r in res.results], axis=0)
```

```python
import contextlib
import numpy as np
import concourse.bass as bass
import concourse.mybir as mybir
from concourse.bass_utils import run_bass_kernel_spmd

F32 = mybir.dt.float32
BF16 = mybir.dt.bfloat16
ALU = mybir.AluOpType
AF = mybir.ActivationFunctionType

D = 1024
EPS = 1e-6
NH = 8
SEQ = 4096
STOP_AFTER = None
SKIP = ''
B1N = None


class _Stop(Exception):
    pass


class Op:
    __slots__ = ("eng", "fn", "deps", "idx", "signal", "dma_key", "cnt")

    def __init__(self, eng, fn, deps, dma_key):
        self.eng = eng
        self.fn = fn
        self.deps = deps
        self.dma_key = dma_key
        self.signal = False
        self.cnt = None


class Sched:
    COMPUTE = ("pe", "dve", "act", "pool")

    def __init__(self, nc):
        self.nc = nc
        self.ops = []
        self.last_w = {}
        self.readers = {}
        self.bar = []
        self.last_eng = {}
        self.last_key = {}

    PSUM_ROOTS = {"PT", "PS", "PO", "PU", "PP", "PY", "PV", "PA", "PF", "PSc", "POc", "PDc"}

    @classmethod
    def _excl(cls, key):
        while not isinstance(key, str):
            key = key[0]
        return key in cls.PSUM_ROOTS

    def add(self, eng, fn, reads=(), writes=(), dma_key=None):
        ex = [r for r in reads if self._excl(r)]
        if ex:
            reads = [r for r in reads if not self._excl(r)]
            writes = list(writes) + [r for r in ex if r not in writes]
        deps = list(self.bar)
        for r in reads:
            w = self.last_w.get(r)
            if w is not None:
                deps.append(w)
        for r in writes:
            w = self.last_w.get(r)
            if w is not None:
                deps.append(w)
            deps.extend(self.readers.get(r, ()))
        if getattr(self, "limit", None) is not None and len(self.ops) >= self.limit:
            raise _Stop()
        op = Op(eng, fn, deps, dma_key)
        op.idx = len(self.ops)
        self.ops.append(op)
        for r in reads:
            self.readers.setdefault(r, []).append(op)
        for r in writes:
            self.last_w[r] = op
            self.readers[r] = []
        if dma_key is None:
            self.last_eng[eng] = op
        else:
            self.last_key[dma_key] = op
        return op

    def I(self, eng, meth, *args, reads=(), writes=(), dma_key=None, **kw):
        return self.add(eng, lambda e: getattr(e, meth)(*args, **kw), reads=reads, writes=writes, dma_key=dma_key)

    def barrier(self):
        self.bar = list(self.last_eng.values()) + list(self.last_key.values())
        self.last_w = {}
        self.readers = {}

    @staticmethod
    def _pe_pe(d, op):
        return d.dma_key is None and op.dma_key is None and d.eng == "pe" and op.eng == "pe"

    def emit(self, final_wait_ops=()):
        nc = self.nc
        for op in self.ops:
            for d in op.deps:
                if not self._pe_pe(d, op):
                    d.signal = True
        for op in final_wait_ops:
            op.signal = True
        for op in self.ops:
            if op.dma_key is not None:
                op.signal = True
        eng_cnt = {e: 0 for e in self.COMPUTE}
        key_cnt = {}
        for op in self.ops:
            if not op.signal:
                continue
            if op.dma_key is not None:
                key_cnt[op.dma_key] = key_cnt.get(op.dma_key, 0) + 1
                op.cnt = key_cnt[op.dma_key] * 16
            else:
                eng_cnt[op.eng] += 1
                op.cnt = eng_cnt[op.eng]
        sems = {}
        with contextlib.ExitStack() as st:
            for e in self.COMPUTE:
                sems[("eng", e)] = st.enter_context(nc.semaphore("s_" + e))
            for i, k in enumerate(sorted(key_cnt, key=str)):
                sems[("dma", k)] = st.enter_context(nc.semaphore("d%d" % i))
            block = st.enter_context(nc.Block())
            queues = {}
            for op in self.ops:
                queues.setdefault(op.eng, []).append(op)
            engmap = {"pe": ("tensor", nc.tensor), "dve": ("vector", nc.vector),
                      "act": ("scalar", nc.scalar), "pool": ("gpsimd", nc.gpsimd),
                      "sp": ("sync", nc.sync)}

            def semof(op):
                if op.dma_key is not None:
                    return sems[("dma", op.dma_key)]
                return sems[("eng", op.eng)]

            def run_queue(eng, ops, final):
                known = {}
                for op in ops:
                    need = {}
                    for d in op.deps:
                        if d.cnt is None or self._pe_pe(d, op):
                            continue
                        s = semof(d)
                        key = id(s)
                        if known.get(key, 0) >= d.cnt:
                            continue
                        if key not in need or need[key][1] < d.cnt:
                            need[key] = (s, d.cnt)
                    for key, (s, c) in need.items():
                        eng.wait_ge(s, c)
                        known[key] = c
                    ins = op.fn(eng)
                    if op.signal:
                        ins.then_inc(semof(op), 16 if op.dma_key is not None else 1)
                for op in final:
                    eng.wait_ge(semof(op), op.cnt)

            for ename, (attr, eng) in engmap.items():
                ops = queues.get(ename, [])
                final = list(final_wait_ops) if ename == "sp" else []
                if not ops and not final:
                    continue

                def body(e, _ops=ops, _final=final):
                    run_queue(e, _ops, _final)
                getattr(block, attr)(body)


class Buf:
    def __init__(self, alloc, name, shape, dt, nbuf=1):
        self.t = [alloc("%s_%d" % (name, i), shape, dt) for i in range(nbuf)]
        self.name = name
        self.n = nbuf

    def __call__(self, i=0):
        j = i % self.n
        return self.t[j], (self.name, j)


def build(S_len=SEQ):
    ctx = {}
    try:
        return _build(S_len, ctx)
    except _Stop:
        return ctx["nc"]


def _build(S_len, ctx):
    NT = S_len // 128
    QB = min(512, S_len)
    NQB = S_len // QB
    SUB = QB // 128
    nc = bass.Bass("TRN2", target_bir_lowering=False)

    def din(name, shape, dt=F32):
        return nc.dram_tensor(name, list(shape), dt, kind="ExternalInput").ap()

    def dscr(name, shape, dt):
        return nc.dram_tensor(name, list(shape), dt, kind="Internal").ap()

    x = din("x", [S_len, D])
    w_in_a = din("w_in_a", [128, 8, 6144])
    g_pre_a = din("g_pre_a", [128, 8])
    gn_gain_a = din("gn_gain_a", [128, 16])
    w_out_a = din("w_out_a", [128, 16, D])
    g_post_a = din("g_post_a", [128, D])
    g_kv = din("g_kv", [128, 8])
    w_kv_a = din("w_kv_a", [128, 8, 320])
    g_kv_lat = din("g_kv_lat", [128, 2])
    w_uk = din("w_uk", [128, 2, D])
    w_uv = din("w_uv", [128, 2, D])
    g_pre_b = din("g_pre_b", [128, 8])
    w_in_b = din("w_in_b", [128, 8, 1408])
    g_q_lat = din("g_q_lat", [128, 3])
    w_uq = din("w_uq", [128, 3, 1536])
    w_out_b = din("w_out_b", [128, 8, D])
    g_post_b = din("g_post_b", [128, D])
    csA = din("csA", [128, NT, 2, 64])
    csB = din("csB", [128, NT, 2, 32])
    t2t_d = din("t2t", [128, 8, 128])
    dxi_d = din("dxi", [128, 8, 128])
    ident_d = din("ident", [128, 128])
    zs_d = din("zs", [128, 8])
    gc_host = None
    out = nc.dram_tensor("out", [S_len, D], F32, kind="ExternalOutput").ap()

    zTa = dscr("zTa", [NT, 128, 16, 128], BF16)
    h1 = dscr("h1", [S_len, D], F32)
    knT = dscr("knT", [8, 128, S_len], BF16)
    krT = dscr("krT", [128, S_len], BF16)
    vS = dscr("vS", [NT, 128, D], BF16)
    qnT = dscr("qnT", [8, 128, S_len], BF16)
    qrT = dscr("qrT", [4, 128, S_len], BF16)
    sgT = dscr("sgT", [8, 128, S_len], BF16)
    zTb = dscr("zTb", [NQB, 128, 8, QB], BF16)

    GC = [float((1.0 - 2.0 ** (-5.0 - h)) ** 128) for h in range(NH)]

    S = Sched(nc)
    ctx["S"] = S
    ctx["nc"] = nc

    def stop_here(tag):
        if STOP_AFTER == tag:
            last = [o for o in S.ops if o.dma_key is not None][-1]
            S.emit(final_wait_ops=[last, S.ops[-1]])
            raise _Stop()

    def load_weight(sb, ps_alloc, src, dst, K, N, scale, stage, eng_cycle, tag):
        i = 0
        for k in range(K):
            for n0 in range(0, N, 2048):
                n1 = min(N, n0 + 2048)
                st_t, st_k = stage(i)
                S.I("sp", "dma_start", out=st_t[:, 0:n1 - n0], in_=src[:, k, n0:n1],
                      writes=[st_k], dma_key=st_k)
                eng = eng_cycle[i % len(eng_cycle)]
                rd = [st_k] + ([("gsc", tag)] if scale is not None else [])
                wkey = (tag, k, n0)
                if scale is None:
                    if eng == "act":
                        S.I("act", "activation", out=dst[:, k, n0:n1], in_=st_t[:, 0:n1 - n0], func=AF.Copy,
                              reads=rd, writes=[wkey])
                    else:
                        S.I(eng, "tensor_copy", out=dst[:, k, n0:n1], in_=st_t[:, 0:n1 - n0],
                              reads=rd, writes=[wkey])
                else:
                    if eng == "act":
                        S.I("act", "activation", out=dst[:, k, n0:n1], in_=st_t[:, 0:n1 - n0], func=AF.Copy, scale=scale[:, k:k + 1],
                              reads=rd, writes=[wkey])
                    else:
                        S.I(eng, "tensor_scalar", out=dst[:, k, n0:n1], in0=st_t[:, 0:n1 - n0], scalar1=scale[:, k:k + 1], scalar2=None, op0=ALU.mult,
                              reads=rd, writes=[wkey])
                i += 1

    def wkeys(tag, K, N):
        return [(tag, k, n0) for k in range(K) for n0 in range(0, N, 2048)]

    def wkey_for(tag, k, c0):
        return (tag, k, (c0 // 2048) * 2048)

    def small_load(dst, src, key):
        S.I("sp", "dma_start", out=dst, in_=src, writes=[key], dma_key=key)

    def rstd_chain(slot_i, ssq_ap, ssq_key, sq, rstd, inv_n, nm):
        sq_t, sq_k = sq(slot_i)
        r_t, r_k = rstd(slot_i)
        S.I("act", "activation", out=sq_t[:], in_=ssq_ap, func=AF.Sqrt, scale=inv_n, bias=epsT[:],
              reads=[ssq_key, "epsT"], writes=[sq_k])
        S.I("dve", "reciprocal", out=r_t[:], in_=sq_t[:], reads=[sq_k], writes=[r_k])
        return r_t, r_k

    with contextlib.ExitStack() as stk:
        def sb(name, shape, dt):
            return stk.enter_context(nc.sbuf_tensor("sb_" + name, list(shape), dt))

        def ps(name, shape, dt):
            return stk.enter_context(nc.psum_tensor("ps_" + name, list(shape), dt))

        epsT = sb("epsT", [128, 1], F32)
        ident = sb("ident", [128, 128], BF16)
        identf = sb("identf", [128, 128], F32)
        S.I("pool", "memset", epsT[:], EPS, writes=["epsT"])
        small_load(identf[:], ident_d[:, :], "identf")
        S.I("dve", "tensor_copy", out=ident[:], in_=identf[:], reads=["identf"], writes=["ident"])

        with contextlib.ExitStack() as stA:
            def sbA(name, shape, dt):
                return stA.enter_context(nc.sbuf_tensor("A1" + name, list(shape), dt))

            def psA(name, shape, dt):
                return stA.enter_context(nc.psum_tensor("A1p" + name, list(shape), dt))

            WinB = sbA("WinB", [128, 8, 6144], BF16)
            stage = Buf(sbA, "stage", [128, 2048], F32, 2)
            gpa = sbA("gpa", [128, 8], F32)
            t2t = sbA("t2t", [128, 8, 128], F32)
            dxi = sbA("dxi", [128, 8, 128], BF16)
            zs = sbA("zs", [128, 8], F32)
            xt = Buf(sbA, "xt", [128, D], F32, 2)
            junk = sbA("junk", [128, D], BF16)
            ssq = Buf(sbA, "ssq", [128, 1], F32, 2)
            sq = Buf(sbA, "sq", [128, 1], F32, 2)
            rstd = Buf(sbA, "rstd", [128, 1], F32, 2)
            hb = Buf(sbA, "hb", [128, D], BF16, 1)
            hT = Buf(sbA, "hT", [128, 8, 128], BF16, 2)
            cs = Buf(sbA, "cs", [128, 2, 64], F32, 2)
            rtmp = Buf(sbA, "rtmp", [128, 4, 4, 64], F32, 2)
            qr = Buf(sbA, "qr", [128, 8, 128], BF16, 1)
            kr = Buf(sbA, "kr", [128, 8, 128], BF16, 2)
            qT = Buf(sbA, "qT", [128, 8, 128], BF16, 2)
            kT = Buf(sbA, "kT", [128, 8, 128], BF16, 2)
            vt = Buf(sbA, "vt", [128, 8, 256], BF16, 2)
            sg = Buf(sbA, "sg", [128, 2048], BF16, 2)
            pS = Buf(sbA, "pS", [128, 8, 128], BF16, 1)
            stats = sbA("stats", [128, 8, 6], F32)
            mv = sbA("mv", [128, 8, 2], F32)
            gsq = sbA("gsq", [128, 8], F32)
            grs = sbA("grs", [128, 8], F32)
            gnb = sbA("gnb", [128, 8], F32)
            on = Buf(sbA, "on", [128, 2048], BF16, 1)
            zT = Buf(sbA, "zT", [128, 16, 128], BF16, 2)
            Rf = sbA("Rf", [128, 8, 256], F32)
            Rb = sbA("Rb", [128, 8, 256], BF16)

            PT = psA("PT", [128, 8, 128], F32)
            PP = Buf(psA, "PP", [128, 512], F32, 2)
            PS_ = psA("PS", [128, 4, 128], F32)
            PO = Buf(psA, "PO", [128, 2, 256], F32, 2)
            PU = psA("PU", [128, 2, 256], F32)

            small_load(gpa[:], g_pre_a[:, :], "gpa")
            S.last_w[("gsc", "WinB")] = S.last_w["gpa"]
            small_load(t2t[:], t2t_d[:, :, :], "t2t")
            small_load(zs[:], zs_d[:, :], "zs")
            st_t, st_k = stage(0)
            S.I("sp", "dma_start", out=st_t[:, 0:1024], in_=dxi_d.rearrange("p h c -> p (h c)"), writes=[st_k], dma_key=st_k)
            S.I("dve", "tensor_copy", out=dxi[:].rearrange("p h c -> p (h c)"), in_=st_t[:, 0:1024], reads=[st_k], writes=["dxi"])
            load_weight(sbA, psA, w_in_a, WinB, 8, 6144, gpa, stage, ["dve", "pool", "act"], "WinB")

            def A_load(t):
                xt_t, xt_k = xt(t)
                cs_t, cs_k = cs(t)
                S.I("sp", "dma_start", out=xt_t[:], in_=x[t * 128:(t + 1) * 128, :], writes=[xt_k], dma_key=xt_k)
                S.I("sp", "dma_start", out=cs_t[:], in_=csA[:, t, :, :], writes=[(cs_k, 0), (cs_k, 1)], dma_key=cs_k)

            def A_S1(t):
                xt_t, xt_k = xt(t)
                cs_t, cs_k = cs(t)
                ssq_t, ssq_k = ssq(t)
                S.I("act", "activation", out=junk[:], in_=xt_t[:], func=AF.Square, accum_out=ssq_t[:],
                      reads=[xt_k], writes=["junk", ssq_k])
                r_t, r_k = rstd_chain(t, ssq_t[:], ssq_k, sq, rstd, 1.0 / D, "x")
                hb_t, hb_k = hb(t)
                S.I("dve", "tensor_scalar", out=hb_t[:], in0=xt_t[:], scalar1=r_t[:], scalar2=None, op0=ALU.mult,
                      reads=[xt_k, r_k], writes=[hb_k])
                for k in range(8):
                    S.I("pe", "matmul", PT[:, k, :], lhsT=hb_t[:, k * 128:(k + 1) * 128], rhs=ident[:], start=True, stop=True,
                          reads=[hb_k, "ident"], writes=[("PT", k // 4)])
                hT_t, hT_k = hT(t)
                S.I("act", "activation", out=hT_t[:], in_=PT[:], func=AF.Copy,
                      reads=[("PT", 0), ("PT", 1)], writes=[hT_k])
                qr_t, qr_k = qr(t)
                kr_t, kr_k = kr(t)
                vt_t, vt_k = vt(t)
                sg_t, sg_k = sg(t)
                for cb in range(12):
                    pp_t, pp_k = PP(cb)
                    for k in range(8):
                        S.I("pe", "matmul", pp_t[:], lhsT=hT_t[:, k, :], rhs=WinB[:, k, cb * 512:(cb + 1) * 512], start=(k == 0), stop=(k == 7),
                              reads=[hT_k, wkey_for("WinB", k, cb * 512)], writes=[pp_k])
                    if cb < 4:
                        dst_t, dst_k = (qr_t, qr_k) if cb < 2 else (kr_t, kr_k)
                        hh = (cb % 2) * 4
                        p3 = pp_t[:].rearrange("p (h d) -> p h d", h=4)
                        x1 = p3[:, :, 0:64]
                        x2 = p3[:, :, 64:128]
                        cosb = cs_t[:, 0, :].unsqueeze(1).to_broadcast([128, 4, 64])
                        sinb = cs_t[:, 1, :].unsqueeze(1).to_broadcast([128, 4, 64])
                        tm_t, tm_k = rtmp(cb)
                        S.I("dve", "tensor_tensor", out=tm_t[:, 0], in0=x1, in1=cosb, op=ALU.mult,
                              reads=[pp_k, (cs_k, 0)], writes=[(tm_k, 0)])
                        S.I("dve", "tensor_tensor", out=tm_t[:, 1], in0=x2, in1=sinb, op=ALU.mult,
                              reads=[pp_k, (cs_k, 1)], writes=[(tm_k, 1)])
                        S.I("dve", "tensor_tensor", out=tm_t[:, 2], in0=x1, in1=sinb, op=ALU.mult,
                              reads=[pp_k, (cs_k, 1)], writes=[(tm_k, 2)])
                        S.I("dve", "tensor_tensor", out=tm_t[:, 3], in0=x2, in1=cosb, op=ALU.mult,
                              reads=[pp_k, (cs_k, 0)], writes=[(tm_k, 3)])
                        S.I("pool", "tensor_tensor", out=dst_t[:, hh:hh + 4, 0:64], in0=tm_t[:, 0], in1=tm_t[:, 1], op=ALU.subtract,
                              reads=[(tm_k, 0), (tm_k, 1)], writes=[(dst_k, cb % 2, 0)])
                        S.I("pool", "tensor_tensor", out=dst_t[:, hh:hh + 4, 64:128], in0=tm_t[:, 2], in1=tm_t[:, 3], op=ALU.add,
                              reads=[(tm_k, 2), (tm_k, 3)], writes=[(dst_k, cb % 2, 1)])
                    elif cb < 8:
                        for j in range(2):
                            h = (cb - 4) * 2 + j
                            S.I("act", "activation", out=vt_t[:, h, :], in_=pp_t[:, j * 256:(j + 1) * 256], func=AF.Copy, scale=zs[:, h:h + 1],
                                  reads=[pp_k, "zs"], writes=[(vt_k, h)])
                    else:
                        c0 = (cb - 8) * 512
                        S.I("act", "activation", out=sg_t[:, c0:c0 + 512], in_=pp_t[:], func=AF.Silu,
                              reads=[pp_k], writes=[(sg_k, cb - 8)])
                for h in range(8):
                    S.I("pe", "matmul", PT[:, h, :], lhsT=qr_t[:, h, :], rhs=dxi[:, h, :], start=True, stop=True,
                          reads=[(qr_k, h // 4, 0), (qr_k, h // 4, 1), "dxi"], writes=[("PT", h // 4)])
                qT_t, qT_k = qT(t)
                S.I("dve", "tensor_copy", out=qT_t[:], in_=PT[:], reads=[("PT", 0), ("PT", 1)], writes=[qT_k])
                for h in range(8):
                    S.I("pe", "matmul", PT[:, h, :], lhsT=kr_t[:, h, :], rhs=ident[:], start=True, stop=True,
                          reads=[(kr_k, h // 4, 0), (kr_k, h // 4, 1), "ident"], writes=[("PT", h // 4)])
                kT_t, kT_k = kT(t)
                S.I("act", "activation", out=kT_t[:], in_=PT[:], func=AF.Copy, reads=[("PT", 0), ("PT", 1)], writes=[kT_k])

            def A_S2(t):
                qT_t, qT_k = qT(t)
                kT_t, kT_k = kT(t)
                kr_t, kr_k = kr(t)
                vt_t, vt_k = vt(t)
                sg_t, sg_k = sg(t)
                pS_t, pS_k = pS(t)
                on_t, on_k = on(t)
                for g in range(2):
                    for j in range(4):
                        h = 4 * g + j
                        S.I("pe", "matmul", PS_[:, j, :], lhsT=kT_t[:, h, :], rhs=qT_t[:, h, :], start=True, stop=True,
                              reads=[kT_k, qT_k], writes=["PS"])
                    S.I("dve", "tensor_tensor", out=pS_t[:, 4 * g:4 * g + 4, :], in0=PS_[:], in1=t2t[:, 4 * g:4 * g + 4, :], op=ALU.mult,
                          reads=["PS"] + ["t2t"], writes=[(pS_k, g)])
                for hp in range(4):
                    po_t, po_k = PO(hp)
                    for j in range(2):
                        h = 2 * hp + j
                        S.I("pe", "matmul", po_t[:, j, :], lhsT=pS_t[:, h, :], rhs=vt_t[:, h, :], start=True, stop=(t == 0),
                              reads=[(pS_k, h // 4), (vt_k, h)], writes=[po_k])
                        if t > 0:
                            S.I("pe", "matmul", po_t[:, j, :], lhsT=qT_t[:, h, :], rhs=Rb[:, h, :], start=False, stop=True,
                                  reads=[qT_k, ("Rb", hp)], writes=[po_k])
                    for j in range(2):
                        h = 2 * hp + j
                        S.I("dve", "bn_stats", out=stats[:, h, :], in_=po_t[:, j, :],
                              reads=[po_k], writes=[("stats", h)])
                        S.I("dve", "bn_aggr", out=mv[:, h, :], in_=stats[:, h, :],
                              reads=[("stats", h)], writes=[("mv", h)])
                    pr = slice(2 * hp, 2 * hp + 2)
                    S.I("act", "activation", out=gsq[:, pr], in_=mv[:, pr, 1], func=AF.Sqrt, bias=epsT[:],
                          reads=[("mv", 2 * hp), ("mv", 2 * hp + 1), "epsT"], writes=[("gsq", hp)])
                    S.I("dve", "reciprocal", out=grs[:, pr], in_=gsq[:, pr], reads=[("gsq", hp)], writes=[("grs", hp)])
                    S.I("dve", "scalar_tensor_tensor", out=gnb[:, pr], in0=mv[:, pr, 0], scalar=-1.0, in1=grs[:, pr], op0=ALU.mult, op1=ALU.mult,
                          reads=[("mv", 2 * hp), ("mv", 2 * hp + 1), ("grs", hp)], writes=[("gnb", hp)])
                    for j in range(2):
                        h = 2 * hp + j
                        S.I("act", "activation", out=on_t[:, h * 256:(h + 1) * 256], in_=po_t[:, j, :], func=AF.Identity, scale=grs[:, h:h + 1], bias=gnb[:, h:h + 1],
                              reads=[po_k, ("grs", hp), ("gnb", hp)], writes=[(on_k, h)])
                    c0 = hp * 512
                    S.I("pool", "tensor_tensor", out=on_t[:, c0:c0 + 512], in0=on_t[:, c0:c0 + 512], in1=sg_t[:, c0:c0 + 512], op=ALU.mult,
                          reads=[(on_k, 2 * hp), (on_k, 2 * hp + 1), (sg_k, hp)], writes=[(on_k, 2 * hp), (on_k, 2 * hp + 1)])
                    if t < NT - 1:
                        for j in range(2):
                            h = 2 * hp + j
                            S.I("pe", "matmul", PU[:, j, :], lhsT=kr_t[:, h, :], rhs=vt_t[:, h, :], start=True, stop=True,
                                  reads=[(kr_k, h // 4, 0), (kr_k, h // 4, 1), (vt_k, h)], writes=["PU"])
                        for j in range(2):
                            h = 2 * hp + j
                            if t == 0:
                                S.I("dve", "tensor_copy", out=Rf[:, h, :], in_=PU[:, j, :],
                                      reads=["PU"], writes=[("Rf", h)])
                            else:
                                S.I("dve", "scalar_tensor_tensor", out=Rf[:, h, :], in0=Rf[:, h, :], scalar=GC[h], in1=PU[:, j, :], op0=ALU.mult, op1=ALU.add,
                                      reads=["PU", ("Rf", h)], writes=[("Rf", h)])
                        S.I("pool", "tensor_copy", out=Rb[:, pr, :], in_=Rf[:, pr, :],
                              reads=[("Rf", 2 * hp), ("Rf", 2 * hp + 1)], writes=[("Rb", hp)])
                zT_t, zT_k = zT(t)
                for r in range(2):
                    for c in range(8):
                        cc = 8 * r + c
                        S.I("pe", "matmul", PT[:, c, :], lhsT=on_t[:, cc * 128:(cc + 1) * 128], rhs=ident[:], start=True, stop=True,
                              reads=[(on_k, cc // 2), "ident"], writes=[("PT", c // 4)])
                    S.I("act", "activation", out=zT_t[:, 8 * r:8 * r + 8, :], in_=PT[:], func=AF.Copy,
                          reads=[("PT", 0), ("PT", 1)], writes=[(zT_k, r)])
                S.I("sp", "dma_start", out=zTa[t], in_=zT_t[:], reads=[(zT_k, 0), (zT_k, 1)], writes=[("zTa", t)], dma_key=("zTst", t % 2))

            A_load(0)
            if NT > 1:
                A_load(1)
            A_S1(0)
            for t in range(NT):
                if t + 1 < NT:
                    A_S1(t + 1)
                if t + 2 < NT:
                    A_load(t + 2)
                A_S2(t)
        S.barrier()
        if STOP_AFTER == "A1":
            S.emit(final_wait_ops=[S.ops[-1]])
            return nc

        def outproj_phase(tagp, w_d, KC, gscale_d, gpost_d, load_z, resid, dst, nblk_sub):
            with contextlib.ExitStack() as stO:
                def sbO(name, shape, dt):
                    return stO.enter_context(nc.sbuf_tensor(tagp + name, list(shape), dt))

                def psO(name, shape, dt):
                    return stO.enter_context(nc.psum_tensor(tagp + "p" + name, list(shape), dt))
                WoB = sbO("WoB", [128, KC, D], BF16)
                stage = Buf(sbO, "stage", [128, 2048], F32, 2)
                gsc = None
                if gscale_d is not None:
                    gsc = sbO("gsc", [128, KC], F32)
                    small_load(gsc[:], gscale_d[:, :], tagp + "gsc")
                    S.last_w[("gsc", tagp + "WoB")] = S.last_w[tagp + "gsc"]
                gpo = sbO("gpo", [128, D], F32)
                small_load(gpo[:], gpost_d[:, :], tagp + "gpo")
                load_weight(sbO, psO, w_d, WoB, KC, D, gsc, stage, ["dve", "pool", "act"], tagp + "WoB")
                xr = Buf(sbO, "xr", [128, D], F32, 3)
                yf = Buf(sbO, "yf", [128, D], F32, 2)
                junk = sbO("junk", [128, 512], BF16)
                ssqy = Buf(sbO, "ssqy", [128, 2], F32, 2)
                ssq1 = Buf(sbO, "ssq1", [128, 1], F32, 2)
                sq = Buf(sbO, "sq", [128, 1], F32, 2)
                rstd = Buf(sbO, "rstd", [128, 1], F32, 2)
                PY = Buf(psO, "PY", [128, 2, 512], F32, 3)
                zload, zget = load_z(sbO)

                def loads(t):
                    zload(t)
                    xr_t, xr_k = xr(t)
                    S.I("sp", "dma_start", out=xr_t[:], in_=resid[t * 128:(t + 1) * 128, :], writes=[xr_k], dma_key=xr_k)
                for t in range(min(2, NT)):
                    loads(t)
                for t in range(NT):
                    if t + 2 < NT:
                        loads(t + 2)
                    lhs_of, zkeys = zget(t)
                    xr_t, xr_k = xr(t)
                    py_t, py_k = PY(t)
                    sy_t, sy_k = ssqy(t)
                    for half in range(2):
                        for c in range(KC):
                            S.I("pe", "matmul", py_t[:, half, :], lhsT=lhs_of(c), rhs=WoB[:, c, half * 512:(half + 1) * 512], start=(c == 0), stop=(c == KC - 1),
                                  reads=zkeys + [wkey_for(tagp + "WoB", c, half * 512)], writes=[(py_k, half)])
                        S.I("act", "activation", out=junk[:], in_=py_t[:, half, :], func=AF.Square, accum_out=sy_t[:, half:half + 1],
                              reads=[(py_k, half)], writes=[tagp + "junk", (sy_k, half)])
                    s1_t, s1_k = ssq1(t)
                    S.I("dve", "tensor_tensor", out=s1_t[:], in0=sy_t[:, 0:1], in1=sy_t[:, 1:2], op=ALU.add,
                          reads=[(sy_k, 0), (sy_k, 1)], writes=[s1_k])
                    r_t, r_k = rstd_chain(t, s1_t[:], s1_k, sq, rstd, 1.0 / D, tagp)
                    yf_t, yf_k = yf(t)
                    for half in range(2):
                        hs = slice(half * 512, (half + 1) * 512)
                        S.I("dve", "scalar_tensor_tensor", out=yf_t[:, hs], in0=py_t[:, half, :], scalar=r_t[:], in1=gpo[:, hs], op0=ALU.mult, op1=ALU.mult,
                              reads=[(py_k, half), r_k, tagp + "gpo"], writes=[(yf_k, half)])
                        S.I("pool", "tensor_tensor", out=yf_t[:, hs], in0=yf_t[:, hs], in1=xr_t[:, hs], op=ALU.add,
                              reads=[(yf_k, half), xr_k], writes=[(yf_k, half)])
                    o = S.I("sp", "dma_start", out=dst[t * 128:(t + 1) * 128, :], in_=yf_t[:],
                              reads=[(yf_k, 0), (yf_k, 1)], writes=[(tagp + "dst", t)], dma_key=(tagp + "yst", t % 2))
                    final_ops.append(o)
            S.barrier()

        final_ops = []

        def load_z_A(sbO):
            zT = Buf(sbO, "zTin", [128, 16, 128], BF16, 3)

            def load(t):
                z_t, z_k = zT(t)
                S.I("sp", "dma_start", out=z_t[:], in_=zTa[t], writes=[z_k], dma_key=z_k)

            def get(t):
                z_t, z_k = zT(t)
                return (lambda c: z_t[:, c, :]), [z_k]
            return load, get

        outproj_phase("A2", w_out_a, 16, gn_gain_a, g_post_a, load_z_A, x, (out if STOP_AFTER == "A2" else h1), 1)
        if STOP_AFTER == "A2":
            S.emit(final_wait_ops=list(final_ops))
            return nc
        final_ops.clear()

        with contextlib.ExitStack() as stB:
            def sbB(name, shape, dt):
                return stB.enter_context(nc.sbuf_tensor("B1" + name, list(shape), dt))

            def psB(name, shape, dt):
                return stB.enter_context(nc.psum_tensor("B1p" + name, list(shape), dt))
            stage = Buf(sbB, "stage", [128, 2048], F32, 2)
            WkvaB = sbB("WkvaB", [128, 8, 320], BF16)
            WukB = sbB("WukB", [128, 2, D], BF16)
            WuvB = sbB("WuvB", [128, 2, D], BF16)
            WinbB = sbB("WinbB", [128, 8, 1408], BF16)
            WuqB = sbB("WuqB", [128, 3, 1536], BF16)
            gkv = sbB("gkv", [128, 8], F32)
            gkl = sbB("gkl", [128, 2], F32)
            gpb = sbB("gpb", [128, 8], F32)
            gql = sbB("gql", [128, 3], F32)
            for nm, dst_, src_ in (("gkv", gkv, g_kv), ("gkl", gkl, g_kv_lat), ("gpb", gpb, g_pre_b), ("gql", gql, g_q_lat)):
                small_load(dst_[:], src_[:, :], "B1" + nm)
            S.last_w[("gsc", "WkvaB")] = S.last_w["B1gkv"]
            S.last_w[("gsc", "WukB")] = S.last_w["B1gkl"]
            S.last_w[("gsc", "WuvB")] = S.last_w["B1gkl"]
            S.last_w[("gsc", "WinbB")] = S.last_w["B1gpb"]
            S.last_w[("gsc", "WuqB")] = S.last_w["B1gql"]
            engs = ["dve", "pool", "act"]
            load_weight(sbB, psB, w_kv_a, WkvaB, 8, 320, gkv, stage, engs, "WkvaB")
            load_weight(sbB, psB, w_in_b, WinbB, 8, 1408, gpb, stage, engs, "WinbB")
            load_weight(sbB, psB, w_uk, WukB, 2, D, gkl, stage, engs, "WukB")
            load_weight(sbB, psB, w_uv, WuvB, 2, D, gkl, stage, engs, "WuvB")
            load_weight(sbB, psB, w_uq, WuqB, 3, 1536, gql, stage, engs, "WuqB")
            if STOP_AFTER == "B1w":
                S.emit(final_wait_ops=[S.ops[-1]])
                return nc
            if B1N is not None:
                S.limit = len(S.ops) + B1N
            if STOP_AFTER == "B1x":
                tst = sbB("tst", [128, D], F32)
                o = S.I("sp", "dma_start", out=tst[:], in_=h1[0:128, :], writes=["tst"], dma_key="tst")
                S.emit(final_wait_ops=[o])
                return nc
            if STOP_AFTER == "B1z":
                xt = Buf(sbB, "xt", [128, D], F32, 2)
                xt_t, xt_k = xt(0)
                o = S.I("sp", "dma_start", out=xt_t[:], in_=h1[0:128, :], writes=[xt_k], dma_key=xt_k)
                S.emit(final_wait_ops=[o])
                return nc
            if STOP_AFTER == "B1y":
                tst = sbB("tst", [128, D], F32)
                o = S.I("sp", "dma_start", out=tst[:], in_=x[0:128, :], writes=["tst"], dma_key="tst")
                S.emit(final_wait_ops=[o])
                return nc

            xt = Buf(sbB, "xt", [128, D], F32, 2)
            junk = sbB("junk", [128, D], BF16)
            ssq = Buf(sbB, "ssq", [128, 1], F32, 2)
            sq = Buf(sbB, "sq", [128, 1], F32, 2)
            rstd = Buf(sbB, "rstd", [128, 1], F32, 2)
            ssqc = Buf(sbB, "ssqc", [128, 1], F32, 2)
            sqc = Buf(sbB, "sqc", [128, 1], F32, 2)
            rstdc = Buf(sbB, "rstdc", [128, 1], F32, 2)
            ssqq = Buf(sbB, "ssqq", [128, 1], F32, 2)
            sqq = Buf(sbB, "sqq", [128, 1], F32, 2)
            rstdq = Buf(sbB, "rstdq", [128, 1], F32, 2)
            hb = Buf(sbB, "hb", [128, D], BF16, 2)
            hTb = Buf(sbB, "hTb", [128, 8, QB], BF16, 2)
            csb = Buf(sbB, "csb", [128, 2, 32], F32, 2)
            chat = Buf(sbB, "chat", [128, 256], BF16, 2)
            kro = Buf(sbB, "kro", [128, 2, 64], BF16, 2)
            ktmp = Buf(sbB, "ktmp", [128, 4, 32], F32, 2)
            chT = Buf(sbB, "chT", [128, 2, QB], BF16, 2)
            krTb = Buf(sbB, "krTb", [128, QB], BF16, 2)
            cqh = Buf(sbB, "cqh", [128, 384], BF16, 2)
            cqT = Buf(sbB, "cqT", [128, 3, QB], BF16, 2)
            knTs = Buf(sbB, "knTs", [128, 8, QB], BF16, 2)
            qnTs = Buf(sbB, "qnTs", [128, 8, QB], BF16, 2)
            sgTs = Buf(sbB, "sgTs", [128, 8, QB], BF16, 2)
            qrTs = Buf(sbB, "qrTs", [128, 4, QB], BF16, 2)
            vsb = Buf(sbB, "vsb", [128, D], BF16, 2)
            qtmp = Buf(sbB, "qtmp", [128, 4, 8, 32], F32, 2)
            qro = Buf(sbB, "qro", [128, 8, 64], BF16, 2)

            PT = psB("PT", [128, 8, 128], F32)
            PA = Buf(psB, "PA", [128, 512], F32, 2)
            PF = Buf(psB, "PF", [128, 512], F32, 2)
            PV = psB("PV", [128, 2, 512], F32)

            def B_load(t):
                xt_t, xt_k = xt(t)
                cs_t, cs_k = csb(t)
                S.I("sp", "dma_start", out=xt_t[:], in_=h1[t * 128:(t + 1) * 128, :], writes=[xt_k], dma_key=xt_k)
                S.I("sp", "dma_start", out=cs_t[:], in_=csB[:, t, :, :], writes=[(cs_k, 0), (cs_k, 1)], dma_key=cs_k)

            for bq in range(NQB):
                hTb_t, hTb_k = hTb(bq)
                chT_t, chT_k = chT(bq)
                krT_t, krT_k = krTb(bq)
                cqT_t, cqT_k = cqT(bq)
                qrT_t, qrT_k = qrTs(bq)
                for sub in range(SUB):
                    t = bq * SUB + sub
                    ts_ = slice(sub * 128, (sub + 1) * 128)
                    xt_t, xt_k = xt(t)
                    cs_t, cs_k = csb(t)
                    if t == 0:
                        B_load(0)
                    if t + 1 < NT:
                        B_load(t + 1)
                    ssq_t, ssq_k = ssq(t)
                    S.I("act", "activation", out=junk[:], in_=xt_t[:], func=AF.Square, accum_out=ssq_t[:],
                          reads=[xt_k], writes=["B1junk", ssq_k])
                    r_t, r_k = rstd_chain(t, ssq_t[:], ssq_k, sq, rstd, 1.0 / D, "b")
                    hb_t, hb_k = hb(t)
                    S.I("dve", "tensor_scalar", out=hb_t[:], in0=xt_t[:], scalar1=r_t[:], scalar2=None, op0=ALU.mult,
                          reads=[xt_k, r_k], writes=[hb_k])
                    for k in range(8):
                        S.I("pe", "matmul", PT[:, k, :], lhsT=hb_t[:, k * 128:(k + 1) * 128], rhs=ident[:], start=True, stop=True,
                              reads=[hb_k, "ident"], writes=[("PT", k // 4)])
                    S.I("act", "activation", out=hTb_t[:, :, ts_], in_=PT[:], func=AF.Copy,
                          reads=[("PT", 0), ("PT", 1)], writes=[(hTb_k, sub)])
                    if t == 0:
                        stop_here("B1a")
                    pa_t, pa_k = PA(2 * t)
                    for k in range(8):
                        S.I("pe", "matmul", pa_t[:, 0:320], lhsT=hTb_t[:, k, ts_], rhs=WkvaB[:, k, :], start=(k == 0), stop=(k == 7),
                              reads=[(hTb_k, sub), wkey_for("WkvaB", k, 0)], writes=[pa_k])
                    sc_t, sc_k = ssqc(t)
                    S.I("act", "activation", out=junk[:, 0:256], in_=pa_t[:, 0:256], func=AF.Square, accum_out=sc_t[:],
                          reads=[pa_k], writes=["B1junk", sc_k])
                    rc_t, rc_k = rstd_chain(t, sc_t[:], sc_k, sqc, rstdc, 1.0 / 256, "c")
                    ch_t, ch_k = chat(t)
                    S.I("dve", "tensor_scalar", out=ch_t[:], in0=pa_t[:, 0:256], scalar1=rc_t[:], scalar2=None, op0=ALU.mult,
                          reads=[pa_k, rc_k], writes=[ch_k])
                    if t == 0:
                        stop_here("B1b")
                    kt_t, kt_k = ktmp(t)
                    ko_t, ko_k = kro(t)
                    x1 = pa_t[:, 256:288]
                    x2 = pa_t[:, 288:320]
                    S.I("dve", "tensor_tensor", out=kt_t[:, 0, :], in0=x1, in1=cs_t[:, 0, :], op=ALU.mult, reads=[pa_k, (cs_k, 0)], writes=[(kt_k, 0)])
                    S.I("dve", "tensor_tensor", out=kt_t[:, 1, :], in0=x2, in1=cs_t[:, 1, :], op=ALU.mult, reads=[pa_k, (cs_k, 1)], writes=[(kt_k, 1)])
                    S.I("dve", "tensor_tensor", out=kt_t[:, 2, :], in0=x1, in1=cs_t[:, 1, :], op=ALU.mult, reads=[pa_k, (cs_k, 1)], writes=[(kt_k, 2)])
                    S.I("dve", "tensor_tensor", out=kt_t[:, 3, :], in0=x2, in1=cs_t[:, 0, :], op=ALU.mult, reads=[pa_k, (cs_k, 0)], writes=[(kt_k, 3)])
                    S.I("pool", "tensor_tensor", out=ko_t[:, 0, 0:32], in0=kt_t[:, 0, :], in1=kt_t[:, 1, :], op=ALU.subtract, reads=[(kt_k, 0), (kt_k, 1)], writes=[(ko_k, 0)])
                    S.I("pool", "tensor_tensor", out=ko_t[:, 0, 32:64], in0=kt_t[:, 2, :], in1=kt_t[:, 3, :], op=ALU.add, reads=[(kt_k, 2), (kt_k, 3)], writes=[(ko_k, 1)])
                    S.I("pool", "tensor_copy", out=ko_t[:, 1, :], in_=ko_t[:, 0, :], reads=[(ko_k, 0), (ko_k, 1)], writes=[(ko_k, 2)])
                    if t == 0:
                        stop_here("B1c")
                    for c in range(2):
                        S.I("pe", "matmul", PT[:, c, :], lhsT=ch_t[:, c * 128:(c + 1) * 128], rhs=ident[:], start=True, stop=True,
                              reads=[ch_k, "ident"], writes=[("PT", c // 4)])
                    S.I("pe", "matmul", PT[:, 2, :], lhsT=ko_t[:].rearrange("p a b -> p (a b)"), rhs=ident[:], start=True, stop=True,
                          reads=[(ko_k, 0), (ko_k, 1), (ko_k, 2), "ident"], writes=[("PT", 0)])
                    S.I("act", "activation", out=chT_t[:, :, ts_], in_=PT[:, 0:2, :], func=AF.Copy,
                          reads=[("PT", 0)], writes=[(chT_k, sub)])
                    S.I("dve", "tensor_copy", out=krT_t[:, ts_], in_=PT[:, 2, :],
                          reads=[("PT", 0)], writes=[(krT_k, sub)])
                    if t == 0:
                        stop_here("B1d")
                    pq_t, pq_k = PA(2 * t + 1)
                    for k in range(8):
                        S.I("pe", "matmul", pq_t[:, 0:384], lhsT=hTb_t[:, k, ts_], rhs=WinbB[:, k, 0:384], start=(k == 0), stop=(k == 7),
                              reads=[(hTb_k, sub), wkey_for("WinbB", k, 0)], writes=[pq_k])
                    sq_t2, sq_k2 = ssqq(t)
                    S.I("act", "activation", out=junk[:, 0:384], in_=pq_t[:, 0:384], func=AF.Square, accum_out=sq_t2[:],
                          reads=[pq_k], writes=["B1junk", sq_k2])
                    rq_t, rq_k = rstd_chain(t, sq_t2[:], sq_k2, sqq, rstdq, 1.0 / 384, "q")
                    cq_t, cq_k = cqh(t)
                    S.I("dve", "tensor_scalar", out=cq_t[:], in0=pq_t[:, 0:384], scalar1=rq_t[:], scalar2=None, op0=ALU.mult,
                          reads=[pq_k, rq_k], writes=[cq_k])
                    for c in range(3):
                        S.I("pe", "matmul", PT[:, 4 + c, :], lhsT=cq_t[:, c * 128:(c + 1) * 128], rhs=ident[:], start=True, stop=True,
                              reads=[cq_k, "ident"], writes=[("PT", 1)])
                    S.I("act", "activation", out=cqT_t[:, :, ts_], in_=PT[:, 4:7, :], func=AF.Copy,
                          reads=[("PT", 1)], writes=[(cqT_k, sub)])
                    if t == 0:
                        stop_here("B1e")
                    for half in range(2):
                        for c in range(2):
                            S.I("pe", "matmul", PV[:, half, :], lhsT=chT_t[:, c, ts_], rhs=WuvB[:, c, half * 512:(half + 1) * 512], start=(c == 0), stop=(c == 1),
                                  reads=[(chT_k, sub), wkey_for("WuvB", c, half * 512)], writes=[("PV", half)])
                    v_t, v_k = vsb(t)
                    S.I("act", "activation", out=v_t[:], in_=PV[:].rearrange("p a b -> p (a b)"), func=AF.Copy,
                          reads=[("PV", 0), ("PV", 1)], writes=[v_k])
                    S.I("sp", "dma_start", out=vS[t], in_=v_t[:], reads=[v_k], writes=[("vS", t)], dma_key=("vst", t % 2))
                    if 'q' not in SKIP:
                        pr_t, pr_k = PF(2 * t)
                        wq_r = WuqB[:].rearrange("p c (h d) -> p c h d", h=8)
                        for c in range(3):
                            S.I("pe", "matmul", pr_t[:].rearrange("p (h d) -> p h d", h=8), lhsT=cqT_t[:, c, ts_], rhs=wq_r[:, c, :, 128:192], start=(c == 0), stop=(c == 2),
                                  reads=[(cqT_k, sub), wkey_for("WuqB", c, 0)], writes=[pr_k])
                        p3 = pr_t[:].rearrange("p (h d) -> p h d", h=8)
                        q1 = p3[:, :, 0:32]
                        q2 = p3[:, :, 32:64]
                        cb_ = cs_t[:, 0, :].unsqueeze(1).to_broadcast([128, 8, 32])
                        sb_ = cs_t[:, 1, :].unsqueeze(1).to_broadcast([128, 8, 32])
                        qt_t, qt_k = qtmp(t)
                        qo_t, qo_k = qro(t)
                        S.I("dve", "tensor_tensor", out=qt_t[:, 0], in0=q1, in1=cb_, op=ALU.mult, reads=[pr_k, (cs_k, 0)], writes=[(qt_k, 0)])
                        S.I("dve", "tensor_tensor", out=qt_t[:, 1], in0=q2, in1=sb_, op=ALU.mult, reads=[pr_k, (cs_k, 1)], writes=[(qt_k, 1)])
                        S.I("dve", "tensor_tensor", out=qt_t[:, 2], in0=q1, in1=sb_, op=ALU.mult, reads=[pr_k, (cs_k, 1)], writes=[(qt_k, 2)])
                        S.I("dve", "tensor_tensor", out=qt_t[:, 3], in0=q2, in1=cb_, op=ALU.mult, reads=[pr_k, (cs_k, 0)], writes=[(qt_k, 3)])
                        S.I("pool", "tensor_tensor", out=qo_t[:, :, 0:32], in0=qt_t[:, 0], in1=qt_t[:, 1], op=ALU.subtract, reads=[(qt_k, 0), (qt_k, 1)], writes=[(qo_k, 0)])
                        S.I("pool", "tensor_tensor", out=qo_t[:, :, 32:64], in0=qt_t[:, 2], in1=qt_t[:, 3], op=ALU.add, reads=[(qt_k, 2), (qt_k, 3)], writes=[(qo_k, 1)])
                        for pr_i in range(4):
                            S.I("pe", "matmul", PT[:, pr_i, :], lhsT=qo_t[:, 2 * pr_i:2 * pr_i + 2, :].rearrange("p a b -> p (a b)"), rhs=ident[:], start=True, stop=True,
                                  reads=[(qo_k, 0), (qo_k, 1), "ident"], writes=[("PT", 0)])
                        S.I("act", "activation", out=qrT_t[:, :, ts_], in_=PT[:, 0:4, :], func=AF.Copy,
                              reads=[("PT", 0)], writes=[(qrT_k, sub)])
                if 'f' not in SKIP:
                    allsub = lambda key: [(key, s_) for s_ in range(SUB)]
                    kn_t, kn_k = knTs(bq)
                    qn_t, qn_k = qnTs(bq)
                    sgT_t, sgT_k = sgTs(bq)
                    fi = 0
                    for h in range(8):
                        pf_t, pf_k = PF(fi); fi += 1
                        for c in range(2):
                            S.I("pe", "matmul", pf_t[:, 0:QB], lhsT=WukB[:, c, h * 128:(h + 1) * 128], rhs=chT_t[:, c, :], start=(c == 0), stop=(c == 1),
                                  reads=allsub(chT_k) + [wkey_for("WukB", c, h * 128)], writes=[pf_k])
                        S.I("dve", "tensor_copy", out=kn_t[:, h, :], in_=pf_t[:, 0:QB], reads=[pf_k], writes=[(kn_k, h)])
                    for h in range(8):
                        pf_t, pf_k = PF(fi); fi += 1
                        for c in range(3):
                            S.I("pe", "matmul", pf_t[:, 0:QB], lhsT=WuqB[:, c, h * 192:h * 192 + 128], rhs=cqT_t[:, c, :], start=(c == 0), stop=(c == 2),
                                  reads=allsub(cqT_k) + [wkey_for("WuqB", c, h * 192), wkey_for("WuqB", c, h * 192 + 127)], writes=[pf_k])
                        S.I("dve", "tensor_copy", out=qn_t[:, h, :], in_=pf_t[:, 0:QB], reads=[pf_k], writes=[(qn_k, h)])
                    for h in range(8):
                        pf_t, pf_k = PF(fi); fi += 1
                        for k in range(8):
                            S.I("pe", "matmul", pf_t[:, 0:QB], lhsT=WinbB[:, k, 384 + h * 128:384 + (h + 1) * 128], rhs=hTb_t[:, k, :], start=(k == 0), stop=(k == 7),
                                  reads=allsub(hTb_k) + [wkey_for("WinbB", k, 384 + h * 128), wkey_for("WinbB", k, 384 + h * 128 + 127)], writes=[pf_k])
                        S.I("act", "activation", out=sgT_t[:, h, :], in_=pf_t[:, 0:QB], func=AF.Silu, reads=[pf_k], writes=[(sgT_k, h)])
                if 's' not in SKIP:
                    bs = slice(bq * QB, (bq + 1) * QB)
                    S.I("sp", "dma_start", out=knT[:, :, bs].rearrange("h p t -> p h t"), in_=kn_t[:],
                          reads=[(kn_k, h) for h in range(8)], writes=[("knT", bq)], dma_key=("knst", bq % 2))
                    S.I("sp", "dma_start", out=qnT[:, :, bs].rearrange("h p t -> p h t"), in_=qn_t[:],
                          reads=[(qn_k, h) for h in range(8)], writes=[("qnT", bq)], dma_key=("qnst", bq % 2))
                    S.I("sp", "dma_start", out=sgT[:, :, bs].rearrange("h p t -> p h t"), in_=sgT_t[:],
                          reads=[(sgT_k, h) for h in range(8)], writes=[("sgT", bq)], dma_key=("sgst", bq % 2))
                    S.I("sp", "dma_start", out=qrT[:, :, bs].rearrange("h p t -> p h t"), in_=qrT_t[:],
                          reads=allsub(qrT_k), writes=[("qrT", bq)], dma_key=("qrst", bq % 2))
                    S.I("sp", "dma_start", out=krT[:, bs], in_=krT_t[:],
                          reads=allsub(krT_k), writes=[("krT", bq)], dma_key=("krst", bq % 2))
        S.barrier()

        if STOP_AFTER == "B1":
            S.emit(final_wait_ops=[S.ops[-1]])
            return nc
        SCALE = float((128 + 64) ** -0.5)
        with contextlib.ExitStack() as stC:
            def sbC(name, shape, dt):
                return stC.enter_context(nc.sbuf_tensor("B2" + name, list(shape), dt))

            def psC(name, shape, dt):
                return stC.enter_context(nc.psum_tensor("B2p" + name, list(shape), dt))
            ones = sbC("ones", [128, 128], BF16)
            S.I("pool", "memset", ones[:], 1.0, writes=["ones"])
            krS = sbC("krS", [128, S_len], BF16)
            S.I("sp", "dma_start", out=krS[:], in_=krT[:, :], writes=["krS"], dma_key="krS")
            knS = Buf(sbC, "knS", [128, S_len], BF16, 2)
            qnS = Buf(sbC, "qnS", [128, S_len], BF16, 2)
            sgS = Buf(sbC, "sgS", [128, S_len], BF16, 2)
            qrS = Buf(sbC, "qrS", [128, S_len], BF16, 2)
            v2S = Buf(sbC, "v2S", [128, NT, 256], BF16, 2)
            pTb = Buf(sbC, "pTb", [128, QB], BF16, 4)
            rden = Buf(sbC, "rden", [128, QB], F32, 2)
            ob = Buf(sbC, "ob", [128, QB], F32, 2)
            zb = Buf(sbC, "zb", [128, QB], BF16, 2)
            PSc = Buf(psC, "PSc", [128, 512], F32, 4)
            POc = Buf(psC, "POc", [128, 512], F32, 2)
            PDc = Buf(psC, "PDc", [128, 512], F32, 2)
            heads = {}

            def load_head(h):
                hp = h // 2
                kn_t, kn_k = knS(h)
                qn_t, qn_k = qnS(h)
                sg_t, sg_k = sgS(h)
                S.I("sp", "dma_start", out=kn_t[:], in_=knT[h], writes=[kn_k], dma_key=kn_k)
                S.I("sp", "dma_start", out=qn_t[:], in_=qnT[h], writes=[qn_k], dma_key=qn_k)
                S.I("sp", "dma_start", out=sg_t[:], in_=sgT[h], writes=[sg_k], dma_key=sg_k)
                qr_t, qr_k = qrS(hp)
                v2_t, v2_k = v2S(hp)
                if h % 2 == 0:
                    S.I("sp", "dma_start", out=qr_t[:], in_=qrT[hp], writes=[qr_k], dma_key=qr_k)
                    S.I("sp", "dma_start", out=v2_t[:], in_=vS[:, :, hp * 256:(hp + 1) * 256].rearrange("t p d -> p t d"), writes=[v2_k], dma_key=v2_k)
                heads[h] = (kn_t, kn_k, qn_t, qn_k, sg_t, sg_k, qr_t, qr_k, v2_t, v2_k)

            items = []
            blk = 0
            for h in range(8):
                for qb in range(NQB):
                    nkt = SUB * qb + SUB
                    for kt in range(nkt):
                        items.append((h, qb, kt, nkt, blk))
                    blk += 1

            def geom(i):
                h, qb, kt, nkt, blk_ = items[i]
                j = kt - SUB * qb
                c0 = 128 * j if j > 0 else 0
                return h, qb, kt, nkt, blk_, j, c0

            def qk(i):
                h, qb, kt, nkt, blk_, j, c0 = geom(i)
                kn_t, kn_k, qn_t, qn_k, sg_t, sg_k, qr_t, qr_k, v2_t, v2_k = heads[h]
                pb = 64 * (h % 2)
                qs = slice(qb * QB + c0, (qb + 1) * QB)
                ks = slice(kt * 128, (kt + 1) * 128)
                ps_t, ps_k = PSc(i)
                pt_t, pt_k = pTb(i)
                S.I("pe", "matmul", ps_t[:, c0:QB], lhsT=kn_t[:, ks], rhs=qn_t[:, qs], start=True, stop=False,
                    reads=[kn_k, qn_k], writes=[ps_k])
                S.I("pe", "matmul", ps_t[:, c0:QB], lhsT=krS[pb:pb + 64, ks], rhs=qr_t[pb:pb + 64, qs], start=False, stop=True,
                    reads=["krS", qr_k], writes=[ps_k])
                S.I("act", "activation", out=pt_t[:, c0:QB], in_=ps_t[:, c0:QB], func=AF.Exp, scale=SCALE,
                    reads=[ps_k], writes=[pt_k])
                if j >= 0:
                    S.I("pool", "memset", pt_t[64:128, c0:c0 + 64], 0.0, reads=[], writes=[pt_k])

            def pv(i):
                h, qb, kt, nkt, blk_, j, c0 = geom(i)
                kn_t, kn_k, qn_t, qn_k, sg_t, sg_k, qr_t, qr_k, v2_t, v2_k = heads[h]
                pt_t, pt_k = pTb(i)
                po_t, po_k = POc(blk_)
                pd_t, pd_k = PDc(blk_)
                S.I("pe", "matmul", po_t[:, c0:QB], lhsT=v2_t[:, kt, (h % 2) * 128:(h % 2) * 128 + 128], rhs=pt_t[:, c0:QB], start=(kt == 0), stop=(kt == nkt - 1),
                    reads=[v2_k, pt_k], writes=[po_k])
                S.I("pe", "matmul", pd_t[:, c0:QB], lhsT=ones[:], rhs=pt_t[:, c0:QB], start=(kt == 0), stop=(kt == nkt - 1),
                    reads=["ones", pt_k], writes=[pd_k])
                if kt == nkt - 1:
                    rd_t, rd_k = rden(blk_)
                    ob_t, ob_k = ob(blk_)
                    zb_t, zb_k = zb(blk_)
                    S.I("dve", "reciprocal", out=rd_t[:], in_=pd_t[:, 0:QB], reads=[pd_k], writes=[rd_k])
                    S.I("dve", "tensor_tensor", out=ob_t[:], in0=po_t[:, 0:QB], in1=rd_t[:], op=ALU.mult, reads=[po_k, rd_k], writes=[ob_k])
                    S.I("pool", "tensor_tensor", out=zb_t[:], in0=ob_t[:], in1=sg_t[:, qb * QB:(qb + 1) * QB], op=ALU.mult, reads=[ob_k, sg_k], writes=[zb_k])
                    S.I("sp", "dma_start", out=zTb[qb, :, h, :], in_=zb_t[:], reads=[zb_k], writes=[("zTb", qb, h)], dma_key=("zbst", blk_ % 2))
                    if qb == NQB - 1 and h + 2 < 8:
                        load_head(h + 2)

            load_head(0)
            load_head(1)
            LOOK = 3
            for i in range(min(LOOK, len(items))):
                qk(i)
            for i in range(len(items)):
                if i + LOOK < len(items):
                    qk(i + LOOK)
                pv(i)
        S.barrier()

        if STOP_AFTER == "B2":
            S.emit(final_wait_ops=[S.ops[-1]])
            return nc
        def load_z_B(sbO):
            zT = Buf(sbO, "zTin", [128, 8, QB], BF16, 3)
            done = set()

            def load(t):
                bq = t // SUB
                for b_ in (bq, bq + 1):
                    if b_ < NQB and b_ not in done:
                        done.add(b_)
                        z_t, z_k = zT(b_)
                        S.I("sp", "dma_start", out=z_t[:], in_=zTb[b_], writes=[z_k], dma_key=z_k)

            def get(t):
                bq, sub = divmod(t, SUB)
                z_t, z_k = zT(bq)
                return (lambda c: z_t[:, c, sub * 128:(sub + 1) * 128]), [z_k]
            return load, get

        outproj_phase("B3", w_out_b, 8, None, g_post_b, load_z_B, h1, out, 1)
        S.emit(final_wait_ops=list(final_ops))
    return nc


def _chunkT(w, K):
    n = w.shape[1]
    return np.ascontiguousarray(w.reshape(K, 128, n).transpose(1, 0, 2))


def _vecT(g, K):
    return np.ascontiguousarray(g.reshape(K, 128).T)


def _rope_tables(S_len, half):
    inv = (np.float32(10000.0) ** (-np.arange(half, dtype=np.float32) / np.float32(half))).astype(np.float32)
    ang = (np.arange(S_len, dtype=np.float32)[:, None] * inv[None, :]).astype(np.float32)
    c = np.cos(ang).astype(np.float32)
    s = np.sin(ang).astype(np.float32)
    nt = S_len // 128
    lay = lambda a: np.ascontiguousarray(a.reshape(nt, 128, half).transpose(1, 0, 2))
    return lay(c), lay(s)


def _decay_tables():
    h = np.arange(NH, dtype=np.float64)
    lg = np.log(1.0 - np.exp2(-5.0 - h))
    i = np.arange(128, dtype=np.float64)
    xi = np.exp(lg[:, None] * (i + 1.0)[None])
    zeta = np.exp(lg[:, None] * (127.0 - i)[None])
    ch = (np.arange(128) // 64)
    c_ = i[:, None]
    m_ = i[None, :]
    same = ch[:, None] == ch[None, :]
    prev = ch[None, :] < ch[:, None]
    Dm = np.zeros((NH, 128, 128))
    for hh in range(NH):
        Dm[hh] = np.where(same, np.exp(lg[hh] * np.abs(c_ - m_)), np.where(prev, np.exp(lg[hh] * (c_ - m_)), 0.0))
    T2 = Dm / (xi[:, :, None] * zeta[:, None, :])
    t2t = np.ascontiguousarray(T2.transpose(2, 0, 1)).astype(np.float32)
    dxi = np.zeros((128, NH, 128), np.float32)
    for hh in range(NH):
        dxi[np.arange(128), hh, np.arange(128)] = xi[hh]
    zs = np.ascontiguousarray((zeta * (128.0 ** -0.5)).T).astype(np.float32)
    return t2t, dxi, zs


def _prep_shared(inp, S_len):
    f = lambda a: np.asarray(a, dtype=np.float32)
    cA, sA = _rope_tables(S_len, 64)
    cB, sB = _rope_tables(S_len, 32)
    t2t, dxi, zs = _decay_tables()
    return {
        "w_in_a": _chunkT(f(inp["w_in_a"])[0], 8),
        "g_pre_a": _vecT(f(inp["g_pre_a"])[0], 8),
        "gn_gain_a": _vecT(f(inp["gn_gain_a"])[0], 16),
        "w_out_a": _chunkT(f(inp["w_out_a"])[0], 16),
        "g_post_a": np.ascontiguousarray(np.broadcast_to(f(inp["g_post_a"])[0][None, :], (128, D))),
        "g_kv": _vecT(f(inp["g_kv"]), 8),
        "w_kv_a": _chunkT(f(inp["w_kv_a"]), 8),
        "g_kv_lat": _vecT(f(inp["g_kv_lat"]), 2),
        "w_uk": _chunkT(f(inp["w_uk"]), 2),
        "w_uv": _chunkT(f(inp["w_uv"]), 2),
        "g_pre_b": _vecT(f(inp["g_pre_b"])[0], 8),
        "w_in_b": _chunkT(f(inp["w_in_b"])[0], 8),
        "g_q_lat": _vecT(f(inp["g_q_lat"])[0], 3),
        "w_uq": _chunkT(f(inp["w_uq"])[0], 3),
        "w_out_b": _chunkT(f(inp["w_out_b"])[0], 8),
        "g_post_b": np.ascontiguousarray(np.broadcast_to(f(inp["g_post_b"])[0][None, :], (128, D))),
        "csA": np.ascontiguousarray(np.stack([cA, sA], axis=2)), "csB": np.ascontiguousarray(np.stack([cB, sB], axis=2)),
        "t2t": t2t, "dxi": dxi, "zs": zs,
        "ident": np.eye(128, dtype=np.float32),
    }


def kernel(**inputs):
    x = np.asarray(inputs["x"], dtype=np.float32)
    B, S_len, _ = x.shape
    shared = _prep_shared(inputs, S_len)
    nc = build(S_len)
    in_maps = []
    for b in range(B):
        m = dict(shared)
        m["x"] = np.ascontiguousarray(x[b])
        in_maps.append(m)
    res = run_bass_kernel_spmd(nc, in_maps, core_ids=list(range(B)))
    return np.stack([np.asarray(r["out"], dtype=np.float32) for r in res.results], axis=0)
```

```python
import contextlib
import numpy as np
import concourse.bass as bass
import concourse.mybir as mybir
from concourse.bass_utils import run_bass_kernel_spmd

F32 = mybir.dt.float32
BF16 = mybir.dt.bfloat16
ALU = mybir.AluOpType
AF = mybir.ActivationFunctionType

D = 1024
EPS = 1e-6
NH = 8
SEQ = 4096
STOP_AFTER = None
SKIP = ''
B1N = None


class _Stop(Exception):
    pass


class Op:
    __slots__ = ("eng", "fn", "deps", "idx", "signal", "dma_key", "cnt")

    def __init__(self, eng, fn, deps, dma_key):
        self.eng = eng
        self.fn = fn
        self.deps = deps
        self.dma_key = dma_key
        self.signal = False
        self.cnt = None


class Sched:
    COMPUTE = ("pe", "dve", "act", "pool")

    def __init__(self, nc):
        self.nc = nc
        self.ops = []
        self.last_w = {}
        self.readers = {}
        self.bar = []
        self.last_eng = {}
        self.last_key = {}

    PSUM_ROOTS = {"PT", "PS", "PO", "PU", "PP", "PY", "PV", "PA", "PF", "PSc", "POc", "PDc"}

    @classmethod
    def _excl(cls, key):
        while not isinstance(key, str):
            key = key[0]
        return key in cls.PSUM_ROOTS

    def add(self, eng, fn, reads=(), writes=(), dma_key=None):
        ex = [r for r in reads if self._excl(r)]
        if ex:
            reads = [r for r in reads if not self._excl(r)]
            writes = list(writes) + [r for r in ex if r not in writes]
        deps = list(self.bar)
        for r in reads:
            w = self.last_w.get(r)
            if w is not None:
                deps.append(w)
        for r in writes:
            w = self.last_w.get(r)
            if w is not None:
                deps.append(w)
            deps.extend(self.readers.get(r, ()))
        if getattr(self, "limit", None) is not None and len(self.ops) >= self.limit:
            raise _Stop()
        op = Op(eng, fn, deps, dma_key)
        op.idx = len(self.ops)
        self.ops.append(op)
        for r in reads:
            self.readers.setdefault(r, []).append(op)
        for r in writes:
            self.last_w[r] = op
            self.readers[r] = []
        if dma_key is None:
            self.last_eng[eng] = op
        else:
            self.last_key[dma_key] = op
        return op

    def I(self, eng, meth, *args, reads=(), writes=(), dma_key=None, **kw):
        return self.add(eng, lambda e: getattr(e, meth)(*args, **kw), reads=reads, writes=writes, dma_key=dma_key)

    def barrier(self):
        self.bar = list(self.last_eng.values()) + list(self.last_key.values())
        self.last_w = {}
        self.readers = {}

    @staticmethod
    def _pe_pe(d, op):
        return d.dma_key is None and op.dma_key is None and d.eng == "pe" and op.eng == "pe"

    def emit(self, final_wait_ops=()):
        nc = self.nc
        for op in self.ops:
            for d in op.deps:
                if not self._pe_pe(d, op):
                    d.signal = True
        for op in final_wait_ops:
            op.signal = True
        for op in self.ops:
            if op.dma_key is not None:
                op.signal = True
        eng_cnt = {e: 0 for e in self.COMPUTE}
        key_cnt = {}
        for op in self.ops:
            if not op.signal:
                continue
            if op.dma_key is not None:
                key_cnt[op.dma_key] = key_cnt.get(op.dma_key, 0) + 1
                op.cnt = key_cnt[op.dma_key] * 16
            else:
                eng_cnt[op.eng] += 1
                op.cnt = eng_cnt[op.eng]
        sems = {}
        with contextlib.ExitStack() as st:
            for e in self.COMPUTE:
                sems[("eng", e)] = st.enter_context(nc.semaphore("s_" + e))
            for i, k in enumerate(sorted(key_cnt, key=str)):
                sems[("dma", k)] = st.enter_context(nc.semaphore("d%d" % i))
            block = st.enter_context(nc.Block())
            queues = {}
            for op in self.ops:
                queues.setdefault(op.eng, []).append(op)
            engmap = {"pe": ("tensor", nc.tensor), "dve": ("vector", nc.vector),
                      "act": ("scalar", nc.scalar), "pool": ("gpsimd", nc.gpsimd),
                      "sp": ("sync", nc.sync)}

            def semof(op):
                if op.dma_key is not None:
                    return sems[("dma", op.dma_key)]
                return sems[("eng", op.eng)]

            def run_queue(eng, ops, final):
                known = {}
                for op in ops:
                    need = {}
                    for d in op.deps:
                        if d.cnt is None or self._pe_pe(d, op):
                            continue
                        s = semof(d)
                        key = id(s)
                        if known.get(key, 0) >= d.cnt:
                            continue
                        if key not in need or need[key][1] < d.cnt:
                            need[key] = (s, d.cnt)
                    for key, (s, c) in need.items():
                        eng.wait_ge(s, c)
                        known[key] = c
                    ins = op.fn(eng)
                    if op.signal:
                        ins.then_inc(semof(op), 16 if op.dma_key is not None else 1)
                for op in final:
                    eng.wait_ge(semof(op), op.cnt)

            for ename, (attr, eng) in engmap.items():
                ops = queues.get(ename, [])
                final = list(final_wait_ops) if ename == "sp" else []
                if not ops and not final:
                    continue

                def body(e, _ops=ops, _final=final):
                    run_queue(e, _ops, _final)
                getattr(block, attr)(body)


class Buf:
    def __init__(self, alloc, name, shape, dt, nbuf=1):
        self.t = [alloc("%s_%d" % (name, i), shape, dt) for i in range(nbuf)]
        self.name = name
        self.n = nbuf

    def __call__(self, i=0):
        j = i % self.n
        return self.t[j], (self.name, j)


def build(S_len=SEQ):
    ctx = {}
    try:
        return _build(S_len, ctx)
    except _Stop:
        return ctx["nc"]


def _build(S_len, ctx):
    NT = S_len // 128
    QB = min(512, S_len)
    NQB = S_len // QB
    SUB = QB // 128
    nc = bass.Bass("TRN2", target_bir_lowering=False)

    def din(name, shape, dt=F32):
        return nc.dram_tensor(name, list(shape), dt, kind="ExternalInput").ap()

    def dscr(name, shape, dt):
        return nc.dram_tensor(name, list(shape), dt, kind="Internal").ap()

    x = din("x", [S_len, D])
    w_in_a = din("w_in_a", [128, 8, 6144])
    g_pre_a = din("g_pre_a", [128, 8])
    gn_gain_a = din("gn_gain_a", [128, 16])
    w_out_a = din("w_out_a", [128, 16, D])
    g_post_a = din("g_post_a", [128, D])
    g_kv = din("g_kv", [128, 8])
    w_kv_a = din("w_kv_a", [128, 8, 320])
    g_kv_lat = din("g_kv_lat", [128, 2])
    w_uk = din("w_uk", [128, 2, D])
    w_uv = din("w_uv", [128, 2, D])
    g_pre_b = din("g_pre_b", [128, 8])
    w_in_b = din("w_in_b", [128, 8, 1408])
    g_q_lat = din("g_q_lat", [128, 3])
    w_uq = din("w_uq", [128, 3, 1536])
    w_out_b = din("w_out_b", [128, 8, D])
    g_post_b = din("g_post_b", [128, D])
    csA = din("csA", [128, NT, 2, 64])
    csB = din("csB", [128, NT, 2, 32])
    t2t_d = din("t2t", [128, 8, 128])
    dxi_d = din("dxi", [128, 8, 128])
    ident_d = din("ident", [128, 128])
    zs_d = din("zs", [128, 8])
    gc_host = None
    out = nc.dram_tensor("out", [S_len, D], F32, kind="ExternalOutput").ap()

    zTa = dscr("zTa", [NT, 128, 16, 128], BF16)
    h1 = dscr("h1", [S_len, D], F32)
    knT = dscr("knT", [8, 128, S_len], BF16)
    krT = dscr("krT", [128, S_len], BF16)
    vS = dscr("vS", [NT, 128, D], BF16)
    qnT = dscr("qnT", [8, 128, S_len], BF16)
    qrT = dscr("qrT", [4, 128, S_len], BF16)
    sgT = dscr("sgT", [8, 128, S_len], BF16)
    zTb = dscr("zTb", [NQB, 128, 8, QB], BF16)

    GC = [float((1.0 - 2.0 ** (-5.0 - h)) ** 128) for h in range(NH)]

    S = Sched(nc)
    ctx["S"] = S
    ctx["nc"] = nc

    def stop_here(tag):
        if STOP_AFTER == tag:
            last = [o for o in S.ops if o.dma_key is not None][-1]
            S.emit(final_wait_ops=[last, S.ops[-1]])
            raise _Stop()

    def load_weight(sb, ps_alloc, src, dst, K, N, scale, stage, eng_cycle, tag):
        i = 0
        for k in range(K):
            for n0 in range(0, N, 2048):
                n1 = min(N, n0 + 2048)
                st_t, st_k = stage(i)
                S.I("sp", "dma_start", out=st_t[:, 0:n1 - n0], in_=src[:, k, n0:n1],
                      writes=[st_k], dma_key=st_k)
                eng = eng_cycle[i % len(eng_cycle)]
                rd = [st_k] + ([("gsc", tag)] if scale is not None else [])
                wkey = (tag, k, n0)
                if scale is None:
                    if eng == "act":
                        S.I("act", "activation", out=dst[:, k, n0:n1], in_=st_t[:, 0:n1 - n0], func=AF.Copy,
                              reads=rd, writes=[wkey])
                    else:
                        S.I(eng, "tensor_copy", out=dst[:, k, n0:n1], in_=st_t[:, 0:n1 - n0],
                              reads=rd, writes=[wkey])
                else:
                    if eng == "act":
                        S.I("act", "activation", out=dst[:, k, n0:n1], in_=st_t[:, 0:n1 - n0], func=AF.Copy, scale=scale[:, k:k + 1],
                              reads=rd, writes=[wkey])
                    else:
                        S.I(eng, "tensor_scalar", out=dst[:, k, n0:n1], in0=st_t[:, 0:n1 - n0], scalar1=scale[:, k:k + 1], scalar2=None, op0=ALU.mult,
                              reads=rd, writes=[wkey])
                i += 1

    def wkeys(tag, K, N):
        return [(tag, k, n0) for k in range(K) for n0 in range(0, N, 2048)]

    def wkey_for(tag, k, c0):
        return (tag, k, (c0 // 2048) * 2048)

    def small_load(dst, src, key):
        S.I("sp", "dma_start", out=dst, in_=src, writes=[key], dma_key=key)

    def rstd_chain(slot_i, ssq_ap, ssq_key, sq, rstd, inv_n, nm):
        sq_t, sq_k = sq(slot_i)
        r_t, r_k = rstd(slot_i)
        S.I("act", "activation", out=sq_t[:], in_=ssq_ap, func=AF.Sqrt, scale=inv_n, bias=epsT[:],
              reads=[ssq_key, "epsT"], writes=[sq_k])
        S.I("dve", "reciprocal", out=r_t[:], in_=sq_t[:], reads=[sq_k], writes=[r_k])
        return r_t, r_k

    with contextlib.ExitStack() as stk:
        def sb(name, shape, dt):
            return stk.enter_context(nc.sbuf_tensor("sb_" + name, list(shape), dt))

        def ps(name, shape, dt):
            return stk.enter_context(nc.psum_tensor("ps_" + name, list(shape), dt))

        epsT = sb("epsT", [128, 1], F32)
        ident = sb("ident", [128, 128], BF16)
        identf = sb("identf", [128, 128], F32)
        S.I("pool", "memset", epsT[:], EPS, writes=["epsT"])
        small_load(identf[:], ident_d[:, :], "identf")
        S.I("dve", "tensor_copy", out=ident[:], in_=identf[:], reads=["identf"], writes=["ident"])

        with contextlib.ExitStack() as stA:
            def sbA(name, shape, dt):
                return stA.enter_context(nc.sbuf_tensor("A1" + name, list(shape), dt))

            def psA(name, shape, dt):
                return stA.enter_context(nc.psum_tensor("A1p" + name, list(shape), dt))

            WinB = sbA("WinB", [128, 8, 6144], BF16)
            stage = Buf(sbA, "stage", [128, 2048], F32, 2)
            gpa = sbA("gpa", [128, 8], F32)
            t2t = sbA("t2t", [128, 8, 128], F32)
            dxi = sbA("dxi", [128, 8, 128], BF16)
            zs = sbA("zs", [128, 8], F32)
            xt = Buf(sbA, "xt", [128, D], F32, 2)
            junk = sbA("junk", [128, D], BF16)
            ssq = Buf(sbA, "ssq", [128, 1], F32, 2)
            sq = Buf(sbA, "sq", [128, 1], F32, 2)
            rstd = Buf(sbA, "rstd", [128, 1], F32, 2)
            hb = Buf(sbA, "hb", [128, D], BF16, 1)
            hT = Buf(sbA, "hT", [128, 8, 128], BF16, 2)
            cs = Buf(sbA, "cs", [128, 2, 64], F32, 2)
            rtmp = Buf(sbA, "rtmp", [128, 4, 4, 64], F32, 2)
            qr = Buf(sbA, "qr", [128, 8, 128], BF16, 1)
            kr = Buf(sbA, "kr", [128, 8, 128], BF16, 2)
            qT = Buf(sbA, "qT", [128, 8, 128], BF16, 2)
            kT = Buf(sbA, "kT", [128, 8, 128], BF16, 2)
            vt = Buf(sbA, "vt", [128, 8, 256], BF16, 2)
            sg = Buf(sbA, "sg", [128, 2048], BF16, 2)
            pS = Buf(sbA, "pS", [128, 8, 128], BF16, 1)
            stats = sbA("stats", [128, 8, 6], F32)
            mv = sbA("mv", [128, 8, 2], F32)
            gsq = sbA("gsq", [128, 8], F32)
            grs = sbA("grs", [128, 8], F32)
            gnb = sbA("gnb", [128, 8], F32)
            on = Buf(sbA, "on", [128, 2048], BF16, 1)
            zT = Buf(sbA, "zT", [128, 16, 128], BF16, 2)
            Rf = sbA("Rf", [128, 8, 256], F32)
            Rb = sbA("Rb", [128, 8, 256], BF16)

            PT = psA("PT", [128, 8, 128], F32)
            PP = Buf(psA, "PP", [128, 512], F32, 2)
            PS_ = psA("PS", [128, 4, 128], F32)
            PO = Buf(psA, "PO", [128, 2, 256], F32, 2)
            PU = psA("PU", [128, 2, 256], F32)

            small_load(gpa[:], g_pre_a[:, :], "gpa")
            S.last_w[("gsc", "WinB")] = S.last_w["gpa"]
            small_load(t2t[:], t2t_d[:, :, :], "t2t")
            small_load(zs[:], zs_d[:, :], "zs")
            st_t, st_k = stage(0)
            S.I("sp", "dma_start", out=st_t[:, 0:1024], in_=dxi_d.rearrange("p h c -> p (h c)"), writes=[st_k], dma_key=st_k)
            S.I("dve", "tensor_copy", out=dxi[:].rearrange("p h c -> p (h c)"), in_=st_t[:, 0:1024], reads=[st_k], writes=["dxi"])
            load_weight(sbA, psA, w_in_a, WinB, 8, 6144, gpa, stage, ["dve", "act"], "WinB")

            def A_load(t):
                xt_t, xt_k = xt(t)
                cs_t, cs_k = cs(t)
                S.I("sp", "dma_start", out=xt_t[:], in_=x[t * 128:(t + 1) * 128, :], writes=[xt_k], dma_key=xt_k)
                S.I("sp", "dma_start", out=cs_t[:], in_=csA[:, t, :, :], writes=[(cs_k, 0), (cs_k, 1)], dma_key=cs_k)

            def A_S1(t):
                xt_t, xt_k = xt(t)
                cs_t, cs_k = cs(t)
                ssq_t, ssq_k = ssq(t)
                S.I("act", "activation", out=junk[:], in_=xt_t[:], func=AF.Square, accum_out=ssq_t[:],
                      reads=[xt_k], writes=["junk", ssq_k])
                r_t, r_k = rstd_chain(t, ssq_t[:], ssq_k, sq, rstd, 1.0 / D, "x")
                hb_t, hb_k = hb(t)
                S.I("dve", "tensor_scalar", out=hb_t[:], in0=xt_t[:], scalar1=r_t[:], scalar2=None, op0=ALU.mult,
                      reads=[xt_k, r_k], writes=[hb_k])
                for k in range(8):
                    S.I("pe", "matmul", PT[:, k, :], lhsT=hb_t[:, k * 128:(k + 1) * 128], rhs=ident[:], start=True, stop=True,
                          reads=[hb_k, "ident"], writes=[("PT", k // 4)])
                hT_t, hT_k = hT(t)
                S.I("act", "activation", out=hT_t[:], in_=PT[:], func=AF.Copy,
                      reads=[("PT", 0), ("PT", 1)], writes=[hT_k])
                qr_t, qr_k = qr(t)
                kr_t, kr_k = kr(t)
                vt_t, vt_k = vt(t)
                sg_t, sg_k = sg(t)
                for cb in range(12):
                    pp_t, pp_k = PP(cb)
                    for k in range(8):
                        S.I("pe", "matmul", pp_t[:], lhsT=hT_t[:, k, :], rhs=WinB[:, k, cb * 512:(cb + 1) * 512], start=(k == 0), stop=(k == 7),
                              reads=[hT_k, wkey_for("WinB", k, cb * 512)], writes=[pp_k])
                    if cb < 4:
                        dst_t, dst_k = (qr_t, qr_k) if cb < 2 else (kr_t, kr_k)
                        hh = (cb % 2) * 4
                        p3 = pp_t[:].rearrange("p (h d) -> p h d", h=4)
                        x1 = p3[:, :, 0:64]
                        x2 = p3[:, :, 64:128]
                        cosb = cs_t[:, 0, :].unsqueeze(1).to_broadcast([128, 4, 64])
                        sinb = cs_t[:, 1, :].unsqueeze(1).to_broadcast([128, 4, 64])
                        tm_t, tm_k = rtmp(cb)
                        S.I("dve", "tensor_tensor", out=tm_t[:, 0], in0=x1, in1=cosb, op=ALU.mult,
                              reads=[pp_k, (cs_k, 0)], writes=[(tm_k, 0)])
                        S.I("dve", "tensor_tensor", out=tm_t[:, 1], in0=x2, in1=sinb, op=ALU.mult,
                              reads=[pp_k, (cs_k, 1)], writes=[(tm_k, 1)])
                        S.I("dve", "tensor_tensor", out=tm_t[:, 2], in0=x1, in1=sinb, op=ALU.mult,
                              reads=[pp_k, (cs_k, 1)], writes=[(tm_k, 2)])
                        S.I("dve", "tensor_tensor", out=tm_t[:, 3], in0=x2, in1=cosb, op=ALU.mult,
                              reads=[pp_k, (cs_k, 0)], writes=[(tm_k, 3)])
                        S.I("pool", "tensor_tensor", out=dst_t[:, hh:hh + 4, 0:64], in0=tm_t[:, 0], in1=tm_t[:, 1], op=ALU.subtract,
                              reads=[(tm_k, 0), (tm_k, 1)], writes=[(dst_k, cb % 2, 0)])
                        S.I("pool", "tensor_tensor", out=dst_t[:, hh:hh + 4, 64:128], in0=tm_t[:, 2], in1=tm_t[:, 3], op=ALU.add,
                              reads=[(tm_k, 2), (tm_k, 3)], writes=[(dst_k, cb % 2, 1)])
                    elif cb < 8:
                        for j in range(2):
                            h = (cb - 4) * 2 + j
                            S.I("act", "activation", out=vt_t[:, h, :], in_=pp_t[:, j * 256:(j + 1) * 256], func=AF.Copy, scale=zs[:, h:h + 1],
                                  reads=[pp_k, "zs"], writes=[(vt_k, h)])
                    else:
                        c0 = (cb - 8) * 512
                        S.I("act", "activation", out=sg_t[:, c0:c0 + 512], in_=pp_t[:], func=AF.Silu,
                              reads=[pp_k], writes=[(sg_k, cb - 8)])
                for h in range(8):
                    S.I("pe", "matmul", PT[:, h, :], lhsT=qr_t[:, h, :], rhs=dxi[:, h, :], start=True, stop=True,
                          reads=[(qr_k, h // 4, 0), (qr_k, h // 4, 1), "dxi"], writes=[("PT", h // 4)])
                qT_t, qT_k = qT(t)
                S.I("dve", "tensor_copy", out=qT_t[:], in_=PT[:], reads=[("PT", 0), ("PT", 1)], writes=[qT_k])
                for h in range(8):
                    S.I("pe", "matmul", PT[:, h, :], lhsT=kr_t[:, h, :], rhs=ident[:], start=True, stop=True,
                          reads=[(kr_k, h // 4, 0), (kr_k, h // 4, 1), "ident"], writes=[("PT", h // 4)])
                kT_t, kT_k = kT(t)
                S.I("act", "activation", out=kT_t[:], in_=PT[:], func=AF.Copy, reads=[("PT", 0), ("PT", 1)], writes=[kT_k])

            def A_S2(t):
                qT_t, qT_k = qT(t)
                kT_t, kT_k = kT(t)
                kr_t, kr_k = kr(t)
                vt_t, vt_k = vt(t)
                sg_t, sg_k = sg(t)
                pS_t, pS_k = pS(t)
                on_t, on_k = on(t)
                for g in range(2):
                    for j in range(4):
                        h = 4 * g + j
                        S.I("pe", "matmul", PS_[:, j, :], lhsT=kT_t[:, h, :], rhs=qT_t[:, h, :], start=True, stop=True,
                              reads=[kT_k, qT_k], writes=["PS"])
                    S.I("dve", "tensor_tensor", out=pS_t[:, 4 * g:4 * g + 4, :], in0=PS_[:], in1=t2t[:, 4 * g:4 * g + 4, :], op=ALU.mult,
                          reads=["PS"] + ["t2t"], writes=[(pS_k, g)])
                for hp in range(4):
                    po_t, po_k = PO(hp)
                    for j in range(2):
                        h = 2 * hp + j
                        S.I("pe", "matmul", po_t[:, j, :], lhsT=pS_t[:, h, :], rhs=vt_t[:, h, :], start=True, stop=(t == 0),
                              reads=[(pS_k, h // 4), (vt_k, h)], writes=[po_k])
                        if t > 0:
                            S.I("pe", "matmul", po_t[:, j, :], lhsT=qT_t[:, h, :], rhs=Rb[:, h, :], start=False, stop=True,
                                  reads=[qT_k, ("Rb", hp)], writes=[po_k])
                    for j in range(2):
                        h = 2 * hp + j
                        S.I("dve", "bn_stats", out=stats[:, h, :], in_=po_t[:, j, :],
                              reads=[po_k], writes=[("stats", h)])
                        S.I("dve", "bn_aggr", out=mv[:, h, :], in_=stats[:, h, :],
                              reads=[("stats", h)], writes=[("mv", h)])
                    pr = slice(2 * hp, 2 * hp + 2)
                    S.I("act", "activation", out=gsq[:, pr], in_=mv[:, pr, 1], func=AF.Sqrt, bias=epsT[:],
                          reads=[("mv", 2 * hp), ("mv", 2 * hp + 1), "epsT"], writes=[("gsq", hp)])
                    S.I("dve", "reciprocal", out=grs[:, pr], in_=gsq[:, pr], reads=[("gsq", hp)], writes=[("grs", hp)])
                    S.I("dve", "scalar_tensor_tensor", out=gnb[:, pr], in0=mv[:, pr, 0], scalar=-1.0, in1=grs[:, pr], op0=ALU.mult, op1=ALU.mult,
                          reads=[("mv", 2 * hp), ("mv", 2 * hp + 1), ("grs", hp)], writes=[("gnb", hp)])
                    for j in range(2):
                        h = 2 * hp + j
                        S.I("act", "activation", out=on_t[:, h * 256:(h + 1) * 256], in_=po_t[:, j, :], func=AF.Identity, scale=grs[:, h:h + 1], bias=gnb[:, h:h + 1],
                              reads=[po_k, ("grs", hp), ("gnb", hp)], writes=[(on_k, h)])
                    c0 = hp * 512
                    S.I("pool", "tensor_tensor", out=on_t[:, c0:c0 + 512], in0=on_t[:, c0:c0 + 512], in1=sg_t[:, c0:c0 + 512], op=ALU.mult,
                          reads=[(on_k, 2 * hp), (on_k, 2 * hp + 1), (sg_k, hp)], writes=[(on_k, 2 * hp), (on_k, 2 * hp + 1)])
                    if t < NT - 1:
                        for j in range(2):
                            h = 2 * hp + j
                            S.I("pe", "matmul", PU[:, j, :], lhsT=kr_t[:, h, :], rhs=vt_t[:, h, :], start=True, stop=True,
                                  reads=[(kr_k, h // 4, 0), (kr_k, h // 4, 1), (vt_k, h)], writes=["PU"])
                        for j in range(2):
                            h = 2 * hp + j
                            if t == 0:
                                S.I("dve", "tensor_copy", out=Rf[:, h, :], in_=PU[:, j, :],
                                      reads=["PU"], writes=[("Rf", h)])
                            else:
                                S.I("dve", "scalar_tensor_tensor", out=Rf[:, h, :], in0=Rf[:, h, :], scalar=GC[h], in1=PU[:, j, :], op0=ALU.mult, op1=ALU.add,
                                      reads=["PU", ("Rf", h)], writes=[("Rf", h)])
                        S.I("pool", "tensor_copy", out=Rb[:, pr, :], in_=Rf[:, pr, :],
                              reads=[("Rf", 2 * hp), ("Rf", 2 * hp + 1)], writes=[("Rb", hp)])
                zT_t, zT_k = zT(t)
                for r in range(2):
                    for c in range(8):
                        cc = 8 * r + c
                        S.I("pe", "matmul", PT[:, c, :], lhsT=on_t[:, cc * 128:(cc + 1) * 128], rhs=ident[:], start=True, stop=True,
                              reads=[(on_k, cc // 2), "ident"], writes=[("PT", c // 4)])
                    S.I("act", "activation", out=zT_t[:, 8 * r:8 * r + 8, :], in_=PT[:], func=AF.Copy,
                          reads=[("PT", 0), ("PT", 1)], writes=[(zT_k, r)])
                S.I("sp", "dma_start", out=zTa[t], in_=zT_t[:], reads=[(zT_k, 0), (zT_k, 1)], writes=[("zTa", t)], dma_key=("zTst", t % 2))

            A_load(0)
            if NT > 1:
                A_load(1)
            A_S1(0)
            for t in range(NT):
                if t + 1 < NT:
                    A_S1(t + 1)
                if t + 2 < NT:
                    A_load(t + 2)
                A_S2(t)
        S.barrier()
        if STOP_AFTER == "A1":
            S.emit(final_wait_ops=[S.ops[-1]])
            return nc

        def outproj_phase(tagp, w_d, KC, gscale_d, gpost_d, load_z, resid, dst, nblk_sub):
            with contextlib.ExitStack() as stO:
                def sbO(name, shape, dt):
                    return stO.enter_context(nc.sbuf_tensor(tagp + name, list(shape), dt))

                def psO(name, shape, dt):
                    return stO.enter_context(nc.psum_tensor(tagp + "p" + name, list(shape), dt))
                WoB = sbO("WoB", [128, KC, D], BF16)
                stage = Buf(sbO, "stage", [128, 2048], F32, 2)
                gsc = None
                if gscale_d is not None:
                    gsc = sbO("gsc", [128, KC], F32)
                    small_load(gsc[:], gscale_d[:, :], tagp + "gsc")
                    S.last_w[("gsc", tagp + "WoB")] = S.last_w[tagp + "gsc"]
                gpo = sbO("gpo", [128, D], F32)
                small_load(gpo[:], gpost_d[:, :], tagp + "gpo")
                load_weight(sbO, psO, w_d, WoB, KC, D, gsc, stage, ["dve", "act"], tagp + "WoB")
                xr = Buf(sbO, "xr", [128, D], F32, 3)
                yf = Buf(sbO, "yf", [128, D], F32, 2)
                junk = sbO("junk", [128, 512], BF16)
                ssqy = Buf(sbO, "ssqy", [128, 2], F32, 2)
                ssq1 = Buf(sbO, "ssq1", [128, 1], F32, 2)
                sq = Buf(sbO, "sq", [128, 1], F32, 2)
                rstd = Buf(sbO, "rstd", [128, 1], F32, 2)
                PY = Buf(psO, "PY", [128, 2, 512], F32, 3)
                zload, zget = load_z(sbO)

                def loads(t):
                    zload(t)
                    xr_t, xr_k = xr(t)
                    S.I("sp", "dma_start", out=xr_t[:], in_=resid[t * 128:(t + 1) * 128, :], writes=[xr_k], dma_key=xr_k)
                for t in range(min(2, NT)):
                    loads(t)
                for t in range(NT):
                    if t + 2 < NT:
                        loads(t + 2)
                    lhs_of, zkeys = zget(t)
                    xr_t, xr_k = xr(t)
                    py_t, py_k = PY(t)
                    sy_t, sy_k = ssqy(t)
                    for half in range(2):
                        for c in range(KC):
                            S.I("pe", "matmul", py_t[:, half, :], lhsT=lhs_of(c), rhs=WoB[:, c, half * 512:(half + 1) * 512], start=(c == 0), stop=(c == KC - 1),
                                  reads=zkeys + [wkey_for(tagp + "WoB", c, half * 512)], writes=[(py_k, half)])
                        S.I("act", "activation", out=junk[:], in_=py_t[:, half, :], func=AF.Square, accum_out=sy_t[:, half:half + 1],
                              reads=[(py_k, half)], writes=[tagp + "junk", (sy_k, half)])
                    s1_t, s1_k = ssq1(t)
                    S.I("dve", "tensor_tensor", out=s1_t[:], in0=sy_t[:, 0:1], in1=sy_t[:, 1:2], op=ALU.add,
                          reads=[(sy_k, 0), (sy_k, 1)], writes=[s1_k])
                    r_t, r_k = rstd_chain(t, s1_t[:], s1_k, sq, rstd, 1.0 / D, tagp)
                    yf_t, yf_k = yf(t)
                    for half in range(2):
                        hs = slice(half * 512, (half + 1) * 512)
                        S.I("dve", "scalar_tensor_tensor", out=yf_t[:, hs], in0=py_t[:, half, :], scalar=r_t[:], in1=gpo[:, hs], op0=ALU.mult, op1=ALU.mult,
                              reads=[(py_k, half), r_k, tagp + "gpo"], writes=[(yf_k, half)])
                        S.I("pool", "tensor_tensor", out=yf_t[:, hs], in0=yf_t[:, hs], in1=xr_t[:, hs], op=ALU.add,
                              reads=[(yf_k, half), xr_k], writes=[(yf_k, half)])
                    o = S.I("sp", "dma_start", out=dst[t * 128:(t + 1) * 128, :], in_=yf_t[:],
                              reads=[(yf_k, 0), (yf_k, 1)], writes=[(tagp + "dst", t)], dma_key=(tagp + "yst", t % 2))
                    final_ops.append(o)
            S.barrier()

        final_ops = []

        def load_z_A(sbO):
            zT = Buf(sbO, "zTin", [128, 16, 128], BF16, 3)

            def load(t):
                z_t, z_k = zT(t)
                S.I("sp", "dma_start", out=z_t[:], in_=zTa[t], writes=[z_k], dma_key=z_k)

            def get(t):
                z_t, z_k = zT(t)
                return (lambda c: z_t[:, c, :]), [z_k]
            return load, get

        outproj_phase("A2", w_out_a, 16, gn_gain_a, g_post_a, load_z_A, x, (out if STOP_AFTER == "A2" else h1), 1)
        if STOP_AFTER == "A2":
            S.emit(final_wait_ops=list(final_ops))
            return nc
        final_ops.clear()

        with contextlib.ExitStack() as stB:
            def sbB(name, shape, dt):
                return stB.enter_context(nc.sbuf_tensor("B1" + name, list(shape), dt))

            def psB(name, shape, dt):
                return stB.enter_context(nc.psum_tensor("B1p" + name, list(shape), dt))
            stage = Buf(sbB, "stage", [128, 2048], F32, 2)
            WkvaB = sbB("WkvaB", [128, 8, 320], BF16)
            WukB = sbB("WukB", [128, 2, D], BF16)
            WuvB = sbB("WuvB", [128, 2, D], BF16)
            WinbB = sbB("WinbB", [128, 8, 1408], BF16)
            WuqB = sbB("WuqB", [128, 3, 1536], BF16)
            gkv = sbB("gkv", [128, 8], F32)
            gkl = sbB("gkl", [128, 2], F32)
            gpb = sbB("gpb", [128, 8], F32)
            gql = sbB("gql", [128, 3], F32)
            for nm, dst_, src_ in (("gkv", gkv, g_kv), ("gkl", gkl, g_kv_lat), ("gpb", gpb, g_pre_b), ("gql", gql, g_q_lat)):
                small_load(dst_[:], src_[:, :], "B1" + nm)
            S.last_w[("gsc", "WkvaB")] = S.last_w["B1gkv"]
            S.last_w[("gsc", "WukB")] = S.last_w["B1gkl"]
            S.last_w[("gsc", "WuvB")] = S.last_w["B1gkl"]
            S.last_w[("gsc", "WinbB")] = S.last_w["B1gpb"]
            S.last_w[("gsc", "WuqB")] = S.last_w["B1gql"]
            engs = ["dve", "act"]
            load_weight(sbB, psB, w_kv_a, WkvaB, 8, 320, gkv, stage, engs, "WkvaB")
            load_weight(sbB, psB, w_in_b, WinbB, 8, 1408, gpb, stage, engs, "WinbB")
            load_weight(sbB, psB, w_uk, WukB, 2, D, gkl, stage, engs, "WukB")
            load_weight(sbB, psB, w_uv, WuvB, 2, D, gkl, stage, engs, "WuvB")
            load_weight(sbB, psB, w_uq, WuqB, 3, 1536, gql, stage, engs, "WuqB")
            if STOP_AFTER == "B1w":
                S.emit(final_wait_ops=[S.ops[-1]])
                return nc
            if B1N is not None:
                S.limit = len(S.ops) + B1N
            if STOP_AFTER == "B1x":
                tst = sbB("tst", [128, D], F32)
                o = S.I("sp", "dma_start", out=tst[:], in_=h1[0:128, :], writes=["tst"], dma_key="tst")
                S.emit(final_wait_ops=[o])
                return nc
            if STOP_AFTER == "B1z":
                xt = Buf(sbB, "xt", [128, D], F32, 2)
                xt_t, xt_k = xt(0)
                o = S.I("sp", "dma_start", out=xt_t[:], in_=h1[0:128, :], writes=[xt_k], dma_key=xt_k)
                S.emit(final_wait_ops=[o])
                return nc
            if STOP_AFTER == "B1y":
                tst = sbB("tst", [128, D], F32)
                o = S.I("sp", "dma_start", out=tst[:], in_=x[0:128, :], writes=["tst"], dma_key="tst")
                S.emit(final_wait_ops=[o])
                return nc

            xt = Buf(sbB, "xt", [128, D], F32, 2)
            junk = sbB("junk", [128, D], BF16)
            ssq = Buf(sbB, "ssq", [128, 1], F32, 2)
            sq = Buf(sbB, "sq", [128, 1], F32, 2)
            rstd = Buf(sbB, "rstd", [128, 1], F32, 2)
            ssqc = Buf(sbB, "ssqc", [128, 1], F32, 2)
            sqc = Buf(sbB, "sqc", [128, 1], F32, 2)
            rstdc = Buf(sbB, "rstdc", [128, 1], F32, 2)
            ssqq = Buf(sbB, "ssqq", [128, 1], F32, 2)
            sqq = Buf(sbB, "sqq", [128, 1], F32, 2)
            rstdq = Buf(sbB, "rstdq", [128, 1], F32, 2)
            hb = Buf(sbB, "hb", [128, D], BF16, 2)
            hTb = Buf(sbB, "hTb", [128, 8, QB], BF16, 2)
            csb = Buf(sbB, "csb", [128, 2, 32], F32, 2)
            chat = Buf(sbB, "chat", [128, 256], BF16, 2)
            kro = Buf(sbB, "kro", [128, 2, 64], BF16, 2)
            ktmp = Buf(sbB, "ktmp", [128, 4, 32], F32, 2)
            chT = Buf(sbB, "chT", [128, 2, QB], BF16, 2)
            krTb = Buf(sbB, "krTb", [128, QB], BF16, 2)
            cqh = Buf(sbB, "cqh", [128, 384], BF16, 2)
            cqT = Buf(sbB, "cqT", [128, 3, QB], BF16, 2)
            knTs = Buf(sbB, "knTs", [128, 8, QB], BF16, 2)
            qnTs = Buf(sbB, "qnTs", [128, 8, QB], BF16, 2)
            sgTs = Buf(sbB, "sgTs", [128, 8, QB], BF16, 2)
            qrTs = Buf(sbB, "qrTs", [128, 4, QB], BF16, 2)
            vsb = Buf(sbB, "vsb", [128, D], BF16, 2)
            qtmp = Buf(sbB, "qtmp", [128, 4, 8, 32], F32, 2)
            qro = Buf(sbB, "qro", [128, 8, 64], BF16, 2)

            PT = psB("PT", [128, 8, 128], F32)
            PA = Buf(psB, "PA", [128, 512], F32, 2)
            PF = Buf(psB, "PF", [128, 512], F32, 2)
            PV = psB("PV", [128, 2, 512], F32)

            def B_load(t):
                xt_t, xt_k = xt(t)
                cs_t, cs_k = csb(t)
                S.I("sp", "dma_start", out=xt_t[:], in_=h1[t * 128:(t + 1) * 128, :], writes=[xt_k], dma_key=xt_k)
                S.I("sp", "dma_start", out=cs_t[:], in_=csB[:, t, :, :], writes=[(cs_k, 0), (cs_k, 1)], dma_key=cs_k)

            for bq in range(NQB):
                hTb_t, hTb_k = hTb(bq)
                chT_t, chT_k = chT(bq)
                krT_t, krT_k = krTb(bq)
                cqT_t, cqT_k = cqT(bq)
                qrT_t, qrT_k = qrTs(bq)
                for sub in range(SUB):
                    t = bq * SUB + sub
                    ts_ = slice(sub * 128, (sub + 1) * 128)
                    xt_t, xt_k = xt(t)
                    cs_t, cs_k = csb(t)
                    if t == 0:
                        B_load(0)
                    if t + 1 < NT:
                        B_load(t + 1)
                    ssq_t, ssq_k = ssq(t)
                    S.I("act", "activation", out=junk[:], in_=xt_t[:], func=AF.Square, accum_out=ssq_t[:],
                          reads=[xt_k], writes=["B1junk", ssq_k])
                    r_t, r_k = rstd_chain(t, ssq_t[:], ssq_k, sq, rstd, 1.0 / D, "b")
                    hb_t, hb_k = hb(t)
                    S.I("dve", "tensor_scalar", out=hb_t[:], in0=xt_t[:], scalar1=r_t[:], scalar2=None, op0=ALU.mult,
                          reads=[xt_k, r_k], writes=[hb_k])
                    for k in range(8):
                        S.I("pe", "matmul", PT[:, k, :], lhsT=hb_t[:, k * 128:(k + 1) * 128], rhs=ident[:], start=True, stop=True,
                              reads=[hb_k, "ident"], writes=[("PT", k // 4)])
                    S.I("act", "activation", out=hTb_t[:, :, ts_], in_=PT[:], func=AF.Copy,
                          reads=[("PT", 0), ("PT", 1)], writes=[(hTb_k, sub)])
                    if t == 0:
                        stop_here("B1a")
                    pa_t, pa_k = PA(2 * t)
                    for k in range(8):
                        S.I("pe", "matmul", pa_t[:, 0:320], lhsT=hTb_t[:, k, ts_], rhs=WkvaB[:, k, :], start=(k == 0), stop=(k == 7),
                              reads=[(hTb_k, sub), wkey_for("WkvaB", k, 0)], writes=[pa_k])
                    sc_t, sc_k = ssqc(t)
                    S.I("act", "activation", out=junk[:, 0:256], in_=pa_t[:, 0:256], func=AF.Square, accum_out=sc_t[:],
                          reads=[pa_k], writes=["B1junk", sc_k])
                    rc_t, rc_k = rstd_chain(t, sc_t[:], sc_k, sqc, rstdc, 1.0 / 256, "c")
                    ch_t, ch_k = chat(t)
                    S.I("dve", "tensor_scalar", out=ch_t[:], in0=pa_t[:, 0:256], scalar1=rc_t[:], scalar2=None, op0=ALU.mult,
                          reads=[pa_k, rc_k], writes=[ch_k])
                    if t == 0:
                        stop_here("B1b")
                    kt_t, kt_k = ktmp(t)
                    ko_t, ko_k = kro(t)
                    x1 = pa_t[:, 256:288]
                    x2 = pa_t[:, 288:320]
                    S.I("dve", "tensor_tensor", out=kt_t[:, 0, :], in0=x1, in1=cs_t[:, 0, :], op=ALU.mult, reads=[pa_k, (cs_k, 0)], writes=[(kt_k, 0)])
                    S.I("dve", "tensor_tensor", out=kt_t[:, 1, :], in0=x2, in1=cs_t[:, 1, :], op=ALU.mult, reads=[pa_k, (cs_k, 1)], writes=[(kt_k, 1)])
                    S.I("dve", "tensor_tensor", out=kt_t[:, 2, :], in0=x1, in1=cs_t[:, 1, :], op=ALU.mult, reads=[pa_k, (cs_k, 1)], writes=[(kt_k, 2)])
                    S.I("dve", "tensor_tensor", out=kt_t[:, 3, :], in0=x2, in1=cs_t[:, 0, :], op=ALU.mult, reads=[pa_k, (cs_k, 0)], writes=[(kt_k, 3)])
                    S.I("pool", "tensor_tensor", out=ko_t[:, 0, 0:32], in0=kt_t[:, 0, :], in1=kt_t[:, 1, :], op=ALU.subtract, reads=[(kt_k, 0), (kt_k, 1)], writes=[(ko_k, 0)])
                    S.I("pool", "tensor_tensor", out=ko_t[:, 0, 32:64], in0=kt_t[:, 2, :], in1=kt_t[:, 3, :], op=ALU.add, reads=[(kt_k, 2), (kt_k, 3)], writes=[(ko_k, 1)])
                    S.I("pool", "tensor_copy", out=ko_t[:, 1, :], in_=ko_t[:, 0, :], reads=[(ko_k, 0), (ko_k, 1)], writes=[(ko_k, 2)])
                    if t == 0:
                        stop_here("B1c")
                    for c in range(2):
                        S.I("pe", "matmul", PT[:, c, :], lhsT=ch_t[:, c * 128:(c + 1) * 128], rhs=ident[:], start=True, stop=True,
                              reads=[ch_k, "ident"], writes=[("PT", c // 4)])
                    S.I("pe", "matmul", PT[:, 2, :], lhsT=ko_t[:].rearrange("p a b -> p (a b)"), rhs=ident[:], start=True, stop=True,
                          reads=[(ko_k, 0), (ko_k, 1), (ko_k, 2), "ident"], writes=[("PT", 0)])
                    S.I("act", "activation", out=chT_t[:, :, ts_], in_=PT[:, 0:2, :], func=AF.Copy,
                          reads=[("PT", 0)], writes=[(chT_k, sub)])
                    S.I("dve", "tensor_copy", out=krT_t[:, ts_], in_=PT[:, 2, :],
                          reads=[("PT", 0)], writes=[(krT_k, sub)])
                    if t == 0:
                        stop_here("B1d")
                    pq_t, pq_k = PA(2 * t + 1)
                    for k in range(8):
                        S.I("pe", "matmul", pq_t[:, 0:384], lhsT=hTb_t[:, k, ts_], rhs=WinbB[:, k, 0:384], start=(k == 0), stop=(k == 7),
                              reads=[(hTb_k, sub), wkey_for("WinbB", k, 0)], writes=[pq_k])
                    sq_t2, sq_k2 = ssqq(t)
                    S.I("act", "activation", out=junk[:, 0:384], in_=pq_t[:, 0:384], func=AF.Square, accum_out=sq_t2[:],
                          reads=[pq_k], writes=["B1junk", sq_k2])
                    rq_t, rq_k = rstd_chain(t, sq_t2[:], sq_k2, sqq, rstdq, 1.0 / 384, "q")
                    cq_t, cq_k = cqh(t)
                    S.I("dve", "tensor_scalar", out=cq_t[:], in0=pq_t[:, 0:384], scalar1=rq_t[:], scalar2=None, op0=ALU.mult,
                          reads=[pq_k, rq_k], writes=[cq_k])
                    for c in range(3):
                        S.I("pe", "matmul", PT[:, 4 + c, :], lhsT=cq_t[:, c * 128:(c + 1) * 128], rhs=ident[:], start=True, stop=True,
                              reads=[cq_k, "ident"], writes=[("PT", 1)])
                    S.I("act", "activation", out=cqT_t[:, :, ts_], in_=PT[:, 4:7, :], func=AF.Copy,
                          reads=[("PT", 1)], writes=[(cqT_k, sub)])
                    if t == 0:
                        stop_here("B1e")
                    for half in range(2):
                        for c in range(2):
                            S.I("pe", "matmul", PV[:, half, :], lhsT=chT_t[:, c, ts_], rhs=WuvB[:, c, half * 512:(half + 1) * 512], start=(c == 0), stop=(c == 1),
                                  reads=[(chT_k, sub), wkey_for("WuvB", c, half * 512)], writes=[("PV", half)])
                    v_t, v_k = vsb(t)
                    S.I("act", "activation", out=v_t[:], in_=PV[:].rearrange("p a b -> p (a b)"), func=AF.Copy,
                          reads=[("PV", 0), ("PV", 1)], writes=[v_k])
                    S.I("sp", "dma_start", out=vS[t], in_=v_t[:], reads=[v_k], writes=[("vS", t)], dma_key=("vst", t % 2))
                    if 'q' not in SKIP:
                        pr_t, pr_k = PF(2 * t)
                        wq_r = WuqB[:].rearrange("p c (h d) -> p c h d", h=8)
                        for c in range(3):
                            S.I("pe", "matmul", pr_t[:].rearrange("p (h d) -> p h d", h=8), lhsT=cqT_t[:, c, ts_], rhs=wq_r[:, c, :, 128:192], start=(c == 0), stop=(c == 2),
                                  reads=[(cqT_k, sub), wkey_for("WuqB", c, 0)], writes=[pr_k])
                        p3 = pr_t[:].rearrange("p (h d) -> p h d", h=8)
                        q1 = p3[:, :, 0:32]
                        q2 = p3[:, :, 32:64]
                        cb_ = cs_t[:, 0, :].unsqueeze(1).to_broadcast([128, 8, 32])
                        sb_ = cs_t[:, 1, :].unsqueeze(1).to_broadcast([128, 8, 32])
                        qt_t, qt_k = qtmp(t)
                        qo_t, qo_k = qro(t)
                        S.I("dve", "tensor_tensor", out=qt_t[:, 0], in0=q1, in1=cb_, op=ALU.mult, reads=[pr_k, (cs_k, 0)], writes=[(qt_k, 0)])
                        S.I("dve", "tensor_tensor", out=qt_t[:, 1], in0=q2, in1=sb_, op=ALU.mult, reads=[pr_k, (cs_k, 1)], writes=[(qt_k, 1)])
                        S.I("dve", "tensor_tensor", out=qt_t[:, 2], in0=q1, in1=sb_, op=ALU.mult, reads=[pr_k, (cs_k, 1)], writes=[(qt_k, 2)])
                        S.I("dve", "tensor_tensor", out=qt_t[:, 3], in0=q2, in1=cb_, op=ALU.mult, reads=[pr_k, (cs_k, 0)], writes=[(qt_k, 3)])
                        S.I("pool", "tensor_tensor", out=qo_t[:, :, 0:32], in0=qt_t[:, 0], in1=qt_t[:, 1], op=ALU.subtract, reads=[(qt_k, 0), (qt_k, 1)], writes=[(qo_k, 0)])
                        S.I("pool", "tensor_tensor", out=qo_t[:, :, 32:64], in0=qt_t[:, 2], in1=qt_t[:, 3], op=ALU.add, reads=[(qt_k, 2), (qt_k, 3)], writes=[(qo_k, 1)])
                        for pr_i in range(4):
                            S.I("pe", "matmul", PT[:, pr_i, :], lhsT=qo_t[:, 2 * pr_i:2 * pr_i + 2, :].rearrange("p a b -> p (a b)"), rhs=ident[:], start=True, stop=True,
                                  reads=[(qo_k, 0), (qo_k, 1), "ident"], writes=[("PT", 0)])
                        S.I("act", "activation", out=qrT_t[:, :, ts_], in_=PT[:, 0:4, :], func=AF.Copy,
                              reads=[("PT", 0)], writes=[(qrT_k, sub)])
                if 'f' not in SKIP:
                    allsub = lambda key: [(key, s_) for s_ in range(SUB)]
                    kn_t, kn_k = knTs(bq)
                    qn_t, qn_k = qnTs(bq)
                    sgT_t, sgT_k = sgTs(bq)
                    fi = 0
                    for h in range(8):
                        pf_t, pf_k = PF(fi); fi += 1
                        for c in range(2):
                            S.I("pe", "matmul", pf_t[:, 0:QB], lhsT=WukB[:, c, h * 128:(h + 1) * 128], rhs=chT_t[:, c, :], start=(c == 0), stop=(c == 1),
                                  reads=allsub(chT_k) + [wkey_for("WukB", c, h * 128)], writes=[pf_k])
                        S.I("dve", "tensor_copy", out=kn_t[:, h, :], in_=pf_t[:, 0:QB], reads=[pf_k], writes=[(kn_k, h)])
                    for h in range(8):
                        pf_t, pf_k = PF(fi); fi += 1
                        for c in range(3):
                            S.I("pe", "matmul", pf_t[:, 0:QB], lhsT=WuqB[:, c, h * 192:h * 192 + 128], rhs=cqT_t[:, c, :], start=(c == 0), stop=(c == 2),
                                  reads=allsub(cqT_k) + [wkey_for("WuqB", c, h * 192), wkey_for("WuqB", c, h * 192 + 127)], writes=[pf_k])
                        S.I("dve", "tensor_copy", out=qn_t[:, h, :], in_=pf_t[:, 0:QB], reads=[pf_k], writes=[(qn_k, h)])
                    for h in range(8):
                        pf_t, pf_k = PF(fi); fi += 1
                        for k in range(8):
                            S.I("pe", "matmul", pf_t[:, 0:QB], lhsT=WinbB[:, k, 384 + h * 128:384 + (h + 1) * 128], rhs=hTb_t[:, k, :], start=(k == 0), stop=(k == 7),
                                  reads=allsub(hTb_k) + [wkey_for("WinbB", k, 384 + h * 128), wkey_for("WinbB", k, 384 + h * 128 + 127)], writes=[pf_k])
                        S.I("act", "activation", out=sgT_t[:, h, :], in_=pf_t[:, 0:QB], func=AF.Silu, reads=[pf_k], writes=[(sgT_k, h)])
                if 's' not in SKIP:
                    bs = slice(bq * QB, (bq + 1) * QB)
                    S.I("sp", "dma_start", out=knT[:, :, bs].rearrange("h p t -> p h t"), in_=kn_t[:],
                          reads=[(kn_k, h) for h in range(8)], writes=[("knT", bq)], dma_key=("knst", bq % 2))
                    S.I("sp", "dma_start", out=qnT[:, :, bs].rearrange("h p t -> p h t"), in_=qn_t[:],
                          reads=[(qn_k, h) for h in range(8)], writes=[("qnT", bq)], dma_key=("qnst", bq % 2))
                    S.I("sp", "dma_start", out=sgT[:, :, bs].rearrange("h p t -> p h t"), in_=sgT_t[:],
                          reads=[(sgT_k, h) for h in range(8)], writes=[("sgT", bq)], dma_key=("sgst", bq % 2))
                    S.I("sp", "dma_start", out=qrT[:, :, bs].rearrange("h p t -> p h t"), in_=qrT_t[:],
                          reads=allsub(qrT_k), writes=[("qrT", bq)], dma_key=("qrst", bq % 2))
                    S.I("sp", "dma_start", out=krT[:, bs], in_=krT_t[:],
                          reads=allsub(krT_k), writes=[("krT", bq)], dma_key=("krst", bq % 2))
        S.barrier()

        if STOP_AFTER == "B1":
            S.emit(final_wait_ops=[S.ops[-1]])
            return nc
        SCALE = float((128 + 64) ** -0.5)
        with contextlib.ExitStack() as stC:
            def sbC(name, shape, dt):
                return stC.enter_context(nc.sbuf_tensor("B2" + name, list(shape), dt))

            def psC(name, shape, dt):
                return stC.enter_context(nc.psum_tensor("B2p" + name, list(shape), dt))
            ones = sbC("ones", [128, 128], BF16)
            S.I("pool", "memset", ones[:], 1.0, writes=["ones"])
            krS = sbC("krS", [128, S_len], BF16)
            S.I("sp", "dma_start", out=krS[:], in_=krT[:, :], writes=["krS"], dma_key="krS")
            knS = Buf(sbC, "knS", [128, S_len], BF16, 2)
            qnS = Buf(sbC, "qnS", [128, S_len], BF16, 2)
            sgS = Buf(sbC, "sgS", [128, S_len], BF16, 2)
            qrS = Buf(sbC, "qrS", [128, S_len], BF16, 2)
            v2S = Buf(sbC, "v2S", [128, NT, 256], BF16, 2)
            pTb = Buf(sbC, "pTb", [128, QB], BF16, 3)
            rden = Buf(sbC, "rden", [128, QB], F32, 2)
            ob = Buf(sbC, "ob", [128, QB], F32, 2)
            zb = Buf(sbC, "zb", [128, QB], BF16, 2)
            PSc = Buf(psC, "PSc", [128, 512], F32, 3)
            POc = Buf(psC, "POc", [128, 512], F32, 2)
            PDc = Buf(psC, "PDc", [128, 512], F32, 2)
            heads = {}

            def load_head(h):
                hp = h // 2
                kn_t, kn_k = knS(h)
                qn_t, qn_k = qnS(h)
                sg_t, sg_k = sgS(h)
                S.I("sp", "dma_start", out=kn_t[:], in_=knT[h], writes=[kn_k], dma_key=kn_k)
                S.I("sp", "dma_start", out=qn_t[:], in_=qnT[h], writes=[qn_k], dma_key=qn_k)
                S.I("sp", "dma_start", out=sg_t[:], in_=sgT[h], writes=[sg_k], dma_key=sg_k)
                qr_t, qr_k = qrS(hp)
                v2_t, v2_k = v2S(hp)
                if h % 2 == 0:
                    S.I("sp", "dma_start", out=qr_t[:], in_=qrT[hp], writes=[qr_k], dma_key=qr_k)
                    S.I("sp", "dma_start", out=v2_t[:], in_=vS[:, :, hp * 256:(hp + 1) * 256].rearrange("t p d -> p t d"), writes=[v2_k], dma_key=v2_k)
                heads[h] = (kn_t, kn_k, qn_t, qn_k, sg_t, sg_k, qr_t, qr_k, v2_t, v2_k)

            items = []
            blk = 0
            for h in range(8):
                for qb in range(NQB):
                    nkt = SUB * qb + SUB
                    for kt in range(nkt):
                        items.append((h, qb, kt, nkt, blk))
                    blk += 1

            def geom(i):
                h, qb, kt, nkt, blk_ = items[i]
                j = kt - SUB * qb
                c0 = 128 * j if j > 0 else 0
                return h, qb, kt, nkt, blk_, j, c0

            def qk(i):
                h, qb, kt, nkt, blk_, j, c0 = geom(i)
                kn_t, kn_k, qn_t, qn_k, sg_t, sg_k, qr_t, qr_k, v2_t, v2_k = heads[h]
                pb = 64 * (h % 2)
                qs = slice(qb * QB + c0, (qb + 1) * QB)
                ks = slice(kt * 128, (kt + 1) * 128)
                ps_t, ps_k = PSc(i)
                pt_t, pt_k = pTb(i)
                S.I("pe", "matmul", ps_t[:, c0:QB], lhsT=kn_t[:, ks], rhs=qn_t[:, qs], start=True, stop=False,
                    reads=[kn_k, qn_k], writes=[ps_k])
                S.I("pe", "matmul", ps_t[:, c0:QB], lhsT=krS[pb:pb + 64, ks], rhs=qr_t[pb:pb + 64, qs], start=False, stop=True,
                    reads=["krS", qr_k], writes=[ps_k])
                S.I("act", "activation", out=pt_t[:, c0:QB], in_=ps_t[:, c0:QB], func=AF.Exp, scale=SCALE,
                    reads=[ps_k], writes=[pt_k])
                if j >= 0:
                    S.I("pool", "memset", pt_t[64:128, c0:c0 + 64], 0.0, reads=[], writes=[pt_k])

            def pv(i):
                h, qb, kt, nkt, blk_, j, c0 = geom(i)
                kn_t, kn_k, qn_t, qn_k, sg_t, sg_k, qr_t, qr_k, v2_t, v2_k = heads[h]
                pt_t, pt_k = pTb(i)
                po_t, po_k = POc(blk_)
                pd_t, pd_k = PDc(blk_)
                S.I("pe", "matmul", po_t[:, c0:QB], lhsT=v2_t[:, kt, (h % 2) * 128:(h % 2) * 128 + 128], rhs=pt_t[:, c0:QB], start=(kt == 0), stop=(kt == nkt - 1),
                    reads=[v2_k, pt_k], writes=[po_k])
                S.I("pe", "matmul", pd_t[:, c0:QB], lhsT=ones[:], rhs=pt_t[:, c0:QB], start=(kt == 0), stop=(kt == nkt - 1),
                    reads=["ones", pt_k], writes=[pd_k])
                if kt == nkt - 1:
                    rd_t, rd_k = rden(blk_)
                    ob_t, ob_k = ob(blk_)
                    zb_t, zb_k = zb(blk_)
                    S.I("dve", "reciprocal", out=rd_t[:], in_=pd_t[:, 0:QB], reads=[pd_k], writes=[rd_k])
                    S.I("dve", "tensor_tensor", out=ob_t[:], in0=po_t[:, 0:QB], in1=rd_t[:], op=ALU.mult, reads=[po_k, rd_k], writes=[ob_k])
                    S.I("pool", "tensor_tensor", out=zb_t[:], in0=ob_t[:], in1=sg_t[:, qb * QB:(qb + 1) * QB], op=ALU.mult, reads=[ob_k, sg_k], writes=[zb_k])
                    S.I("sp", "dma_start", out=zTb[qb, :, h, :], in_=zb_t[:], reads=[zb_k], writes=[("zTb", qb, h)], dma_key=("zbst", blk_ % 2))
                    if qb == NQB - 1 and h + 2 < 8:
                        load_head(h + 2)

            load_head(0)
            load_head(1)
            LOOK = 2
            for i in range(min(LOOK, len(items))):
                qk(i)
            for i in range(len(items)):
                if i + LOOK < len(items):
                    qk(i + LOOK)
                pv(i)
        S.barrier()

        if STOP_AFTER == "B2":
            S.emit(final_wait_ops=[S.ops[-1]])
            return nc
        def load_z_B(sbO):
            zT = Buf(sbO, "zTin", [128, 8, QB], BF16, 3)
            done = set()

            def load(t):
                bq = t // SUB
                for b_ in (bq, bq + 1):
                    if b_ < NQB and b_ not in done:
                        done.add(b_)
                        z_t, z_k = zT(b_)
                        S.I("sp", "dma_start", out=z_t[:], in_=zTb[b_], writes=[z_k], dma_key=z_k)

            def get(t):
                bq, sub = divmod(t, SUB)
                z_t, z_k = zT(bq)
                return (lambda c: z_t[:, c, sub * 128:(sub + 1) * 128]), [z_k]
            return load, get

        outproj_phase("B3", w_out_b, 8, None, g_post_b, load_z_B, h1, out, 1)
        S.emit(final_wait_ops=list(final_ops))
    return nc


def _chunkT(w, K):
    n = w.shape[1]
    return np.ascontiguousarray(w.reshape(K, 128, n).transpose(1, 0, 2))


def _vecT(g, K):
    return np.ascontiguousarray(g.reshape(K, 128).T)


def _rope_tables(S_len, half):
    inv = (np.float32(10000.0) ** (-np.arange(half, dtype=np.float32) / np.float32(half))).astype(np.float32)
    ang = (np.arange(S_len, dtype=np.float32)[:, None] * inv[None, :]).astype(np.float32)
    c = np.cos(ang).astype(np.float32)
    s = np.sin(ang).astype(np.float32)
    nt = S_len // 128
    lay = lambda a: np.ascontiguousarray(a.reshape(nt, 128, half).transpose(1, 0, 2))
    return lay(c), lay(s)


def _decay_tables():
    h = np.arange(NH, dtype=np.float64)
    lg = np.log(1.0 - np.exp2(-5.0 - h))
    i = np.arange(128, dtype=np.float64)
    xi = np.exp(lg[:, None] * (i + 1.0)[None])
    zeta = np.exp(lg[:, None] * (127.0 - i)[None])
    ch = (np.arange(128) // 64)
    c_ = i[:, None]
    m_ = i[None, :]
    same = ch[:, None] == ch[None, :]
    prev = ch[None, :] < ch[:, None]
    Dm = np.zeros((NH, 128, 128))
    for hh in range(NH):
        Dm[hh] = np.where(same, np.exp(lg[hh] * np.abs(c_ - m_)), np.where(prev, np.exp(lg[hh] * (c_ - m_)), 0.0))
    T2 = Dm / (xi[:, :, None] * zeta[:, None, :])
    t2t = np.ascontiguousarray(T2.transpose(2, 0, 1)).astype(np.float32)
    dxi = np.zeros((128, NH, 128), np.float32)
    for hh in range(NH):
        dxi[np.arange(128), hh, np.arange(128)] = xi[hh]
    zs = np.ascontiguousarray((zeta * (128.0 ** -0.5)).T).astype(np.float32)
    return t2t, dxi, zs


def _prep_shared(inp, S_len):
    f = lambda a: np.asarray(a, dtype=np.float32)
    cA, sA = _rope_tables(S_len, 64)
    cB, sB = _rope_tables(S_len, 32)
    t2t, dxi, zs = _decay_tables()
    return {
        "w_in_a": _chunkT(f(inp["w_in_a"])[0], 8),
        "g_pre_a": _vecT(f(inp["g_pre_a"])[0], 8),
        "gn_gain_a": _vecT(f(inp["gn_gain_a"])[0], 16),
        "w_out_a": _chunkT(f(inp["w_out_a"])[0], 16),
        "g_post_a": np.ascontiguousarray(np.broadcast_to(f(inp["g_post_a"])[0][None, :], (128, D))),
        "g_kv": _vecT(f(inp["g_kv"]), 8),
        "w_kv_a": _chunkT(f(inp["w_kv_a"]), 8),
        "g_kv_lat": _vecT(f(inp["g_kv_lat"]), 2),
        "w_uk": _chunkT(f(inp["w_uk"]), 2),
        "w_uv": _chunkT(f(inp["w_uv"]), 2),
        "g_pre_b": _vecT(f(inp["g_pre_b"])[0], 8),
        "w_in_b": _chunkT(f(inp["w_in_b"])[0], 8),
        "g_q_lat": _vecT(f(inp["g_q_lat"])[0], 3),
        "w_uq": _chunkT(f(inp["w_uq"])[0], 3),
        "w_out_b": _chunkT(f(inp["w_out_b"])[0], 8),
        "g_post_b": np.ascontiguousarray(np.broadcast_to(f(inp["g_post_b"])[0][None, :], (128, D))),
        "csA": np.ascontiguousarray(np.stack([cA, sA], axis=2)), "csB": np.ascontiguousarray(np.stack([cB, sB], axis=2)),
        "t2t": t2t, "dxi": dxi, "zs": zs,
        "ident": np.eye(128, dtype=np.float32),
    }


def kernel(**inputs):
    x = np.asarray(inputs["x"], dtype=np.float32)
    B, S_len, _ = x.shape
    shared = _prep_shared(inputs, S_len)
    nc = build(S_len)
    in_maps = []
    for b in range(B):
        m = dict(shared)
        m["x"] = np.ascontiguousarray(x[b])
        in_maps.append(m)
    res = run_bass_kernel_spmd(nc, in_maps, core_ids=list(range(B)))
    return np.stack([np.asarray(r["out"], dtype=np.float32) for r in res.results], axis=0)
```

```python
import contextlib
import numpy as np
import concourse.bass as bass
import concourse.mybir as mybir
from concourse.bass_utils import run_bass_kernel_spmd

F32 = mybir.dt.float32
BF16 = mybir.dt.bfloat16
ALU = mybir.AluOpType
AF = mybir.ActivationFunctionType

D = 1024
EPS = 1e-6
NH = 8
SEQ = 4096
STOP_AFTER = None
SKIP = ''
B1N = None


class _Stop(Exception):
    pass


class Op:
    __slots__ = ("eng", "fn", "deps", "idx", "signal", "dma_key", "cnt")

    def __init__(self, eng, fn, deps, dma_key):
        self.eng = eng
        self.fn = fn
        self.deps = deps
        self.dma_key = dma_key
        self.signal = False
        self.cnt = None


class Sched:
    COMPUTE = ("pe", "dve", "act", "pool")

    def __init__(self, nc):
        self.nc = nc
        self.ops = []
        self.last_w = {}
        self.readers = {}
        self.bar = []
        self.last_eng = {}
        self.last_key = {}

    PSUM_ROOTS = {"PT", "PS", "PO", "PU", "PP", "PY", "PV", "PA", "PF", "PSc", "POc", "PDc"}

    @classmethod
    def _excl(cls, key):
        while not isinstance(key, str):
            key = key[0]
        return key in cls.PSUM_ROOTS

    def add(self, eng, fn, reads=(), writes=(), dma_key=None):
        ex = [r for r in reads if self._excl(r)]
        if ex:
            reads = [r for r in reads if not self._excl(r)]
            writes = list(writes) + [r for r in ex if r not in writes]
        deps = list(self.bar)
        for r in reads:
            w = self.last_w.get(r)
            if w is not None:
                deps.append(w)
        for r in writes:
            w = self.last_w.get(r)
            if w is not None:
                deps.append(w)
            deps.extend(self.readers.get(r, ()))
        if getattr(self, "limit", None) is not None and len(self.ops) >= self.limit:
            raise _Stop()
        op = Op(eng, fn, deps, dma_key)
        op.idx = len(self.ops)
        self.ops.append(op)
        for r in reads:
            self.readers.setdefault(r, []).append(op)
        for r in writes:
            self.last_w[r] = op
            self.readers[r] = []
        if dma_key is None:
            self.last_eng[eng] = op
        else:
            self.last_key[dma_key] = op
        return op

    def I(self, eng, meth, *args, reads=(), writes=(), dma_key=None, **kw):
        return self.add(eng, lambda e: getattr(e, meth)(*args, **kw), reads=reads, writes=writes, dma_key=dma_key)

    def barrier(self):
        self.bar = list(self.last_eng.values()) + list(self.last_key.values())
        self.last_w = {}
        self.readers = {}

    @staticmethod
    def _pe_pe(d, op):
        return d.dma_key is None and op.dma_key is None and d.eng == "pe" and op.eng == "pe"

    def emit(self, final_wait_ops=()):
        nc = self.nc
        for op in self.ops:
            for d in op.deps:
                if not self._pe_pe(d, op):
                    d.signal = True
        for op in final_wait_ops:
            op.signal = True
        for op in self.ops:
            if op.dma_key is not None:
                op.signal = True
        eng_cnt = {e: 0 for e in self.COMPUTE}
        key_cnt = {}
        for op in self.ops:
            if not op.signal:
                continue
            if op.dma_key is not None:
                key_cnt[op.dma_key] = key_cnt.get(op.dma_key, 0) + 1
                op.cnt = key_cnt[op.dma_key] * 16
            else:
                eng_cnt[op.eng] += 1
                op.cnt = eng_cnt[op.eng]
        sems = {}
        with contextlib.ExitStack() as st:
            for e in self.COMPUTE:
                sems[("eng", e)] = st.enter_context(nc.semaphore("s_" + e))
            for i, k in enumerate(sorted(key_cnt, key=str)):
                sems[("dma", k)] = st.enter_context(nc.semaphore("d%d" % i))
            block = st.enter_context(nc.Block())
            queues = {}
            for op in self.ops:
                queues.setdefault(op.eng, []).append(op)
            engmap = {"pe": ("tensor", nc.tensor), "dve": ("vector", nc.vector),
                      "act": ("scalar", nc.scalar), "pool": ("gpsimd", nc.gpsimd),
                      "sp": ("sync", nc.sync)}

            def semof(op):
                if op.dma_key is not None:
                    return sems[("dma", op.dma_key)]
                return sems[("eng", op.eng)]

            def run_queue(eng, ops, final):
                known = {}
                for op in ops:
                    need = {}
                    for d in op.deps:
                        if d.cnt is None or self._pe_pe(d, op):
                            continue
                        s = semof(d)
                        key = id(s)
                        if known.get(key, 0) >= d.cnt:
                            continue
                        if key not in need or need[key][1] < d.cnt:
                            need[key] = (s, d.cnt)
                    for key, (s, c) in need.items():
                        eng.wait_ge(s, c)
                        known[key] = c
                    ins = op.fn(eng)
                    if op.signal:
                        ins.then_inc(semof(op), 16 if op.dma_key is not None else 1)
                for op in final:
                    eng.wait_ge(semof(op), op.cnt)

            for ename, (attr, eng) in engmap.items():
                ops = queues.get(ename, [])
                final = list(final_wait_ops) if ename == "sp" else []
                if not ops and not final:
                    continue

                def body(e, _ops=ops, _final=final):
                    run_queue(e, _ops, _final)
                getattr(block, attr)(body)


class Buf:
    def __init__(self, alloc, name, shape, dt, nbuf=1):
        self.t = [alloc("%s_%d" % (name, i), shape, dt) for i in range(nbuf)]
        self.name = name
        self.n = nbuf

    def __call__(self, i=0):
        j = i % self.n
        return self.t[j], (self.name, j)


def build(S_len=SEQ):
    ctx = {}
    try:
        return _build(S_len, ctx)
    except _Stop:
        return ctx["nc"]


def _build(S_len, ctx):
    NT = S_len // 128
    QB = min(512, S_len)
    NQB = S_len // QB
    SUB = QB // 128
    nc = bass.Bass("TRN2", target_bir_lowering=False)

    def din(name, shape, dt=F32):
        return nc.dram_tensor(name, list(shape), dt, kind="ExternalInput").ap()

    def dscr(name, shape, dt):
        return nc.dram_tensor(name, list(shape), dt, kind="Internal").ap()

    x = din("x", [S_len, D])
    w_in_a = din("w_in_a", [128, 8, 6144])
    g_pre_a = din("g_pre_a", [128, 8])
    gn_gain_a = din("gn_gain_a", [128, 16])
    w_out_a = din("w_out_a", [128, 16, D])
    g_post_a = din("g_post_a", [128, D])
    g_kv = din("g_kv", [128, 8])
    w_kv_a = din("w_kv_a", [128, 8, 320])
    g_kv_lat = din("g_kv_lat", [128, 2])
    w_uk = din("w_uk", [128, 2, D])
    w_uv = din("w_uv", [128, 2, D])
    g_pre_b = din("g_pre_b", [128, 8])
    w_in_b = din("w_in_b", [128, 8, 1408])
    g_q_lat = din("g_q_lat", [128, 3])
    w_uq = din("w_uq", [128, 3, 1536])
    w_out_b = din("w_out_b", [128, 8, D])
    g_post_b = din("g_post_b", [128, D])
    csA = din("csA", [128, NT, 2, 64])
    csB = din("csB", [128, NT, 2, 32])
    t2t_d = din("t2t", [128, 8, 128])
    dxi_d = din("dxi", [128, 8, 128])
    ident_d = din("ident", [128, 128])
    zs_d = din("zs", [128, 8])
    gc_host = None
    out = nc.dram_tensor("out", [S_len, D], F32, kind="ExternalOutput").ap()

    zTa = dscr("zTa", [NT, 128, 16, 128], BF16)
    h1 = dscr("h1", [S_len, D], F32)
    knT = dscr("knT", [8, 128, S_len], BF16)
    krT = dscr("krT", [128, S_len], BF16)
    vS = dscr("vS", [NT, 128, D], BF16)
    qnT = dscr("qnT", [8, 128, S_len], BF16)
    qrT = dscr("qrT", [4, 128, S_len], BF16)
    sgT = dscr("sgT", [8, 128, S_len], BF16)
    zTb = dscr("zTb", [NQB, 128, 8, QB], BF16)

    GC = [float((1.0 - 2.0 ** (-5.0 - h)) ** 128) for h in range(NH)]

    S = Sched(nc)
    ctx["S"] = S
    ctx["nc"] = nc

    def stop_here(tag):
        if STOP_AFTER == tag:
            last = [o for o in S.ops if o.dma_key is not None][-1]
            S.emit(final_wait_ops=[last, S.ops[-1]])
            raise _Stop()

    def load_weight(sb, ps_alloc, src, dst, K, N, scale, stage, eng_cycle, tag):
        i = 0
        for k in range(K):
            for n0 in range(0, N, 2048):
                n1 = min(N, n0 + 2048)
                st_t, st_k = stage(i)
                S.I("sp", "dma_start", out=st_t[:, 0:n1 - n0], in_=src[:, k, n0:n1],
                      writes=[st_k], dma_key=st_k)
                eng = eng_cycle[i % len(eng_cycle)]
                rd = [st_k] + ([("gsc", tag)] if scale is not None else [])
                wkey = (tag, k, n0)
                if scale is None:
                    if eng == "act":
                        S.I("act", "activation", out=dst[:, k, n0:n1], in_=st_t[:, 0:n1 - n0], func=AF.Copy,
                              reads=rd, writes=[wkey])
                    else:
                        S.I(eng, "tensor_copy", out=dst[:, k, n0:n1], in_=st_t[:, 0:n1 - n0],
                              reads=rd, writes=[wkey])
                else:
                    if eng == "act":
                        S.I("act", "activation", out=dst[:, k, n0:n1], in_=st_t[:, 0:n1 - n0], func=AF.Copy, scale=scale[:, k:k + 1],
                              reads=rd, writes=[wkey])
                    else:
                        S.I(eng, "tensor_scalar", out=dst[:, k, n0:n1], in0=st_t[:, 0:n1 - n0], scalar1=scale[:, k:k + 1], scalar2=None, op0=ALU.mult,
                              reads=rd, writes=[wkey])
                i += 1

    def wkeys(tag, K, N):
        return [(tag, k, n0) for k in range(K) for n0 in range(0, N, 2048)]

    def wkey_for(tag, k, c0):
        return (tag, k, (c0 // 2048) * 2048)

    def small_load(dst, src, key):
        S.I("sp", "dma_start", out=dst, in_=src, writes=[key], dma_key=key)

    def rstd_chain(slot_i, ssq_ap, ssq_key, sq, rstd, inv_n, nm):
        sq_t, sq_k = sq(slot_i)
        r_t, r_k = rstd(slot_i)
        S.I("act", "activation", out=sq_t[:], in_=ssq_ap, func=AF.Sqrt, scale=inv_n, bias=epsT[:],
              reads=[ssq_key, "epsT"], writes=[sq_k])
        S.I("dve", "reciprocal", out=r_t[:], in_=sq_t[:], reads=[sq_k], writes=[r_k])
        return r_t, r_k

    with contextlib.ExitStack() as stk:
        def sb(name, shape, dt):
            return stk.enter_context(nc.sbuf_tensor("sb_" + name, list(shape), dt))

        def ps(name, shape, dt):
            return stk.enter_context(nc.psum_tensor("ps_" + name, list(shape), dt))

        epsT = sb("epsT", [128, 1], F32)
        ident = sb("ident", [128, 128], BF16)
        identf = sb("identf", [128, 128], F32)
        S.I("pool", "memset", epsT[:], EPS, writes=["epsT"])
        small_load(identf[:], ident_d[:, :], "identf")
        S.I("dve", "tensor_copy", out=ident[:], in_=identf[:], reads=["identf"], writes=["ident"])

        with contextlib.ExitStack() as stA:
            def sbA(name, shape, dt):
                return stA.enter_context(nc.sbuf_tensor("A1" + name, list(shape), dt))

            def psA(name, shape, dt):
                return stA.enter_context(nc.psum_tensor("A1p" + name, list(shape), dt))

            WinB = sbA("WinB", [128, 8, 6144], BF16)
            stage = Buf(sbA, "stage", [128, 2048], F32, 2)
            gpa = sbA("gpa", [128, 8], F32)
            t2t = sbA("t2t", [128, 8, 128], F32)
            dxi = sbA("dxi", [128, 8, 128], BF16)
            zs = sbA("zs", [128, 8], F32)
            xt = Buf(sbA, "xt", [128, D], F32, 2)
            junk = sbA("junk", [128, D], BF16)
            ssq = Buf(sbA, "ssq", [128, 1], F32, 2)
            sq = Buf(sbA, "sq", [128, 1], F32, 2)
            rstd = Buf(sbA, "rstd", [128, 1], F32, 2)
            hb = Buf(sbA, "hb", [128, D], BF16, 1)
            hT = Buf(sbA, "hT", [128, 8, 128], BF16, 2)
            cs = Buf(sbA, "cs", [128, 2, 64], F32, 2)
            rtmp = Buf(sbA, "rtmp", [128, 4, 4, 64], F32, 2)
            qr = Buf(sbA, "qr", [128, 8, 128], BF16, 1)
            kr = Buf(sbA, "kr", [128, 8, 128], BF16, 2)
            qT = Buf(sbA, "qT", [128, 8, 128], BF16, 2)
            kT = Buf(sbA, "kT", [128, 8, 128], BF16, 2)
            vt = Buf(sbA, "vt", [128, 8, 256], BF16, 2)
            sg = Buf(sbA, "sg", [128, 2048], BF16, 2)
            pS = Buf(sbA, "pS", [128, 8, 128], BF16, 1)
            stats = sbA("stats", [128, 8, 6], F32)
            mv = sbA("mv", [128, 8, 2], F32)
            gsq = sbA("gsq", [128, 8], F32)
            grs = sbA("grs", [128, 8], F32)
            gnb = sbA("gnb", [128, 8], F32)
            on = Buf(sbA, "on", [128, 2048], BF16, 1)
            zT = Buf(sbA, "zT", [128, 16, 128], BF16, 2)
            Rf = sbA("Rf", [128, 8, 256], F32)
            Rb = sbA("Rb", [128, 8, 256], BF16)

            PT = psA("PT", [128, 8, 128], F32)
            PP = Buf(psA, "PP", [128, 512], F32, 2)
            PS_ = psA("PS", [128, 4, 128], F32)
            PO = Buf(psA, "PO", [128, 2, 256], F32, 2)
            PU = psA("PU", [128, 2, 256], F32)

            small_load(gpa[:], g_pre_a[:, :], "gpa")
            S.last_w[("gsc", "WinB")] = S.last_w["gpa"]
            small_load(t2t[:], t2t_d[:, :, :], "t2t")
            small_load(zs[:], zs_d[:, :], "zs")
            st_t, st_k = stage(0)
            S.I("sp", "dma_start", out=st_t[:, 0:1024], in_=dxi_d.rearrange("p h c -> p (h c)"), writes=[st_k], dma_key=st_k)
            S.I("dve", "tensor_copy", out=dxi[:].rearrange("p h c -> p (h c)"), in_=st_t[:, 0:1024], reads=[st_k], writes=["dxi"])
            load_weight(sbA, psA, w_in_a, WinB, 8, 6144, gpa, stage, ["dve", "act"], "WinB")

            def A_load(t):
                xt_t, xt_k = xt(t)
                cs_t, cs_k = cs(t)
                S.I("sp", "dma_start", out=xt_t[:], in_=x[t * 128:(t + 1) * 128, :], writes=[xt_k], dma_key=xt_k)
                S.I("sp", "dma_start", out=cs_t[:], in_=csA[:, t, :, :], writes=[(cs_k, 0), (cs_k, 1)], dma_key=cs_k)

            def A_S1(t):
                xt_t, xt_k = xt(t)
                cs_t, cs_k = cs(t)
                ssq_t, ssq_k = ssq(t)
                S.I("act", "activation", out=junk[:], in_=xt_t[:], func=AF.Square, accum_out=ssq_t[:],
                      reads=[xt_k], writes=["junk", ssq_k])
                r_t, r_k = rstd_chain(t, ssq_t[:], ssq_k, sq, rstd, 1.0 / D, "x")
                hb_t, hb_k = hb(t)
                S.I("dve", "tensor_scalar", out=hb_t[:], in0=xt_t[:], scalar1=r_t[:], scalar2=None, op0=ALU.mult,
                      reads=[xt_k, r_k], writes=[hb_k])
                for k in range(8):
                    S.I("pe", "matmul", PT[:, k, :], lhsT=hb_t[:, k * 128:(k + 1) * 128], rhs=ident[:], start=True, stop=True,
                          reads=[hb_k, "ident"], writes=[("PT", k // 4)])
                hT_t, hT_k = hT(t)
                S.I("act", "activation", out=hT_t[:], in_=PT[:], func=AF.Copy,
                      reads=[("PT", 0), ("PT", 1)], writes=[hT_k])
                qr_t, qr_k = qr(t)
                kr_t, kr_k = kr(t)
                vt_t, vt_k = vt(t)
                sg_t, sg_k = sg(t)
                for cb in range(12):
                    pp_t, pp_k = PP(cb)
                    for k in range(8):
                        S.I("pe", "matmul", pp_t[:], lhsT=hT_t[:, k, :], rhs=WinB[:, k, cb * 512:(cb + 1) * 512], start=(k == 0), stop=(k == 7),
                              reads=[hT_k, wkey_for("WinB", k, cb * 512)], writes=[pp_k])
                    if cb < 4:
                        dst_t, dst_k = (qr_t, qr_k) if cb < 2 else (kr_t, kr_k)
                        hh = (cb % 2) * 4
                        p3 = pp_t[:].rearrange("p (h d) -> p h d", h=4)
                        x1 = p3[:, :, 0:64]
                        x2 = p3[:, :, 64:128]
                        cosb = cs_t[:, 0, :].unsqueeze(1).to_broadcast([128, 4, 64])
                        sinb = cs_t[:, 1, :].unsqueeze(1).to_broadcast([128, 4, 64])
                        tm_t, tm_k = rtmp(cb)
                        S.I("dve", "tensor_tensor", out=tm_t[:, 0], in0=x1, in1=cosb, op=ALU.mult,
                              reads=[pp_k, (cs_k, 0)], writes=[(tm_k, 0)])
                        S.I("dve", "tensor_tensor", out=tm_t[:, 1], in0=x2, in1=sinb, op=ALU.mult,
                              reads=[pp_k, (cs_k, 1)], writes=[(tm_k, 1)])
                        S.I("dve", "tensor_tensor", out=tm_t[:, 2], in0=x1, in1=sinb, op=ALU.mult,
                              reads=[pp_k, (cs_k, 1)], writes=[(tm_k, 2)])
                        S.I("dve", "tensor_tensor", out=tm_t[:, 3], in0=x2, in1=cosb, op=ALU.mult,
                              reads=[pp_k, (cs_k, 0)], writes=[(tm_k, 3)])
                        S.I("pool", "tensor_tensor", out=dst_t[:, hh:hh + 4, 0:64], in0=tm_t[:, 0], in1=tm_t[:, 1], op=ALU.subtract,
                              reads=[(tm_k, 0), (tm_k, 1)], writes=[(dst_k, cb % 2, 0)])
                        S.I("pool", "tensor_tensor", out=dst_t[:, hh:hh + 4, 64:128], in0=tm_t[:, 2], in1=tm_t[:, 3], op=ALU.add,
                              reads=[(tm_k, 2), (tm_k, 3)], writes=[(dst_k, cb % 2, 1)])
                    elif cb < 8:
                        for j in range(2):
                            h = (cb - 4) * 2 + j
                            S.I("act", "activation", out=vt_t[:, h, :], in_=pp_t[:, j * 256:(j + 1) * 256], func=AF.Copy, scale=zs[:, h:h + 1],
                                  reads=[pp_k, "zs"], writes=[(vt_k, h)])
                    else:
                        c0 = (cb - 8) * 512
                        S.I("act", "activation", out=sg_t[:, c0:c0 + 512], in_=pp_t[:], func=AF.Silu,
                              reads=[pp_k], writes=[(sg_k, cb - 8)])
                for h in range(8):
                    S.I("pe", "matmul", PT[:, h, :], lhsT=qr_t[:, h, :], rhs=dxi[:, h, :], start=True, stop=True,
                          reads=[(qr_k, h // 4, 0), (qr_k, h // 4, 1), "dxi"], writes=[("PT", h // 4)])
                qT_t, qT_k = qT(t)
                S.I("dve", "tensor_copy", out=qT_t[:], in_=PT[:], reads=[("PT", 0), ("PT", 1)], writes=[qT_k])
                for h in range(8):
                    S.I("pe", "matmul", PT[:, h, :], lhsT=kr_t[:, h, :], rhs=ident[:], start=True, stop=True,
                          reads=[(kr_k, h // 4, 0), (kr_k, h // 4, 1), "ident"], writes=[("PT", h // 4)])
                kT_t, kT_k = kT(t)
                S.I("act", "activation", out=kT_t[:], in_=PT[:], func=AF.Copy, reads=[("PT", 0), ("PT", 1)], writes=[kT_k])

            def A_S2(t):
                qT_t, qT_k = qT(t)
                kT_t, kT_k = kT(t)
                kr_t, kr_k = kr(t)
                vt_t, vt_k = vt(t)
                sg_t, sg_k = sg(t)
                pS_t, pS_k = pS(t)
                on_t, on_k = on(t)
                for g in range(2):
                    for j in range(4):
                        h = 4 * g + j
                        S.I("pe", "matmul", PS_[:, j, :], lhsT=kT_t[:, h, :], rhs=qT_t[:, h, :], start=True, stop=True,
                              reads=[kT_k, qT_k], writes=["PS"])
                    S.I("dve", "tensor_tensor", out=pS_t[:, 4 * g:4 * g + 4, :], in0=PS_[:], in1=t2t[:, 4 * g:4 * g + 4, :], op=ALU.mult,
                          reads=["PS"] + ["t2t"], writes=[(pS_k, g)])
                for hp in range(4):
                    po_t, po_k = PO(hp)
                    for j in range(2):
                        h = 2 * hp + j
                        S.I("pe", "matmul", po_t[:, j, :], lhsT=pS_t[:, h, :], rhs=vt_t[:, h, :], start=True, stop=(t == 0),
                              reads=[(pS_k, h // 4), (vt_k, h)], writes=[po_k])
                        if t > 0:
                            S.I("pe", "matmul", po_t[:, j, :], lhsT=qT_t[:, h, :], rhs=Rb[:, h, :], start=False, stop=True,
                                  reads=[qT_k, ("Rb", hp)], writes=[po_k])
                    for j in range(2):
                        h = 2 * hp + j
                        S.I("dve", "bn_stats", out=stats[:, h, :], in_=po_t[:, j, :],
                              reads=[po_k], writes=[("stats", h)])
                        S.I("dve", "bn_aggr", out=mv[:, h, :], in_=stats[:, h, :],
                              reads=[("stats", h)], writes=[("mv", h)])
                    pr = slice(2 * hp, 2 * hp + 2)
                    S.I("act", "activation", out=gsq[:, pr], in_=mv[:, pr, 1], func=AF.Sqrt, bias=epsT[:],
                          reads=[("mv", 2 * hp), ("mv", 2 * hp + 1), "epsT"], writes=[("gsq", hp)])
                    S.I("dve", "reciprocal", out=grs[:, pr], in_=gsq[:, pr], reads=[("gsq", hp)], writes=[("grs", hp)])
                    S.I("dve", "scalar_tensor_tensor", out=gnb[:, pr], in0=mv[:, pr, 0], scalar=-1.0, in1=grs[:, pr], op0=ALU.mult, op1=ALU.mult,
                          reads=[("mv", 2 * hp), ("mv", 2 * hp + 1), ("grs", hp)], writes=[("gnb", hp)])
                    for j in range(2):
                        h = 2 * hp + j
                        S.I("act", "activation", out=on_t[:, h * 256:(h + 1) * 256], in_=po_t[:, j, :], func=AF.Identity, scale=grs[:, h:h + 1], bias=gnb[:, h:h + 1],
                              reads=[po_k, ("grs", hp), ("gnb", hp)], writes=[(on_k, h)])
                    c0 = hp * 512
                    S.I("pool", "tensor_tensor", out=on_t[:, c0:c0 + 512], in0=on_t[:, c0:c0 + 512], in1=sg_t[:, c0:c0 + 512], op=ALU.mult,
                          reads=[(on_k, 2 * hp), (on_k, 2 * hp + 1), (sg_k, hp)], writes=[(on_k, 2 * hp), (on_k, 2 * hp + 1)])
                    if t < NT - 1:
                        for j in range(2):
                            h = 2 * hp + j
                            S.I("pe", "matmul", PU[:, j, :], lhsT=kr_t[:, h, :], rhs=vt_t[:, h, :], start=True, stop=True,
                                  reads=[(kr_k, h // 4, 0), (kr_k, h // 4, 1), (vt_k, h)], writes=["PU"])
                        for j in range(2):
                            h = 2 * hp + j
                            if t == 0:
                                S.I("dve", "tensor_copy", out=Rf[:, h, :], in_=PU[:, j, :],
                                      reads=["PU"], writes=[("Rf", h)])
                            else:
                                S.I("dve", "scalar_tensor_tensor", out=Rf[:, h, :], in0=Rf[:, h, :], scalar=GC[h], in1=PU[:, j, :], op0=ALU.mult, op1=ALU.add,
                                      reads=["PU", ("Rf", h)], writes=[("Rf", h)])
                        S.I("pool", "tensor_copy", out=Rb[:, pr, :], in_=Rf[:, pr, :],
                              reads=[("Rf", 2 * hp), ("Rf", 2 * hp + 1)], writes=[("Rb", hp)])
                zT_t, zT_k = zT(t)
                for r in range(2):
                    for c in range(8):
                        cc = 8 * r + c
                        S.I("pe", "matmul", PT[:, c, :], lhsT=on_t[:, cc * 128:(cc + 1) * 128], rhs=ident[:], start=True, stop=True,
                              reads=[(on_k, cc // 2), "ident"], writes=[("PT", c // 4)])
                    S.I("act", "activation", out=zT_t[:, 8 * r:8 * r + 8, :], in_=PT[:], func=AF.Copy,
                          reads=[("PT", 0), ("PT", 1)], writes=[(zT_k, r)])
                S.I("sp", "dma_start", out=zTa[t], in_=zT_t[:], reads=[(zT_k, 0), (zT_k, 1)], writes=[("zTa", t)], dma_key=("zTst", t % 2))

            A_load(0)
            if NT > 1:
                A_load(1)
            A_S1(0)
            for t in range(NT):
                if t + 1 < NT:
                    A_S1(t + 1)
                if t + 2 < NT:
                    A_load(t + 2)
                A_S2(t)
        S.barrier()
        if STOP_AFTER == "A1":
            S.emit(final_wait_ops=[S.ops[-1]])
            return nc

        def outproj_phase(tagp, w_d, KC, gscale_d, gpost_d, load_z, resid, dst, nblk_sub):
            with contextlib.ExitStack() as stO:
                def sbO(name, shape, dt):
                    return stO.enter_context(nc.sbuf_tensor(tagp + name, list(shape), dt))

                def psO(name, shape, dt):
                    return stO.enter_context(nc.psum_tensor(tagp + "p" + name, list(shape), dt))
                WoB = sbO("WoB", [128, KC, D], BF16)
                stage = Buf(sbO, "stage", [128, 2048], F32, 3)
                gsc = None
                if gscale_d is not None:
                    gsc = sbO("gsc", [128, KC], F32)
                    small_load(gsc[:], gscale_d[:, :], tagp + "gsc")
                    S.last_w[("gsc", tagp + "WoB")] = S.last_w[tagp + "gsc"]
                gpo = sbO("gpo", [128, D], F32)
                small_load(gpo[:], gpost_d[:, :], tagp + "gpo")
                load_weight(sbO, psO, w_d, WoB, KC, D, gsc, stage, ["dve", "act"], tagp + "WoB")
                xr = Buf(sbO, "xr", [128, D], F32, 3)
                yf = Buf(sbO, "yf", [128, D], F32, 2)
                junk = sbO("junk", [128, 512], BF16)
                ssqy = Buf(sbO, "ssqy", [128, 2], F32, 2)
                ssq1 = Buf(sbO, "ssq1", [128, 1], F32, 2)
                sq = Buf(sbO, "sq", [128, 1], F32, 2)
                rstd = Buf(sbO, "rstd", [128, 1], F32, 2)
                PY = Buf(psO, "PY", [128, 2, 512], F32, 3)
                zload, zget = load_z(sbO)

                def loads(t):
                    zload(t)
                    xr_t, xr_k = xr(t)
                    S.I("sp", "dma_start", out=xr_t[:], in_=resid[t * 128:(t + 1) * 128, :], writes=[xr_k], dma_key=xr_k)
                for t in range(min(2, NT)):
                    loads(t)
                for t in range(NT):
                    if t + 2 < NT:
                        loads(t + 2)
                    lhs_of, zkeys = zget(t)
                    xr_t, xr_k = xr(t)
                    py_t, py_k = PY(t)
                    sy_t, sy_k = ssqy(t)
                    for half in range(2):
                        for c in range(KC):
                            S.I("pe", "matmul", py_t[:, half, :], lhsT=lhs_of(c), rhs=WoB[:, c, half * 512:(half + 1) * 512], start=(c == 0), stop=(c == KC - 1),
                                  reads=zkeys + [wkey_for(tagp + "WoB", c, half * 512)], writes=[(py_k, half)])
                        S.I("act", "activation", out=junk[:], in_=py_t[:, half, :], func=AF.Square, accum_out=sy_t[:, half:half + 1],
                              reads=[(py_k, half)], writes=[tagp + "junk", (sy_k, half)])
                    s1_t, s1_k = ssq1(t)
                    S.I("dve", "tensor_tensor", out=s1_t[:], in0=sy_t[:, 0:1], in1=sy_t[:, 1:2], op=ALU.add,
                          reads=[(sy_k, 0), (sy_k, 1)], writes=[s1_k])
                    r_t, r_k = rstd_chain(t, s1_t[:], s1_k, sq, rstd, 1.0 / D, tagp)
                    yf_t, yf_k = yf(t)
                    for half in range(2):
                        hs = slice(half * 512, (half + 1) * 512)
                        S.I("dve", "scalar_tensor_tensor", out=yf_t[:, hs], in0=py_t[:, half, :], scalar=r_t[:], in1=gpo[:, hs], op0=ALU.mult, op1=ALU.mult,
                              reads=[(py_k, half), r_k, tagp + "gpo"], writes=[(yf_k, half)])
                        S.I("pool", "tensor_tensor", out=yf_t[:, hs], in0=yf_t[:, hs], in1=xr_t[:, hs], op=ALU.add,
                              reads=[(yf_k, half), xr_k], writes=[(yf_k, half)])
                    o = S.I("sp", "dma_start", out=dst[t * 128:(t + 1) * 128, :], in_=yf_t[:],
                              reads=[(yf_k, 0), (yf_k, 1)], writes=[(tagp + "dst", t)], dma_key=(tagp + "yst", t % 2))
                    final_ops.append(o)
            S.barrier()

        final_ops = []

        def load_z_A(sbO):
            zT = Buf(sbO, "zTin", [128, 16, 128], BF16, 3)

            def load(t):
                z_t, z_k = zT(t)
                S.I("sp", "dma_start", out=z_t[:], in_=zTa[t], writes=[z_k], dma_key=z_k)

            def get(t):
                z_t, z_k = zT(t)
                return (lambda c: z_t[:, c, :]), [z_k]
            return load, get

        outproj_phase("A2", w_out_a, 16, gn_gain_a, g_post_a, load_z_A, x, (out if STOP_AFTER == "A2" else h1), 1)
        if STOP_AFTER == "A2":
            S.emit(final_wait_ops=list(final_ops))
            return nc
        final_ops.clear()

        with contextlib.ExitStack() as stB:
            def sbB(name, shape, dt):
                return stB.enter_context(nc.sbuf_tensor("B1" + name, list(shape), dt))

            def psB(name, shape, dt):
                return stB.enter_context(nc.psum_tensor("B1p" + name, list(shape), dt))
            stage = Buf(sbB, "stage", [128, 2048], F32, 2)
            WkvaB = sbB("WkvaB", [128, 8, 320], BF16)
            WukB = sbB("WukB", [128, 2, D], BF16)
            WuvB = sbB("WuvB", [128, 2, D], BF16)
            WinbB = sbB("WinbB", [128, 8, 1408], BF16)
            WuqB = sbB("WuqB", [128, 3, 1536], BF16)
            gkv = sbB("gkv", [128, 8], F32)
            gkl = sbB("gkl", [128, 2], F32)
            gpb = sbB("gpb", [128, 8], F32)
            gql = sbB("gql", [128, 3], F32)
            for nm, dst_, src_ in (("gkv", gkv, g_kv), ("gkl", gkl, g_kv_lat), ("gpb", gpb, g_pre_b), ("gql", gql, g_q_lat)):
                small_load(dst_[:], src_[:, :], "B1" + nm)
            S.last_w[("gsc", "WkvaB")] = S.last_w["B1gkv"]
            S.last_w[("gsc", "WukB")] = S.last_w["B1gkl"]
            S.last_w[("gsc", "WuvB")] = S.last_w["B1gkl"]
            S.last_w[("gsc", "WinbB")] = S.last_w["B1gpb"]
            S.last_w[("gsc", "WuqB")] = S.last_w["B1gql"]
            engs = ["dve", "act"]
            load_weight(sbB, psB, w_kv_a, WkvaB, 8, 320, gkv, stage, engs, "WkvaB")
            load_weight(sbB, psB, w_in_b, WinbB, 8, 1408, gpb, stage, engs, "WinbB")
            load_weight(sbB, psB, w_uk, WukB, 2, D, gkl, stage, engs, "WukB")
            load_weight(sbB, psB, w_uv, WuvB, 2, D, gkl, stage, engs, "WuvB")
            load_weight(sbB, psB, w_uq, WuqB, 3, 1536, gql, stage, engs, "WuqB")
            if STOP_AFTER == "B1w":
                S.emit(final_wait_ops=[S.ops[-1]])
                return nc
            if B1N is not None:
                S.limit = len(S.ops) + B1N
            if STOP_AFTER == "B1x":
                tst = sbB("tst", [128, D], F32)
                o = S.I("sp", "dma_start", out=tst[:], in_=h1[0:128, :], writes=["tst"], dma_key="tst")
                S.emit(final_wait_ops=[o])
                return nc
            if STOP_AFTER == "B1z":
                xt = Buf(sbB, "xt", [128, D], F32, 2)
                xt_t, xt_k = xt(0)
                o = S.I("sp", "dma_start", out=xt_t[:], in_=h1[0:128, :], writes=[xt_k], dma_key=xt_k)
                S.emit(final_wait_ops=[o])
                return nc
            if STOP_AFTER == "B1y":
                tst = sbB("tst", [128, D], F32)
                o = S.I("sp", "dma_start", out=tst[:], in_=x[0:128, :], writes=["tst"], dma_key="tst")
                S.emit(final_wait_ops=[o])
                return nc

            xt = Buf(sbB, "xt", [128, D], F32, 2)
            junk = sbB("junk", [128, D], BF16)
            ssq = Buf(sbB, "ssq", [128, 1], F32, 2)
            sq = Buf(sbB, "sq", [128, 1], F32, 2)
            rstd = Buf(sbB, "rstd", [128, 1], F32, 2)
            ssqc = Buf(sbB, "ssqc", [128, 1], F32, 2)
            sqc = Buf(sbB, "sqc", [128, 1], F32, 2)
            rstdc = Buf(sbB, "rstdc", [128, 1], F32, 2)
            ssqq = Buf(sbB, "ssqq", [128, 1], F32, 2)
            sqq = Buf(sbB, "sqq", [128, 1], F32, 2)
            rstdq = Buf(sbB, "rstdq", [128, 1], F32, 2)
            hb = Buf(sbB, "hb", [128, D], BF16, 2)
            hTb = Buf(sbB, "hTb", [128, 8, QB], BF16, 2)
            csb = Buf(sbB, "csb", [128, 2, 32], F32, 2)
            chat = Buf(sbB, "chat", [128, 256], BF16, 2)
            kro = Buf(sbB, "kro", [128, 2, 64], BF16, 2)
            ktmp = Buf(sbB, "ktmp", [128, 4, 32], F32, 2)
            chT = Buf(sbB, "chT", [128, 2, QB], BF16, 2)
            krTb = Buf(sbB, "krTb", [128, QB], BF16, 2)
            cqh = Buf(sbB, "cqh", [128, 384], BF16, 2)
            cqT = Buf(sbB, "cqT", [128, 3, QB], BF16, 2)
            knTs = Buf(sbB, "knTs", [128, 8, QB], BF16, 2)
            qnTs = Buf(sbB, "qnTs", [128, 8, QB], BF16, 2)
            sgTs = Buf(sbB, "sgTs", [128, 8, QB], BF16, 2)
            qrTs = Buf(sbB, "qrTs", [128, 4, QB], BF16, 2)
            vsb = Buf(sbB, "vsb", [128, D], BF16, 2)
            qtmp = Buf(sbB, "qtmp", [128, 4, 8, 32], F32, 2)
            qro = Buf(sbB, "qro", [128, 8, 64], BF16, 2)

            PT = psB("PT", [128, 8, 128], F32)
            PA = Buf(psB, "PA", [128, 512], F32, 2)
            PF = Buf(psB, "PF", [128, 512], F32, 2)
            PV = psB("PV", [128, 2, 512], F32)

            def B_load(t):
                xt_t, xt_k = xt(t)
                cs_t, cs_k = csb(t)
                S.I("sp", "dma_start", out=xt_t[:], in_=h1[t * 128:(t + 1) * 128, :], writes=[xt_k], dma_key=xt_k)
                S.I("sp", "dma_start", out=cs_t[:], in_=csB[:, t, :, :], writes=[(cs_k, 0), (cs_k, 1)], dma_key=cs_k)

            for bq in range(NQB):
                hTb_t, hTb_k = hTb(bq)
                chT_t, chT_k = chT(bq)
                krT_t, krT_k = krTb(bq)
                cqT_t, cqT_k = cqT(bq)
                qrT_t, qrT_k = qrTs(bq)
                for sub in range(SUB):
                    t = bq * SUB + sub
                    ts_ = slice(sub * 128, (sub + 1) * 128)
                    xt_t, xt_k = xt(t)
                    cs_t, cs_k = csb(t)
                    if t == 0:
                        B_load(0)
                    if t + 1 < NT:
                        B_load(t + 1)
                    ssq_t, ssq_k = ssq(t)
                    S.I("act", "activation", out=junk[:], in_=xt_t[:], func=AF.Square, accum_out=ssq_t[:],
                          reads=[xt_k], writes=["B1junk", ssq_k])
                    r_t, r_k = rstd_chain(t, ssq_t[:], ssq_k, sq, rstd, 1.0 / D, "b")
                    hb_t, hb_k = hb(t)
                    S.I("dve", "tensor_scalar", out=hb_t[:], in0=xt_t[:], scalar1=r_t[:], scalar2=None, op0=ALU.mult,
                          reads=[xt_k, r_k], writes=[hb_k])
                    for k in range(8):
                        S.I("pe", "matmul", PT[:, k, :], lhsT=hb_t[:, k * 128:(k + 1) * 128], rhs=ident[:], start=True, stop=True,
                              reads=[hb_k, "ident"], writes=[("PT", k // 4)])
                    S.I("act", "activation", out=hTb_t[:, :, ts_], in_=PT[:], func=AF.Copy,
                          reads=[("PT", 0), ("PT", 1)], writes=[(hTb_k, sub)])
                    if t == 0:
                        stop_here("B1a")
                    pa_t, pa_k = PA(2 * t)
                    for k in range(8):
                        S.I("pe", "matmul", pa_t[:, 0:320], lhsT=hTb_t[:, k, ts_], rhs=WkvaB[:, k, :], start=(k == 0), stop=(k == 7),
                              reads=[(hTb_k, sub), wkey_for("WkvaB", k, 0)], writes=[pa_k])
                    sc_t, sc_k = ssqc(t)
                    S.I("act", "activation", out=junk[:, 0:256], in_=pa_t[:, 0:256], func=AF.Square, accum_out=sc_t[:],
                          reads=[pa_k], writes=["B1junk", sc_k])
                    rc_t, rc_k = rstd_chain(t, sc_t[:], sc_k, sqc, rstdc, 1.0 / 256, "c")
                    ch_t, ch_k = chat(t)
                    S.I("dve", "tensor_scalar", out=ch_t[:], in0=pa_t[:, 0:256], scalar1=rc_t[:], scalar2=None, op0=ALU.mult,
                          reads=[pa_k, rc_k], writes=[ch_k])
                    if t == 0:
                        stop_here("B1b")
                    kt_t, kt_k = ktmp(t)
                    ko_t, ko_k = kro(t)
                    x1 = pa_t[:, 256:288]
                    x2 = pa_t[:, 288:320]
                    S.I("dve", "tensor_tensor", out=kt_t[:, 0, :], in0=x1, in1=cs_t[:, 0, :], op=ALU.mult, reads=[pa_k, (cs_k, 0)], writes=[(kt_k, 0)])
                    S.I("dve", "tensor_tensor", out=kt_t[:, 1, :], in0=x2, in1=cs_t[:, 1, :], op=ALU.mult, reads=[pa_k, (cs_k, 1)], writes=[(kt_k, 1)])
                    S.I("dve", "tensor_tensor", out=kt_t[:, 2, :], in0=x1, in1=cs_t[:, 1, :], op=ALU.mult, reads=[pa_k, (cs_k, 1)], writes=[(kt_k, 2)])
                    S.I("dve", "tensor_tensor", out=kt_t[:, 3, :], in0=x2, in1=cs_t[:, 0, :], op=ALU.mult, reads=[pa_k, (cs_k, 0)], writes=[(kt_k, 3)])
                    S.I("pool", "tensor_tensor", out=ko_t[:, 0, 0:32], in0=kt_t[:, 0, :], in1=kt_t[:, 1, :], op=ALU.subtract, reads=[(kt_k, 0), (kt_k, 1)], writes=[(ko_k, 0)])
                    S.I("pool", "tensor_tensor", out=ko_t[:, 0, 32:64], in0=kt_t[:, 2, :], in1=kt_t[:, 3, :], op=ALU.add, reads=[(kt_k, 2), (kt_k, 3)], writes=[(ko_k, 1)])
                    S.I("pool", "tensor_copy", out=ko_t[:, 1, :], in_=ko_t[:, 0, :], reads=[(ko_k, 0), (ko_k, 1)], writes=[(ko_k, 2)])
                    if t == 0:
                        stop_here("B1c")
                    for c in range(2):
                        S.I("pe", "matmul", PT[:, c, :], lhsT=ch_t[:, c * 128:(c + 1) * 128], rhs=ident[:], start=True, stop=True,
                              reads=[ch_k, "ident"], writes=[("PT", c // 4)])
                    S.I("pe", "matmul", PT[:, 2, :], lhsT=ko_t[:].rearrange("p a b -> p (a b)"), rhs=ident[:], start=True, stop=True,
                          reads=[(ko_k, 0), (ko_k, 1), (ko_k, 2), "ident"], writes=[("PT", 0)])
                    S.I("act", "activation", out=chT_t[:, :, ts_], in_=PT[:, 0:2, :], func=AF.Copy,
                          reads=[("PT", 0)], writes=[(chT_k, sub)])
                    S.I("dve", "tensor_copy", out=krT_t[:, ts_], in_=PT[:, 2, :],
                          reads=[("PT", 0)], writes=[(krT_k, sub)])
                    if t == 0:
                        stop_here("B1d")
                    pq_t, pq_k = PA(2 * t + 1)
                    for k in range(8):
                        S.I("pe", "matmul", pq_t[:, 0:384], lhsT=hTb_t[:, k, ts_], rhs=WinbB[:, k, 0:384], start=(k == 0), stop=(k == 7),
                              reads=[(hTb_k, sub), wkey_for("WinbB", k, 0)], writes=[pq_k])
                    sq_t2, sq_k2 = ssqq(t)
                    S.I("act", "activation", out=junk[:, 0:384], in_=pq_t[:, 0:384], func=AF.Square, accum_out=sq_t2[:],
                          reads=[pq_k], writes=["B1junk", sq_k2])
                    rq_t, rq_k = rstd_chain(t, sq_t2[:], sq_k2, sqq, rstdq, 1.0 / 384, "q")
                    cq_t, cq_k = cqh(t)
                    S.I("dve", "tensor_scalar", out=cq_t[:], in0=pq_t[:, 0:384], scalar1=rq_t[:], scalar2=None, op0=ALU.mult,
                          reads=[pq_k, rq_k], writes=[cq_k])
                    for c in range(3):
                        S.I("pe", "matmul", PT[:, 4 + c, :], lhsT=cq_t[:, c * 128:(c + 1) * 128], rhs=ident[:], start=True, stop=True,
                              reads=[cq_k, "ident"], writes=[("PT", 1)])
                    S.I("act", "activation", out=cqT_t[:, :, ts_], in_=PT[:, 4:7, :], func=AF.Copy,
                          reads=[("PT", 1)], writes=[(cqT_k, sub)])
                    if t == 0:
                        stop_here("B1e")
                    for half in range(2):
                        for c in range(2):
                            S.I("pe", "matmul", PV[:, half, :], lhsT=chT_t[:, c, ts_], rhs=WuvB[:, c, half * 512:(half + 1) * 512], start=(c == 0), stop=(c == 1),
                                  reads=[(chT_k, sub), wkey_for("WuvB", c, half * 512)], writes=[("PV", half)])
                    v_t, v_k = vsb(t)
                    S.I("act", "activation", out=v_t[:], in_=PV[:].rearrange("p a b -> p (a b)"), func=AF.Copy,
                          reads=[("PV", 0), ("PV", 1)], writes=[v_k])
                    S.I("sp", "dma_start", out=vS[t], in_=v_t[:], reads=[v_k], writes=[("vS", t)], dma_key=("vst", t % 2))
                    if 'q' not in SKIP:
                        pr_t, pr_k = PF(2 * t)
                        wq_r = WuqB[:].rearrange("p c (h d) -> p c h d", h=8)
                        for c in range(3):
                            S.I("pe", "matmul", pr_t[:].rearrange("p (h d) -> p h d", h=8), lhsT=cqT_t[:, c, ts_], rhs=wq_r[:, c, :, 128:192], start=(c == 0), stop=(c == 2),
                                  reads=[(cqT_k, sub), wkey_for("WuqB", c, 0)], writes=[pr_k])
                        p3 = pr_t[:].rearrange("p (h d) -> p h d", h=8)
                        q1 = p3[:, :, 0:32]
                        q2 = p3[:, :, 32:64]
                        cb_ = cs_t[:, 0, :].unsqueeze(1).to_broadcast([128, 8, 32])
                        sb_ = cs_t[:, 1, :].unsqueeze(1).to_broadcast([128, 8, 32])
                        qt_t, qt_k = qtmp(t)
                        qo_t, qo_k = qro(t)
                        S.I("dve", "tensor_tensor", out=qt_t[:, 0], in0=q1, in1=cb_, op=ALU.mult, reads=[pr_k, (cs_k, 0)], writes=[(qt_k, 0)])
                        S.I("dve", "tensor_tensor", out=qt_t[:, 1], in0=q2, in1=sb_, op=ALU.mult, reads=[pr_k, (cs_k, 1)], writes=[(qt_k, 1)])
                        S.I("dve", "tensor_tensor", out=qt_t[:, 2], in0=q1, in1=sb_, op=ALU.mult, reads=[pr_k, (cs_k, 1)], writes=[(qt_k, 2)])
                        S.I("dve", "tensor_tensor", out=qt_t[:, 3], in0=q2, in1=cb_, op=ALU.mult, reads=[pr_k, (cs_k, 0)], writes=[(qt_k, 3)])
                        S.I("pool", "tensor_tensor", out=qo_t[:, :, 0:32], in0=qt_t[:, 0], in1=qt_t[:, 1], op=ALU.subtract, reads=[(qt_k, 0), (qt_k, 1)], writes=[(qo_k, 0)])
                        S.I("pool", "tensor_tensor", out=qo_t[:, :, 32:64], in0=qt_t[:, 2], in1=qt_t[:, 3], op=ALU.add, reads=[(qt_k, 2), (qt_k, 3)], writes=[(qo_k, 1)])
                        for pr_i in range(4):
                            S.I("pe", "matmul", PT[:, pr_i, :], lhsT=qo_t[:, 2 * pr_i:2 * pr_i + 2, :].rearrange("p a b -> p (a b)"), rhs=ident[:], start=True, stop=True,
                                  reads=[(qo_k, 0), (qo_k, 1), "ident"], writes=[("PT", 0)])
                        S.I("act", "activation", out=qrT_t[:, :, ts_], in_=PT[:, 0:4, :], func=AF.Copy,
                              reads=[("PT", 0)], writes=[(qrT_k, sub)])
                if 'f' not in SKIP:
                    allsub = lambda key: [(key, s_) for s_ in range(SUB)]
                    kn_t, kn_k = knTs(bq)
                    qn_t, qn_k = qnTs(bq)
                    sgT_t, sgT_k = sgTs(bq)
                    fi = 0
                    for h in range(8):
                        pf_t, pf_k = PF(fi); fi += 1
                        for c in range(2):
                            S.I("pe", "matmul", pf_t[:, 0:QB], lhsT=WukB[:, c, h * 128:(h + 1) * 128], rhs=chT_t[:, c, :], start=(c == 0), stop=(c == 1),
                                  reads=allsub(chT_k) + [wkey_for("WukB", c, h * 128)], writes=[pf_k])
                        S.I("dve", "tensor_copy", out=kn_t[:, h, :], in_=pf_t[:, 0:QB], reads=[pf_k], writes=[(kn_k, h)])
                    for h in range(8):
                        pf_t, pf_k = PF(fi); fi += 1
                        for c in range(3):
                            S.I("pe", "matmul", pf_t[:, 0:QB], lhsT=WuqB[:, c, h * 192:h * 192 + 128], rhs=cqT_t[:, c, :], start=(c == 0), stop=(c == 2),
                                  reads=allsub(cqT_k) + [wkey_for("WuqB", c, h * 192), wkey_for("WuqB", c, h * 192 + 127)], writes=[pf_k])
                        S.I("dve", "tensor_copy", out=qn_t[:, h, :], in_=pf_t[:, 0:QB], reads=[pf_k], writes=[(qn_k, h)])
                    for h in range(8):
                        pf_t, pf_k = PF(fi); fi += 1
                        for k in range(8):
                            S.I("pe", "matmul", pf_t[:, 0:QB], lhsT=WinbB[:, k, 384 + h * 128:384 + (h + 1) * 128], rhs=hTb_t[:, k, :], start=(k == 0), stop=(k == 7),
                                  reads=allsub(hTb_k) + [wkey_for("WinbB", k, 384 + h * 128), wkey_for("WinbB", k, 384 + h * 128 + 127)], writes=[pf_k])
                        S.I("act", "activation", out=sgT_t[:, h, :], in_=pf_t[:, 0:QB], func=AF.Silu, reads=[pf_k], writes=[(sgT_k, h)])
                if 's' not in SKIP:
                    bs = slice(bq * QB, (bq + 1) * QB)
                    S.I("sp", "dma_start", out=knT[:, :, bs].rearrange("h p t -> p h t"), in_=kn_t[:],
                          reads=[(kn_k, h) for h in range(8)], writes=[("knT", bq)], dma_key=("knst", bq % 2))
                    S.I("sp", "dma_start", out=qnT[:, :, bs].rearrange("h p t -> p h t"), in_=qn_t[:],
                          reads=[(qn_k, h) for h in range(8)], writes=[("qnT", bq)], dma_key=("qnst", bq % 2))
                    S.I("sp", "dma_start", out=sgT[:, :, bs].rearrange("h p t -> p h t"), in_=sgT_t[:],
                          reads=[(sgT_k, h) for h in range(8)], writes=[("sgT", bq)], dma_key=("sgst", bq % 2))
                    S.I("sp", "dma_start", out=qrT[:, :, bs].rearrange("h p t -> p h t"), in_=qrT_t[:],
                          reads=allsub(qrT_k), writes=[("qrT", bq)], dma_key=("qrst", bq % 2))
                    S.I("sp", "dma_start", out=krT[:, bs], in_=krT_t[:],
                          reads=allsub(krT_k), writes=[("krT", bq)], dma_key=("krst", bq % 2))
        S.barrier()

        if STOP_AFTER == "B1":
            S.emit(final_wait_ops=[S.ops[-1]])
            return nc
        SCALE = float((128 + 64) ** -0.5)
        with contextlib.ExitStack() as stC:
            def sbC(name, shape, dt):
                return stC.enter_context(nc.sbuf_tensor("B2" + name, list(shape), dt))

            def psC(name, shape, dt):
                return stC.enter_context(nc.psum_tensor("B2p" + name, list(shape), dt))
            ones = sbC("ones", [128, 128], BF16)
            S.I("pool", "memset", ones[:], 1.0, writes=["ones"])
            krS = sbC("krS", [128, S_len], BF16)
            S.I("sp", "dma_start", out=krS[:], in_=krT[:, :], writes=["krS"], dma_key="krS")
            knS = Buf(sbC, "knS", [128, S_len], BF16, 2)
            qnS = Buf(sbC, "qnS", [128, S_len], BF16, 2)
            sgS = Buf(sbC, "sgS", [128, S_len], BF16, 2)
            qrS = Buf(sbC, "qrS", [128, S_len], BF16, 2)
            v2S = Buf(sbC, "v2S", [128, NT, 256], BF16, 2)
            pTb = Buf(sbC, "pTb", [128, QB], BF16, 3)
            rden = Buf(sbC, "rden", [128, QB], F32, 2)
            ob = Buf(sbC, "ob", [128, QB], F32, 2)
            zb = Buf(sbC, "zb", [128, QB], BF16, 2)
            PSc = Buf(psC, "PSc", [128, 512], F32, 3)
            POc = Buf(psC, "POc", [128, 512], F32, 2)
            PDc = Buf(psC, "PDc", [128, 512], F32, 2)
            heads = {}

            def load_head(h):
                hp = h // 2
                kn_t, kn_k = knS(h)
                qn_t, qn_k = qnS(h)
                sg_t, sg_k = sgS(h)
                S.I("sp", "dma_start", out=kn_t[:], in_=knT[h], writes=[kn_k], dma_key=kn_k)
                S.I("sp", "dma_start", out=qn_t[:], in_=qnT[h], writes=[qn_k], dma_key=qn_k)
                S.I("sp", "dma_start", out=sg_t[:], in_=sgT[h], writes=[sg_k], dma_key=sg_k)
                qr_t, qr_k = qrS(hp)
                v2_t, v2_k = v2S(hp)
                if h % 2 == 0:
                    S.I("sp", "dma_start", out=qr_t[:], in_=qrT[hp], writes=[qr_k], dma_key=qr_k)
                    S.I("sp", "dma_start", out=v2_t[:], in_=vS[:, :, hp * 256:(hp + 1) * 256].rearrange("t p d -> p t d"), writes=[v2_k], dma_key=v2_k)
                heads[h] = (kn_t, kn_k, qn_t, qn_k, sg_t, sg_k, qr_t, qr_k, v2_t, v2_k)

            items = []
            blk = 0
            for h in range(8):
                for qb in range(NQB):
                    nkt = SUB * qb + SUB
                    for kt in range(nkt):
                        items.append((h, qb, kt, nkt, blk))
                    blk += 1

            def geom(i):
                h, qb, kt, nkt, blk_ = items[i]
                j = kt - SUB * qb
                c0 = 128 * j if j > 0 else 0
                return h, qb, kt, nkt, blk_, j, c0

            def qk(i):
                h, qb, kt, nkt, blk_, j, c0 = geom(i)
                kn_t, kn_k, qn_t, qn_k, sg_t, sg_k, qr_t, qr_k, v2_t, v2_k = heads[h]
                pb = 64 * (h % 2)
                qs = slice(qb * QB + c0, (qb + 1) * QB)
                ks = slice(kt * 128, (kt + 1) * 128)
                ps_t, ps_k = PSc(i)
                pt_t, pt_k = pTb(i)
                S.I("pe", "matmul", ps_t[:, c0:QB], lhsT=kn_t[:, ks], rhs=qn_t[:, qs], start=True, stop=False,
                    reads=[kn_k, qn_k], writes=[ps_k])
                S.I("pe", "matmul", ps_t[:, c0:QB], lhsT=krS[pb:pb + 64, ks], rhs=qr_t[pb:pb + 64, qs], start=False, stop=True,
                    reads=["krS", qr_k], writes=[ps_k])
                S.I("act", "activation", out=pt_t[:, c0:QB], in_=ps_t[:, c0:QB], func=AF.Exp, scale=SCALE,
                    reads=[ps_k], writes=[pt_k])
                if j >= 0:
                    S.I("pool", "memset", pt_t[64:128, c0:c0 + 64], 0.0, reads=[], writes=[pt_k])

            def pv(i):
                h, qb, kt, nkt, blk_, j, c0 = geom(i)
                kn_t, kn_k, qn_t, qn_k, sg_t, sg_k, qr_t, qr_k, v2_t, v2_k = heads[h]
                pt_t, pt_k = pTb(i)
                po_t, po_k = POc(blk_)
                pd_t, pd_k = PDc(blk_)
                S.I("pe", "matmul", po_t[:, c0:QB], lhsT=v2_t[:, kt, (h % 2) * 128:(h % 2) * 128 + 128], rhs=pt_t[:, c0:QB], start=(kt == 0), stop=(kt == nkt - 1),
                    reads=[v2_k, pt_k], writes=[po_k])
                S.I("pe", "matmul", pd_t[:, c0:QB], lhsT=ones[:], rhs=pt_t[:, c0:QB], start=(kt == 0), stop=(kt == nkt - 1),
                    reads=["ones", pt_k], writes=[pd_k])
                if kt == nkt - 1:
                    rd_t, rd_k = rden(blk_)
                    ob_t, ob_k = ob(blk_)
                    zb_t, zb_k = zb(blk_)
                    S.I("dve", "reciprocal", out=rd_t[:], in_=pd_t[:, 0:QB], reads=[pd_k], writes=[rd_k])
                    S.I("dve", "tensor_tensor", out=ob_t[:], in0=po_t[:, 0:QB], in1=rd_t[:], op=ALU.mult, reads=[po_k, rd_k], writes=[ob_k])
                    S.I("pool", "tensor_tensor", out=zb_t[:], in0=ob_t[:], in1=sg_t[:, qb * QB:(qb + 1) * QB], op=ALU.mult, reads=[ob_k, sg_k], writes=[zb_k])
                    S.I("sp", "dma_start", out=zTb[qb, :, h, :], in_=zb_t[:], reads=[zb_k], writes=[("zTb", qb, h)], dma_key=("zbst", blk_ % 2))
                    if qb == NQB - 1 and h + 2 < 8:
                        load_head(h + 2)

            load_head(0)
            load_head(1)
            LOOK = 2
            for i in range(min(LOOK, len(items))):
                qk(i)
            for i in range(len(items)):
                if i + LOOK < len(items):
                    qk(i + LOOK)
                pv(i)
        S.barrier()

        if STOP_AFTER == "B2":
            S.emit(final_wait_ops=[S.ops[-1]])
            return nc
        def load_z_B(sbO):
            zT = Buf(sbO, "zTin", [128, 8, QB], BF16, 3)
            done = set()

            def load(t):
                bq = t // SUB
                for b_ in (bq, bq + 1):
                    if b_ < NQB and b_ not in done:
                        done.add(b_)
                        z_t, z_k = zT(b_)
                        S.I("sp", "dma_start", out=z_t[:], in_=zTb[b_], writes=[z_k], dma_key=z_k)

            def get(t):
                bq, sub = divmod(t, SUB)
                z_t, z_k = zT(bq)
                return (lambda c: z_t[:, c, sub * 128:(sub + 1) * 128]), [z_k]
            return load, get

        outproj_phase("B3", w_out_b, 8, None, g_post_b, load_z_B, h1, out, 1)
        S.emit(final_wait_ops=list(final_ops))
    return nc


def _chunkT(w, K):
    n = w.shape[1]
    return np.ascontiguousarray(w.reshape(K, 128, n).transpose(1, 0, 2))


def _vecT(g, K):
    return np.ascontiguousarray(g.reshape(K, 128).T)


def _rope_tables(S_len, half):
    inv = (np.float32(10000.0) ** (-np.arange(half, dtype=np.float32) / np.float32(half))).astype(np.float32)
    ang = (np.arange(S_len, dtype=np.float32)[:, None] * inv[None, :]).astype(np.float32)
    c = np.cos(ang).astype(np.float32)
    s = np.sin(ang).astype(np.float32)
    nt = S_len // 128
    lay = lambda a: np.ascontiguousarray(a.reshape(nt, 128, half).transpose(1, 0, 2))
    return lay(c), lay(s)


def _decay_tables():
    h = np.arange(NH, dtype=np.float64)
    lg = np.log(1.0 - np.exp2(-5.0 - h))
    i = np.arange(128, dtype=np.float64)
    xi = np.exp(lg[:, None] * (i + 1.0)[None])
    zeta = np.exp(lg[:, None] * (127.0 - i)[None])
    ch = (np.arange(128) // 64)
    c_ = i[:, None]
    m_ = i[None, :]
    same = ch[:, None] == ch[None, :]
    prev = ch[None, :] < ch[:, None]
    Dm = np.zeros((NH, 128, 128))
    for hh in range(NH):
        Dm[hh] = np.where(same, np.exp(lg[hh] * np.abs(c_ - m_)), np.where(prev, np.exp(lg[hh] * (c_ - m_)), 0.0))
    T2 = Dm / (xi[:, :, None] * zeta[:, None, :])
    t2t = np.ascontiguousarray(T2.transpose(2, 0, 1)).astype(np.float32)
    dxi = np.zeros((128, NH, 128), np.float32)
    for hh in range(NH):
        dxi[np.arange(128), hh, np.arange(128)] = xi[hh]
    zs = np.ascontiguousarray((zeta * (128.0 ** -0.5)).T).astype(np.float32)
    return t2t, dxi, zs


def _prep_shared(inp, S_len):
    f = lambda a: np.asarray(a, dtype=np.float32)
    cA, sA = _rope_tables(S_len, 64)
    cB, sB = _rope_tables(S_len, 32)
    t2t, dxi, zs = _decay_tables()
    return {
        "w_in_a": _chunkT(f(inp["w_in_a"])[0], 8),
        "g_pre_a": _vecT(f(inp["g_pre_a"])[0], 8),
        "gn_gain_a": _vecT(f(inp["gn_gain_a"])[0], 16),
        "w_out_a": _chunkT(f(inp["w_out_a"])[0], 16),
        "g_post_a": np.ascontiguousarray(np.broadcast_to(f(inp["g_post_a"])[0][None, :], (128, D))),
        "g_kv": _vecT(f(inp["g_kv"]), 8),
        "w_kv_a": _chunkT(f(inp["w_kv_a"]), 8),
        "g_kv_lat": _vecT(f(inp["g_kv_lat"]), 2),
        "w_uk": _chunkT(f(inp["w_uk"]), 2),
        "w_uv": _chunkT(f(inp["w_uv"]), 2),
        "g_pre_b": _vecT(f(inp["g_pre_b"])[0], 8),
        "w_in_b": _chunkT(f(inp["w_in_b"])[0], 8),
        "g_q_lat": _vecT(f(inp["g_q_lat"])[0], 3),
        "w_uq": _chunkT(f(inp["w_uq"])[0], 3),
        "w_out_b": _chunkT(f(inp["w_out_b"])[0], 8),
        "g_post_b": np.ascontiguousarray(np.broadcast_to(f(inp["g_post_b"])[0][None, :], (128, D))),
        "csA": np.ascontiguousarray(np.stack([cA, sA], axis=2)), "csB": np.ascontiguousarray(np.stack([cB, sB], axis=2)),
        "t2t": t2t, "dxi": dxi, "zs": zs,
        "ident": np.eye(128, dtype=np.float32),
    }


def kernel(**inputs):
    x = np.asarray(inputs["x"], dtype=np.float32)
    B, S_len, _ = x.shape
    shared = _prep_shared(inputs, S_len)
    nc = build(S_len)
    in_maps = []
    for b in range(B):
        m = dict(shared)
        m["x"] = np.ascontiguousarray(x[b])
        in_maps.append(m)
    res = run_bass_kernel_spmd(nc, in_maps, core_ids=list(range(B)))
    return np.stack([np.asarray(r["out"], dtype=np.float32) for r in res.results], axis=0)
```
